# Optimizing a Trainium2 kernel written in Bass

```python
import math
import jax, jax.numpy as jnp
from jax import lax
import numpy as np

D_MODEL = 2048
BATCH = 4
SEQ = 2048
DEPTH = 2
DEC_BATCH = 128
DEC_SEQ = 8
PAST_LEN = 8192
PAGE_SIZE = 128

N_BRANCH = 4
SSM_INNER = D_MODEL // 2
SSM_HEADDIM = 64
SSM_HEADS = SSM_INNER // SSM_HEADDIM
SSM_GROUPS = 4
SSM_STATE = 128
SSM_CONV = 4
SSM_CHUNK = 128
SSM_CONV_DIM = SSM_INNER + 2 * SSM_GROUPS * SSM_STATE
CF_WIDTH = D_MODEL // 2
CF_CONV_WIDTH = 31
SC_WIDTH = D_MODEL // 2
SC_CONV_WIDTH = 3
ATT_HEAD_DIM = 64
ATT_HEADS = 16
ATT_KV_HEADS = 4
ATT_REP = ATT_HEADS // ATT_KV_HEADS
ATT_WIDTH = ATT_HEADS * ATT_HEAD_DIM
ATT_KV_WIDTH = ATT_KV_HEADS * ATT_HEAD_DIM
WINDOW = 128
ATT_BLOCK = 128
WIN_BUF = min(WINDOW, PAST_LEN)
EPS = 1e-6

IN_SPLITS = (SSM_INNER, SSM_CONV_DIM, SSM_HEADS,
             2 * CF_WIDTH, CF_WIDTH,
             3 * SC_WIDTH, SC_WIDTH,
             ATT_WIDTH, ATT_KV_WIDTH, ATT_KV_WIDTH, ATT_WIDTH,
             N_BRANCH * D_MODEL)
IN_COLS = sum(IN_SPLITS)

kernel_name = "hybrid_ssd_conformer_shortconv_swa_step"


def _in_offsets():
    return [int(o) for o in np.cumsum(IN_SPLITS)[:-1]]


def rms_norm(x, w, eps=EPS):
    xf = x.astype(jnp.float32)
    xf = xf * lax.rsqrt(jnp.mean(xf * xf, axis=-1, keepdims=True) + eps)
    return (xf * w.astype(jnp.float32)).astype(x.dtype)


def layer_norm(x, w, b, eps=1e-5):
    xf = x.astype(jnp.float32)
    mu = jnp.mean(xf, axis=-1, keepdims=True)
    xc = xf - mu
    var = jnp.mean(xc * xc, axis=-1, keepdims=True)
    return (xc * lax.rsqrt(var + eps) * w.astype(jnp.float32) + b.astype(jnp.float32)).astype(x.dtype)


def causal_dwconv(x, prefix, w, b=None):
    k = w.shape[0]
    xp = jnp.concatenate([prefix.astype(x.dtype), x], axis=1)
    y = lax.conv_general_dilated(xp, w[:, None, :].astype(x.dtype), window_strides=(1,),
                                 padding='VALID', dimension_numbers=('NWC', 'WIO', 'NWC'),
                                 feature_group_count=x.shape[-1])
    if b is not None:
        y = y + b.astype(x.dtype)
    return y, xp[:, xp.shape[1] - (k - 1):]


def segsum_exp(cs):
    n = cs.shape[-1]
    diff = cs[..., :, None] - cs[..., None, :]
    mask = jnp.tril(jnp.ones((n, n), dtype=bool))
    return jnp.exp(jnp.where(mask, diff, -jnp.inf))


def ssd_scan(x, dt, a, bmat, cmat, h0):
    b_, t_, h_, p_ = x.shape
    g_, n_ = bmat.shape[2], bmat.shape[3]
    r_ = h_ // g_
    lc = SSM_CHUNK if t_ % SSM_CHUNK == 0 else t_
    nc = t_ // lc
    xf = (x.astype(jnp.float32) * dt[..., None]).reshape(b_, nc, lc, g_, r_, p_)
    bf = bmat.astype(jnp.float32).reshape(b_, nc, lc, g_, n_)
    cf = cmat.astype(jnp.float32).reshape(b_, nc, lc, g_, n_)
    da = (dt * a).reshape(b_, nc, lc, g_, r_)
    cs = jnp.cumsum(da, axis=2)
    lmat = segsum_exp(jnp.moveaxis(cs, 2, -1))
    cb = jnp.einsum('bclgn,bcsgn->bcgls', cf, bf)
    y_diag = jnp.einsum('bcgrls,bcsgrp->bclgrp', cb[:, :, :, None] * lmat, xf)
    decay = jnp.exp(cs[:, :, -1:] - cs)
    chunk_states = jnp.einsum('bclgn,bclgr,bclgrp->bcgrpn', bf, decay, xf)
    chunk_decay = jnp.exp(cs[:, :, -1])

    def step(h, inp):
        s_c, d_c = inp
        return h * d_c[..., None, None] + s_c, h

    h0g = h0.astype(jnp.float32).reshape(b_, g_, r_, p_, n_)
    h_final, h_prev = lax.scan(step, h0g, (jnp.moveaxis(chunk_states, 1, 0),
                                           jnp.moveaxis(chunk_decay, 1, 0)))
    h_prev = jnp.moveaxis(h_prev, 0, 1)
    y_off = jnp.einsum('bclgn,bcgrpn,bclgr->bclgrp', cf, h_prev, jnp.exp(cs))
    y = (y_diag + y_off).reshape(b_, t_, h_, p_)
    return y, h_final.reshape(b_, h_, p_, n_)


def swa_sink_attend(q, k, v, q_pos, k_pos, sinks):
    s = jnp.einsum('bnqhrd,bnkhd->bnhrqk', q.astype(jnp.float32), k.astype(jnp.float32))
    s = s * (ATT_HEAD_DIM ** -0.5)
    rel = q_pos[:, :, None] - k_pos[:, None, :]
    valid = (rel >= 0) & (rel < WINDOW) & (k_pos[:, None, :] >= 0)
    s = jnp.where(valid[None, :, None, None], s, -jnp.inf)
    sink = sinks.astype(jnp.float32).reshape(1, 1, ATT_KV_HEADS, ATT_REP, 1, 1)
    m = jnp.maximum(jnp.max(s, axis=-1, keepdims=True), sink)
    p = jnp.exp(s - m)
    denom = jnp.sum(p, axis=-1, keepdims=True) + jnp.exp(sink - m)
    return jnp.einsum('bnhrqk,bnkhd->bnqhrd', p / denom, v.astype(jnp.float32))


def decoder_layer(x, h_ssm, buf_ssm, buf_cf, buf_sc, kv_buf, lp, prompt):
    (norm_w, w_in, ssm_conv_w, ssm_conv_b, dt_bias, a_log, d_skip, ssm_norm_w, w_out_ssm,
     cf_conv_w, cf_conv_b, cf_ln_w, cf_ln_b, w_out_cf, sc_conv_w, w_out_sc,
     sinks, w_out_att, w_o) = lp
    b_, t_, _ = x.shape
    xn = rms_norm(x, norm_w)
    proj = jnp.einsum('btd,de->bte', xn, w_in.astype(x.dtype))
    (z, xbc, dt_raw, cf_in, cf_gate, sc_in, sc_gate,
     q, k, v, att_gate, merge) = jnp.split(proj, _in_offsets(), axis=-1)

    xbc, new_buf_ssm = causal_dwconv(xbc, buf_ssm, ssm_conv_w, ssm_conv_b)
    xbc = jax.nn.silu(xbc)
    xs, bs, cs = jnp.split(xbc, [SSM_INNER, SSM_INNER + SSM_GROUPS * SSM_STATE], axis=-1)
    dt = jax.nn.softplus(dt_raw.astype(jnp.float32) + dt_bias.astype(jnp.float32))
    a = -jnp.exp(a_log.astype(jnp.float32))
    xh = xs.reshape(b_, t_, SSM_HEADS, SSM_HEADDIM)
    y_a, new_h = ssd_scan(xh, dt, a, bs.reshape(b_, t_, SSM_GROUPS, SSM_STATE),
                          cs.reshape(b_, t_, SSM_GROUPS, SSM_STATE), h_ssm)
    y_a = y_a + d_skip.astype(jnp.float32)[:, None] * xh.astype(jnp.float32)
    y_a = y_a.reshape(b_, t_, SSM_INNER).astype(x.dtype) * jax.nn.silu(z)
    out_a = rms_norm(y_a, ssm_norm_w, 1e-5) @ w_out_ssm.astype(x.dtype)

    u = cf_in[..., :CF_WIDTH] * jax.nn.sigmoid(cf_in[..., CF_WIDTH:])
    u, new_buf_cf = causal_dwconv(u, buf_cf, cf_conv_w, cf_conv_b)
    u = jax.nn.silu(layer_norm(u, cf_ln_w, cf_ln_b))
    out_b = (u * jax.nn.silu(cf_gate)) @ w_out_cf.astype(x.dtype)

    gb, gc, sv = jnp.split(sc_in, 3, axis=-1)
    u, new_buf_sc = causal_dwconv(gc * sv, buf_sc, sc_conv_w)
    out_c = (gb * u * jax.nn.silu(sc_gate)) @ w_out_sc.astype(x.dtype)

    q = q.reshape(b_, t_, ATT_KV_HEADS, ATT_REP, ATT_HEAD_DIM)
    k = k.reshape(b_, t_, ATT_KV_HEADS, ATT_HEAD_DIM)
    v = v.reshape(b_, t_, ATT_KV_HEADS, ATT_HEAD_DIM)
    if prompt:
        nb = t_ // ATT_BLOCK
        qb = q.reshape(b_, nb, ATT_BLOCK, ATT_KV_HEADS, ATT_REP, ATT_HEAD_DIM)
        kb = k.reshape(b_, nb, ATT_BLOCK, ATT_KV_HEADS, ATT_HEAD_DIM)
        vb = v.reshape(b_, nb, ATT_BLOCK, ATT_KV_HEADS, ATT_HEAD_DIM)
        k_band = jnp.concatenate([jnp.concatenate([jnp.zeros_like(kb[:, :1]), kb[:, :-1]], axis=1), kb], axis=2)
        v_band = jnp.concatenate([jnp.concatenate([jnp.zeros_like(vb[:, :1]), vb[:, :-1]], axis=1), vb], axis=2)
        pos = jnp.arange(t_, dtype=jnp.int32).reshape(nb, ATT_BLOCK)
        k_pos = jnp.concatenate([pos - ATT_BLOCK, pos], axis=1)
        o = swa_sink_attend(qb, k_band, v_band, pos, k_pos, sinks)
        new_k = k[:, t_ - WIN_BUF:]
        new_v = v[:, t_ - WIN_BUF:]
    else:
        k_buf, v_buf = kv_buf
        k_all = jnp.concatenate([k_buf.astype(k.dtype), k], axis=1)
        v_all = jnp.concatenate([v_buf.astype(v.dtype), v], axis=1)
        q_pos = PAST_LEN + jnp.arange(t_, dtype=jnp.int32)
        k_pos = jnp.concatenate([PAST_LEN - WIN_BUF + jnp.arange(WIN_BUF, dtype=jnp.int32), q_pos])
        o = swa_sink_attend(q[:, None], k_all[:, None], v_all[:, None], q_pos[None], k_pos[None], sinks)
        new_k = k_all[:, t_:]
        new_v = v_all[:, t_:]
    o = o.reshape(b_, t_, ATT_WIDTH).astype(x.dtype)
    out_d = (o * jax.nn.silu(att_gate)) @ w_out_att.astype(x.dtype)

    g = jax.nn.sigmoid(merge.reshape(b_, t_, N_BRANCH, D_MODEL))
    h = g[:, :, 0] * out_a + g[:, :, 1] * out_b + g[:, :, 2] * out_c + g[:, :, 3] * out_d
    x = x + h @ w_o.astype(x.dtype)
    return x, (new_h.astype(h_ssm.dtype), new_buf_ssm, new_buf_cf, new_buf_sc, new_k, new_v)


def setup_inputs(seed: int = 0) -> dict:
    key = jax.random.key(seed)
    ks = jax.random.split(key, 32)
    f32 = jnp.float32

    def nrm(k, shape, scale):
        return jax.random.normal(k, shape, f32) * scale

    dt0 = jnp.exp(jax.random.uniform(ks[12], (DEPTH, SSM_HEADS), f32, math.log(1e-3), math.log(1e-1)))
    return {
        "x_prompt": nrm(ks[0], (BATCH, SEQ, D_MODEL), 1.0),
        "x_sample": nrm(ks[1], (DEC_BATCH, DEC_SEQ, D_MODEL), 1.0),
        "state_ssm": nrm(ks[2], (DEPTH, DEC_BATCH, SSM_HEADS, SSM_HEADDIM, SSM_STATE), 0.5),
        "state_conv_ssm": nrm(ks[3], (DEPTH, DEC_BATCH, SSM_CONV - 1, SSM_CONV_DIM), 1.0),
        "state_conv_cf": nrm(ks[4], (DEPTH, DEC_BATCH, CF_CONV_WIDTH - 1, CF_WIDTH), 1.0),
        "state_conv_sc": nrm(ks[5], (DEPTH, DEC_BATCH, SC_CONV_WIDTH - 1, SC_WIDTH), 1.0),
        "cache_k": nrm(ks[6], (DEPTH, DEC_BATCH, WIN_BUF, ATT_KV_HEADS, ATT_HEAD_DIM), 1.0),
        "cache_v": nrm(ks[7], (DEPTH, DEC_BATCH, WIN_BUF, ATT_KV_HEADS, ATT_HEAD_DIM), 1.0),
        "norm_w": 1.0 + nrm(ks[8], (DEPTH, D_MODEL), 0.02),
        "w_in": nrm(ks[9], (DEPTH, D_MODEL, IN_COLS), D_MODEL ** -0.5),
        "ssm_conv_w": nrm(ks[10], (DEPTH, SSM_CONV, SSM_CONV_DIM), SSM_CONV ** -0.5),
        "ssm_conv_b": nrm(ks[11], (DEPTH, SSM_CONV_DIM), 0.02),
        "ssm_dt_bias": dt0 + jnp.log(-jnp.expm1(-dt0)),
        "ssm_a_log": jnp.log(jax.random.uniform(ks[13], (DEPTH, SSM_HEADS), f32, 1.0, 16.0)),
        "ssm_d": 1.0 + nrm(ks[14], (DEPTH, SSM_HEADS), 0.1),
        "ssm_norm_w": 1.0 + nrm(ks[15], (DEPTH, SSM_INNER), 0.02),
        "w_out_ssm": nrm(ks[16], (DEPTH, SSM_INNER, D_MODEL), SSM_INNER ** -0.5),
        "cf_conv_w": nrm(ks[17], (DEPTH, CF_CONV_WIDTH, CF_WIDTH), CF_CONV_WIDTH ** -0.5),
        "cf_conv_b": nrm(ks[18], (DEPTH, CF_WIDTH), 0.02),
        "cf_ln_w": 1.0 + nrm(ks[19], (DEPTH, CF_WIDTH), 0.02),
        "cf_ln_b": nrm(ks[20], (DEPTH, CF_WIDTH), 0.02),
        "w_out_cf": nrm(ks[21], (DEPTH, CF_WIDTH, D_MODEL), CF_WIDTH ** -0.5),
        "sc_conv_w": nrm(ks[22], (DEPTH, SC_CONV_WIDTH, SC_WIDTH), SC_CONV_WIDTH ** -0.5),
        "w_out_sc": nrm(ks[23], (DEPTH, SC_WIDTH, D_MODEL), SC_WIDTH ** -0.5),
        "att_sinks": nrm(ks[24], (DEPTH, ATT_HEADS), 0.5),
        "w_out_att": nrm(ks[25], (DEPTH, ATT_WIDTH, D_MODEL), ATT_WIDTH ** -0.5),
        "w_o": nrm(ks[26], (DEPTH, D_MODEL, D_MODEL), D_MODEL ** -0.5),
        "final_norm_w": 1.0 + nrm(ks[27], (D_MODEL,), 0.02),
    }


def reference(x_prompt, x_sample, state_ssm, state_conv_ssm, state_conv_cf, state_conv_sc,
              cache_k, cache_v, norm_w, w_in, ssm_conv_w, ssm_conv_b, ssm_dt_bias, ssm_a_log,
              ssm_d, ssm_norm_w, w_out_ssm, cf_conv_w, cf_conv_b, cf_ln_w, cf_ln_b, w_out_cf,
              sc_conv_w, w_out_sc, att_sinks, w_out_att, w_o, final_norm_w):
    layer_params = [(norm_w[l], w_in[l], ssm_conv_w[l], ssm_conv_b[l], ssm_dt_bias[l], ssm_a_log[l],
                     ssm_d[l], ssm_norm_w[l], w_out_ssm[l], cf_conv_w[l], cf_conv_b[l], cf_ln_w[l],
                     cf_ln_b[l], w_out_cf[l], sc_conv_w[l], w_out_sc[l], att_sinks[l], w_out_att[l], w_o[l])
                    for l in range(DEPTH)]

    bp = x_prompt.shape[0]
    dtp = x_prompt.dtype
    xp = x_prompt
    p_states = []
    for l in range(DEPTH):
        xp, st = decoder_layer(
            xp,
            jnp.zeros((bp, SSM_HEADS, SSM_HEADDIM, SSM_STATE), dtp),
            jnp.zeros((bp, SSM_CONV - 1, SSM_CONV_DIM), dtp),
            jnp.zeros((bp, CF_CONV_WIDTH - 1, CF_WIDTH), dtp),
            jnp.zeros((bp, SC_CONV_WIDTH - 1, SC_WIDTH), dtp),
            None, layer_params[l], True)
        p_states.append(st)
    y_prompt = rms_norm(xp, final_norm_w)

    xs = x_sample
    s_states = []
    for l in range(DEPTH):
        xs, st = decoder_layer(xs, state_ssm[l], state_conv_ssm[l], state_conv_cf[l], state_conv_sc[l],
                               (cache_k[l], cache_v[l]), layer_params[l], False)
        s_states.append(st)
    y_sample = rms_norm(xs, final_norm_w)

    p_ssm, p_conv_ssm, p_conv_cf, p_conv_sc, p_k, p_v = [jnp.stack(s, axis=0) for s in zip(*p_states)]
    s_ssm, s_conv_ssm, s_conv_cf, s_conv_sc, s_k, s_v = [jnp.stack(s, axis=0) for s in zip(*s_states)]
    return (y_prompt, y_sample, p_ssm, p_conv_ssm, p_conv_cf, p_conv_sc, p_k, p_v,
            s_ssm, s_conv_ssm, s_conv_cf, s_conv_sc, s_k, s_v)
```

```python
import contextlib
import numpy as np
import concourse.bass as bass
import concourse.mybir as mybir

F32 = mybir.dt.float32
BF16 = mybir.dt.bfloat16
AF = mybir.ActivationFunctionType
ALU = mybir.AluOpType
AX = mybir.AxisListType


class Res:
    __slots__ = ("name", "last_write", "reads", "chan", "psum")

    def __init__(self, name):
        self.name = name
        self.psum = False
        self.last_write = None
        self.reads = {}
        self.chan = None


class V:
    __slots__ = ("buf", "ap")

    def __init__(self, buf, ap):
        self.buf = buf
        self.ap = ap


class Buf:
    def __init__(self, kb, name, t, space):
        self.kb = kb
        self.name = name
        self.t = t
        self.space = space
        self.res = Res(name)

    def __getitem__(self, idx):
        return V(self, self.t[idx])

    def pat(self, offset, pattern):
        return V(self, bass.AP(self.t, offset, [list(p) for p in pattern]))


class EngState:
    def __init__(self, name, handle, sem):
        self.name = name
        self.h = handle
        self.sem = sem
        self.count = 0
        self.waited = {}
        self.thunks = []


class KB:
    def __init__(self):
        self.nc = bass.Bass("TRN2", target_bir_lowering=False)
        self.stack = contextlib.ExitStack()
        nc = self.nc
        self.sems = {}
        self.engs = {}
        for name, h in (("pe", nc.tensor), ("act", nc.scalar), ("dve", nc.vector),
                        ("pool", nc.gpsimd), ("sp", nc.sync)):
            sem = self.stack.enter_context(nc.semaphore("sem_" + name))
            self.engs[name] = EngState(name, h, sem)
            self.sems[name] = sem
        self.nchan = 0
        self.chans = {}
        self.n_inst = 0

    def sb(self, name, shape, dtype):
        t = self.stack.enter_context(self.nc.sbuf_tensor(name, list(shape), dtype))
        return Buf(self, name, t, "sb")

    def ps(self, name, shape, dtype):
        t = self.stack.enter_context(self.nc.psum_tensor(name, list(shape), dtype))
        b = Buf(self, name, t, "ps")
        b.res.psum = True
        return b

    def dram(self, name, shape, dtype, kind):
        return self.nc.dram_tensor(name, list(shape), dtype, kind=kind)

    def new_chan(self, key):
        if key in self.chans:
            return key
        sem = self.stack.enter_context(self.nc.semaphore("ch_" + key))
        self.nchan += 1
        assert self.nchan < 130, "too many dma channels"
        self.chans[key] = [sem, 0]
        self.sems[key] = sem
        return key

    def _deps(self, reads, writes):
        deps = {}

        def add(k, v):
            if deps.get(k, 0) < v:
                deps[k] = v
        for r in reads:
            if r.last_write is not None:
                add(*r.last_write)
            if getattr(r, "psum", False):
                for k, v in r.reads.items():
                    add(k, v)
        for w in writes:
            if w.last_write is not None:
                add(*w.last_write)
            for k, v in w.reads.items():
                add(k, v)
        return deps

    def _emit_waits(self, es, deps):
        for k, v in deps.items():
            if k == es.name and k in ("pe", "sp"):
                continue
            if es.waited.get(k, 0) >= v:
                continue
            if k in self.engs and v > self.engs[k].count:
                raise RuntimeError("wait on a not-yet-signaled %s op (value %d > %d): signal that matmul" % (k, v, self.engs[k].count))
            es.waited[k] = v
            sem = self.sems[k]
            es.thunks.append(lambda h=es.h, sem=sem, v=v: h.wait_ge(sem, v))

    def op(self, eng, fn, reads, writes, sig=True):
        es = self.engs[eng]
        reads = [r.buf.res for r in reads if isinstance(r, V)]
        writes = [w.buf.res for w in writes if isinstance(w, V)]
        deps = self._deps(reads, writes)
        self._emit_waits(es, deps)
        self.n_inst += 1
        import sys as _sys
        fr = _sys._getframe(2)
        tag = []
        while fr is not None and len(tag) < 5:
            tag.append(fr.f_lineno)
            fr = fr.f_back
        if sig:
            es.count += 1
            val = es.count

            def th(h=es.h, sem=es.sem, tag=tag):
                try:
                    fn(h).then_inc(sem, 1)
                except Exception:
                    print("FAILED OP at lines", tag)
                    raise
        else:
            val = es.count + 1

            def th(h=es.h, tag=tag):
                try:
                    fn(h)
                except Exception:
                    print("FAILED OP at lines", tag)
                    raise
        es.thunks.append(th)
        for r in reads:
            if r.reads.get(eng, 0) < val:
                r.reads[eng] = val
        for w in writes:
            w.last_write = (eng, val)
            w.reads = {}

    def dma(self, out, in_, q="sp", chan_res=None, **kw):
        es = self.engs[q]
        reads = [in_.buf.res] if isinstance(in_, V) else []
        writes = [out.buf.res] if isinstance(out, V) else []
        owner = (writes + reads)
        if owner:
            res = owner[0]
            if writes:
                if res.chan is None:
                    res.chan = self.new_chan(res.name)
                ck = res.chan
            else:
                ck = self.new_chan(res.name + "_rd")
        else:
            ck = "misc"
            if ck not in self.chans:
                self.new_chan(ck)
        deps = self._deps(reads, writes)
        self._emit_waits(es, deps)
        ch = self.chans[ck]
        ch[1] += 16
        val = ch[1]
        sem = ch[0]
        o = out.ap if isinstance(out, V) else out
        i = in_.ap if isinstance(in_, V) else in_
        self.n_inst += 1
        es.thunks.append(lambda h=es.h, o=o, i=i, sem=sem, kw=kw:
                         h.dma_start(out=o, in_=i, **kw).then_inc(sem, 16))
        for r in reads:
            r.reads[ck] = val
        for w in writes:
            w.last_write = (ck, val)
            w.reads = {}

    @staticmethod
    def _a(x):
        return x.ap if isinstance(x, V) else x

    def mm(self, out, lhsT, rhs, start=True, stop=True, sig=None, **kw):
        if sig is None:
            sig = stop
        o, l, r = out.ap, lhsT.ap, rhs.ap
        self.op("pe", lambda h: h.matmul(o, l, r, start=start, stop=stop, **kw),
                [lhsT, rhs], [out], sig=sig)

    def tr(self, out, in_, ident, sig=True):
        o, i, d = out.ap, in_.ap, ident.ap
        self.op("pe", lambda h: h.transpose(o, i, d), [in_, ident], [out], sig=sig)

    def act(self, out, in_, func, bias=None, scale=None, accum=None, eng="act"):
        o, i = out.ap, in_.ap
        kw = {}
        if bias is not None:
            kw["bias"] = self._a(bias)
        if scale is not None:
            kw["scale"] = self._a(scale)
        if accum is not None:
            kw["accum_out"] = accum.ap
        self.op(eng, lambda h: h.activation(o, i, func, **kw),
                [in_, bias, scale], [out, accum])

    def tt(self, out, a, b, op, eng="dve"):
        o, x, y = out.ap, a.ap, b.ap
        self.op(eng, lambda h: h.tensor_tensor(o, x, y, op), [a, b], [out])

    def ts(self, out, a, s1, s2, op0, op1=None, eng="dve", accum=None):
        o, x = out.ap, a.ap
        p1, p2 = self._a(s1), self._a(s2)
        kw = {}
        if accum is not None:
            kw["accum_out"] = accum.ap
        if op1 is None:
            self.op(eng, lambda h: h.tensor_scalar(o, x, p1, None, op0, **kw), [a, s1], [out, accum])
        else:
            self.op(eng, lambda h: h.tensor_scalar(o, x, p1, p2, op0, op1, **kw), [a, s1, s2], [out, accum])

    def stt(self, out, a, scalar, b, op0, op1, accum=None):
        o, x, y = out.ap, a.ap, b.ap
        s = self._a(scalar)
        kw = {}
        if accum is not None:
            kw["accum_out"] = accum.ap
        self.op("dve", lambda h: h.scalar_tensor_tensor(o, x, s, y, op0, op1, **kw),
                [a, scalar, b], [out, accum])

    def cp(self, out, in_, eng="dve"):
        o, i = out.ap, in_.ap
        if eng == "act":
            self.op(eng, lambda h: h.copy(o, i), [in_], [out])
        else:
            self.op(eng, lambda h: h.tensor_copy(o, i), [in_], [out])

    def memset(self, out, val, eng="dve"):
        o = out.ap
        self.op(eng, lambda h: h.memset(o, val), [], [out])

    def red(self, out, in_, op, eng="dve", axis=None):
        o, i = out.ap, in_.ap
        ax = AX.X if axis is None else axis
        self.op(eng, lambda h: h.tensor_reduce(o, i, ax, op), [in_], [out])

    def recip(self, out, in_):
        o, i = out.ap, in_.ap
        self.op("dve", lambda h: h.reciprocal(o, i), [in_], [out])

    def barrier(self):
        for name, es in self.engs.items():
            deps = {n: e.count for n, e in self.engs.items() if n != name and e.count > 0}
            for k, (sem, cnt) in self.chans.items():
                if cnt > 0 and not k.startswith("wslot"):
                    deps[k] = cnt
            self._emit_waits(es, deps)

    def finish(self):
        sp = self.engs["sp"]
        for k, (sem, cnt) in self.chans.items():
            if cnt > 0:
                sp.thunks.append(lambda h=sp.h, sem=sem, cnt=cnt: h.wait_ge(sem, cnt))
        for name, es in self.engs.items():
            if name != "sp" and es.count > 0:
                sp.thunks.append(lambda h=sp.h, sem=es.sem, c=es.count: h.wait_ge(sem, c))
        with self.nc.Block() as block:
            @block.tensor
            def _(e):
                for t in self.engs["pe"].thunks:
                    t()

            @block.scalar
            def _(e):
                for t in self.engs["act"].thunks:
                    t()

            @block.vector
            def _(e):
                for t in self.engs["dve"].thunks:
                    t()

            @block.gpsimd
            def _(e):
                for t in self.engs["pool"].thunks:
                    t()

            @block.sync
            def _(e):
                for t in self.engs["sp"].thunks:
                    t()
        self.stack.close()
        return self.nc


class SubBuf(Buf):
    def __init__(self, arena, name, off, shape):
        self.kb = arena.kb
        self.name = name
        self.space = "sb"
        self.res = Res(name)
        self.arena = arena
        self.off = off
        self.shape = list(shape)
        n = int(np.prod(shape[1:]))
        base = arena.buf.t[0:shape[0], off:off + n]
        if len(shape) == 2:
            self.t = base
        elif len(shape) == 3:
            self.t = base.rearrange("p (a b) -> p a b", a=shape[1])
        elif len(shape) == 4:
            self.t = base.rearrange("p (a b c) -> p a b c", a=shape[1], b=shape[2])
        else:
            raise ValueError(shape)

    def pat(self, offset, pattern):
        F = self.arena.F
        pat = [list(p) for p in pattern]
        assert pat[0][0] in (F, 0), (pat, F)
        return V(self, bass.AP(self.arena.buf.t, self.off + offset, pat))


class Arena:
    def __init__(self, kb, name, ncols, dtype):
        self.kb = kb
        self.name = name
        self.F = ncols
        self.buf = kb.sb(name, [128, ncols], dtype)
        self.top = 0
        self.n = 0

    def reset(self):
        kb = self.kb
        fence = {n: e.count for n, e in kb.engs.items() if e.count > 0}
        for k, (sem, cnt) in kb.chans.items():
            if cnt > 0 and not k.startswith("wslot"):
                fence[k] = cnt
        self.fence = fence
        self.top = 0

    def alloc(self, name, shape):
        n = int(np.prod(shape[1:]))
        n = (n + 3) // 4 * 4
        off = self.top
        assert off + n <= self.F, ("arena overflow", self.name, name, off + n, self.F)
        self.top += n
        sbuf = SubBuf(self, "ar_" + self.name + "_" + name, off, shape)
        sbuf.res.reads = dict(getattr(self, "fence", {}))
        return sbuf


from concourse.bass_utils import run_bass_kernel_spmd

D = 2048
INC = 21008
Z0, XBC0, DT0, CFA0, CFB0, CFG0 = 0, 1024, 3072, 3088, 4112, 5136
SCB0, SCC0, SCV0, SCG0 = 6160, 7184, 8208, 9232
Q0, K0, V0, AG0, MG0 = 10256, 11280, 11536, 11792, 12816
NSLOT = 3
NEG = -30000.0

C_ID, C_TRIP, C_TRIS, C_SAMES, C_SELS, C_LASTP, C_LASTS, C_MASKP, C_ONES, C_END = (
    0, 128, 256, 384, 512, 528, 544, 560, 816, 944)
NMASKS = 17 * 128


def make_consts():
    c = np.zeros((128, C_END), np.float32)
    c[:, C_ID:C_ID + 128] = np.eye(128)
    s = np.arange(128)[:, None]
    l = np.arange(128)[None, :]
    c[:, C_TRIP:C_TRIP + 128] = (s <= l)
    same = (s // 8) == (l // 8)
    c[:, C_TRIS:C_TRIS + 128] = (s <= l) & same
    c[:, C_SAMES:C_SAMES + 128] = same
    c[:, C_SELS:C_SELS + 16] = (s // 8) == np.arange(16)[None, :]
    c[127, C_LASTP] = 1.0
    c[:, C_LASTS:C_LASTS + 16] = (s == (np.arange(16)[None, :] * 8 + 7))
    q = np.arange(128)[:, None]
    k = np.arange(128)[None, :]
    mp = np.zeros((128, 256), np.float32)
    mp[:, 0:128] = np.where(k > q, 0.0, NEG)
    mp[:, 128:256] = np.where(k <= q, 0.0, NEG)
    c[:, C_MASKP:C_MASKP + 256] = mp
    c[:, C_ONES:C_ONES + 128] = 1.0
    ms = np.full((128, NMASKS), NEG, np.float32)
    for qq in range(128):
        sq, i = qq // 8, qq % 8
        ms[qq, sq * 128 + i + 1: sq * 128 + 128] = 0.0
        ms[qq, 16 * 128 + sq * 8: 16 * 128 + sq * 8 + i + 1] = 0.0
    return c, ms


def build(branches=(0, 1, 2, 3)):
    kb = KB()
    nc = kb.nc

    def din(name, shape):
        return kb.dram(name, shape, F32, "ExternalInput").ap()

    def dout(name, shape):
        return kb.dram(name, shape, F32, "ExternalOutput").ap()

    xp = din("xp", [2048, D]); xs = din("xs", [128, D])
    st_ssm = din("st_ssm", [2, 16, 1024, 128]); st_cssm = din("st_cssm", [2, 48, 2048])
    st_ccf = din("st_ccf", [2, 480, 1024]); st_csc = din("st_csc", [2, 32, 1024])
    ck = din("ck", [2, 16, 128, 256]); cv = din("cv", [2, 16, 128, 256])
    norm_w = din("norm_w", [2, D]); w_in = din("w_in", [2, D, INC])
    ssm_conv_w = din("ssm_conv_w", [2, 4, 2048]); ssm_conv_b = din("ssm_conv_b", [2, 2048])
    ssm_dt_bias = din("ssm_dt_bias", [2, 16]); ssm_a_log = din("ssm_a_log", [2, 16]); ssm_d = din("ssm_d", [2, 16])
    ssm_norm_w = din("ssm_norm_w", [2, 1024]); w_out_ssm = din("w_out_ssm", [2, 1024, D])
    cf_conv_w = din("cf_conv_w", [2, 31, 1024]); cf_conv_b = din("cf_conv_b", [2, 1024])
    cf_ln_w = din("cf_ln_w", [2, 1024]); cf_ln_b = din("cf_ln_b", [2, 1024]); w_out_cf = din("w_out_cf", [2, 1024, D])
    sc_conv_w = din("sc_conv_w", [2, 3, 1024]); w_out_sc = din("w_out_sc", [2, 1024, D])
    att_sinks = din("att_sinks", [2, 16]); w_out_att = din("w_out_att", [2, 1024, D])
    w_o = din("w_o", [2, D, D]); final_norm_w = din("final_norm_w", [1, D])
    consts_d = din("consts", [128, C_END]); masks_d = din("masks", [128, NMASKS])
    w_outs = [w_out_ssm, w_out_cf, w_out_sc, w_out_att]

    y_p = dout("y_p", [2048, D]); y_s = dout("y_s", [128, D])
    p_ssm = dout("p_ssm", [2, 1024, 128]); p_cssm = dout("p_cssm", [2, 3, 2048])
    p_ccf = dout("p_ccf", [2, 30, 1024]); p_csc = dout("p_csc", [2, 2, 1024])
    p_k = dout("p_k", [2, 128, 256]); p_v = dout("p_v", [2, 128, 256])
    s_ssm = dout("s_ssm", [2, 16, 1024, 128]); s_cssm = dout("s_cssm", [2, 48, 2048])
    s_ccf = dout("s_ccf", [2, 480, 1024]); s_csc = dout("s_csc", [2, 32, 1024])
    s_k = dout("s_k", [2, 16, 128, 256]); s_v = dout("s_v", [2, 16, 128, 256])
    dbg_d = dout("dbg", [128, 6, 4096]) if DBG_CORES else None

    def dump(i, view, n):
        if dbg_d is not None:
            kb.dma(dbg_d[:, i, 0:n], view, q="pool")

    x = kb.sb("x", [128, 4, D], F32)
    xnT = kb.sb("xnT", [128, 16, 512], BF16)
    hacc = kb.sb("hacc", [128, 16, 512], BF16)
    slots = [kb.sb("wslot%d" % i, [128, 16, 512], BF16) for i in range(NSLOT)]
    cst = kb.sb("cst", [128, C_END], F32)
    identb = kb.sb("identb", [128, 128], BF16)
    r16 = kb.sb("r16", [128, 4, 2, 16], F32)
    pcA = kb.sb("pcA", [128, 16, 10], F32)
    pcB = kb.sb("pcB", [128, 8, 74], F32)
    ss = kb.sb("ss", [128, 8], F32)
    hnat = [kb.sb("hnat%d" % l, [128, 8, 128], F32) for l in range(2)]
    tailA = [kb.sb("tailA%d" % l, [128, 16, 3], F32) for l in range(2)]
    tailB = [kb.sb("tailB%d" % l, [128, 8, 30], F32) for l in range(2)]
    tailC = [kb.sb("tailC%d" % l, [128, 8, 2], F32) for l in range(2)]
    KTprev = [kb.sb("KTprev%d" % l, [128, 2, 128], BF16) for l in range(2)]
    Vprev = [kb.sb("Vprev%d" % l, [128, 256], BF16) for l in range(2)]
    arF = Arena(kb, "F", 10496, F32)
    arB = Arena(kb, "B", 17920, BF16)

    pf = [kb.ps("pf%d" % i, [128, 512], F32) for i in range(4)]
    pw = kb.ps("pw", [128, 1024], F32)
    pb = [kb.ps("pb%d" % i, [128, 1024], BF16) for i in range(2)]
    rot = {"f": 0, "b": 0, "e": 0}

    def bank():
        rot["f"] = (rot["f"] + 1) % 4
        return pf[rot["f"]]

    def bbank():
        rot["b"] = (rot["b"] + 1) % 2
        return pb[rot["b"]]

    def ev():
        rot["e"] ^= 1
        return "act" if rot["e"] else "dve"

    identf = V(cst, cst.t[:, C_ID:C_ID + 128])

    def cview(c0, n, rows=128):
        return V(cst, cst.t[0:rows, c0:c0 + n])

    kb.dma(cst[:, :], consts_d, q="sp")
    kb.cp(identb[:, :], identf)
    for buf in hnat + tailA + tailB + tailC:
        kb.memset(V(buf, buf.t.ap()), 0.0, eng="pool")
    for buf in KTprev + Vprev:
        kb.memset(V(buf, buf.t.ap()), 0.0, eng="pool")
    for i, src in enumerate((ssm_dt_bias, ssm_a_log, ssm_d, att_sinks)):
        kb.dma(V(r16, r16.t[:, i, :, :].rearrange("p l h -> p (l h)")),
               src.rearrange("l h -> (l h)").partition_broadcast(128), q="sp")
    kb.act(r16[:, 1, :, :], r16[:, 1, :, :], AF.Exp)
    kb.ts(r16[:, 1, :, :], r16[:, 1, :, :], -1.0, None, ALU.mult)

    def grpv(buf, R, grp):
        if grp == 1:
            return buf[:, 0:R]
        return V(buf, buf.t[:, 0:R].rearrange("p (s w) -> p s w", s=grp))

    def load_T(dram2d, R, C, dst_fn, stage, grp=1):
        kb.dma(stage[0:R, 0:C], dram2d, q="sp")
        for c in range(C // 128):
            p = bank()
            kb.tr(p[:, 0:R], stage[0:R, c * 128:(c + 1) * 128], V(cst, cst.t[0:R, C_ID:C_ID + R]))
            kb.cp(dst_fn(c), grpv(p, R, grp), eng=ev())

    def store_T(src_fn, R, C, dram2d, stage, tmp, grp=1):
        for c in range(C // 128):
            kb.cp(grpv(tmp, R, grp), src_fn(c), eng="dve")
            p = bank()
            kb.tr(p[0:R, 0:128], tmp[:, 0:R], identf)
            kb.cp(stage[0:R, c * 128:(c + 1) * 128], p[0:R, 0:128], eng="act")
        kb.dma(dram2d, stage[0:R, 0:C], q="sp")

    stg = arF.alloc("stg", [128, 2048])
    for l in range(2):
        kb.dma(stg[l * 5:l * 5 + 4, :], ssm_conv_w[l], q="sp")
        kb.dma(stg[l * 5 + 4:l * 5 + 5, :], ssm_conv_b[l:l + 1, :], q="sp")
    for c in range(16):
        p = bank()
        kb.tr(p[:, 0:10], stg[0:10, c * 128:(c + 1) * 128], V(cst, cst.t[0:10, C_ID:C_ID + 10]))
        kb.cp(pcA[:, c, :], p[:, 0:10], eng=ev())
    stg2 = arF.alloc("stg2", [128, 1024])
    for l in range(2):
        o = l * 37
        kb.dma(stg2[o:o + 31, :], cf_conv_w[l], q="sp")
        kb.dma(stg2[o + 31:o + 32, :], cf_conv_b[l:l + 1, :], q="sp")
        kb.dma(stg2[o + 32:o + 33, :], cf_ln_w[l:l + 1, :], q="sp")
        kb.dma(stg2[o + 33:o + 34, :], cf_ln_b[l:l + 1, :], q="sp")
        kb.dma(stg2[o + 34:o + 37, :], sc_conv_w[l], q="sp")
    for c in range(8):
        p = bank()
        kb.tr(p[:, 0:74], stg2[0:74, c * 128:(c + 1) * 128], V(cst, cst.t[0:74, C_ID:C_ID + 74]))
        kb.cp(pcB[:, c, :], p[:, 0:74], eng=ev())

    def layer_plan(l):
        P = []

        def win(name, c0, n):
            P.append((name, w_in[l].rearrange("(k p) c -> p k c", p=128)[:, :, c0:c0 + n], 16, n))

        def outp(b):
            for blk in range(4):
                P.append(("wout%d_%d" % (b, blk),
                          w_outs[b][l].rearrange("(k p) c -> p k c", p=128)[:, :, blk * 512:(blk + 1) * 512], 8, 512))
                win("merge%d_%d" % (b, blk), MG0 + b * 2048 + blk * 512, 512)
        if 2 in branches:
            for nm, c0 in (("scc", SCC0), ("scv", SCV0), ("scb", SCB0), ("scg", SCG0)):
                for b in range(2):
                    win("%s%d" % (nm, b), c0 + 512 * b, 512)
            outp(2)
        if 1 in branches:
            for b in range(2):
                win("cfb%d" % b, CFB0 + 512 * b, 512)
            for b in range(2):
                win("cfa%d" % b, CFA0 + 512 * b, 512)
            for b in range(2):
                win("cfg%d" % b, CFG0 + 512 * b, 512)
            outp(1)
        if 0 in branches:
            for b in range(4):
                win("xbc%d" % b, XBC0 + 512 * b, 512)
            for b in range(2):
                win("z%d" % b, Z0 + 512 * b, 512)
            win("dt", DT0, 16)
            outp(0)
        if 3 in branches:
            for b in range(2):
                win("ag%d" % b, AG0 + 512 * b, 512)
            for b in range(2):
                win("q%d" % b, Q0 + 512 * b, 512)
            win("kv", K0, 512)
            outp(3)
        for blk in range(4):
            P.append(("wo%d" % blk, w_o[l].rearrange("(k p) c -> p k c", p=128)[:, :, blk * 512:(blk + 1) * 512], 16, 512))
        return P

    chunks = CHUNKS if CHUNKS else [("P", i) for i in range(4)] + [("S", 0)]
    plan = []
    for ch in chunks:
        for l in range(2):
            plan += layer_plan(l)
    wst = {"issued": 0, "next": 0}
    NPL = len(layer_plan(0))
    wscr = kb.dram("wscr", [2 * NPL, 128, 8192], BF16, "Internal").ap() if len(chunks) > 1 else None

    def w_issue(j):
        name, src, kc, n = plan[j]
        slot = slots[j % NSLOT]
        bi = j % (2 * NPL)
        if wscr is not None:
            scr = wscr[bi][:, 0:kc * n].rearrange("p (k c) -> p k c", k=kc)
            if j >= 2 * NPL:
                kb.dma(slot[:, 0:kc, 0:n], scr, q="pool")
                return
            kb.dma(slot[:, 0:kc, 0:n], src, q="pool")
            kb.dma(scr, slot[:, 0:kc, 0:n], q="sp")
            return
        if isinstance(src, tuple):
            _, l, m = src
            base = w_in[l].rearrange("(k p) c -> p k c", p=128)
            for half in range(2):
                for r in range(4):
                    c0 = Q0 + m * 512 + half * 256 + r * 64
                    d0 = r * 128 + half * 64
                    kb.dma(slot[:, :, d0:d0 + 64], base[:, :, c0:c0 + 64], q="pool")
        else:
            kb.dma(slot[:, 0:kc, 0:n], src, q="pool")

    def wget(name, hold=0):
        j = wst["next"]
        assert plan[j][0] == name, (plan[j][0], name)
        while wst["issued"] < min(len(plan), j + NSLOT - hold):
            w_issue(wst["issued"])
            wst["issued"] += 1
        wst["next"] += 1
        return slots[j % NSLOT]

    def proj_fm(slot, kc, col_lo, ncols, rhs, T):
        p = bank()
        for k in range(kc):
            kb.mm(p[0:ncols, 0:T], slot[:, k, col_lo:col_lo + ncols], rhs[:, k, 0:T], start=(k == 0), stop=(k == kc - 1))
        return p

    def proj_tm(slot, kc, col_lo, ncols, j, p=None):
        if p is None:
            p = bank()
        for k in range(kc):
            kb.mm(p[:, 0:ncols], xnT[:, k, j * 128:(j + 1) * 128], slot[:, k, col_lo:col_lo + ncols],
                  start=(k == 0), stop=(k == kc - 1))
        return p

    def rstd_of(src, C, eps, junk):
        kb.act(junk, src, AF.Square, accum=ss[:, 0:1])
        kb.ts(ss[:, 1:2], ss[:, 0:1], 1.0 / C, eps, ALU.mult, ALU.add)
        kb.act(ss[:, 2:3], ss[:, 1:2], AF.Sqrt)
        kb.recip(ss[:, 3:4], ss[:, 2:3])
        return ss[:, 3:4]

    def transpose_to_fm(src_bf, ncol_chunks, dst_fn):
        for c0 in range(0, ncol_chunks, 8):
            n = min(8, ncol_chunks - c0)
            p = bbank()
            for i in range(n):
                kb.tr(p[:, i * 128:(i + 1) * 128], src_bf[:, (c0 + i) * 128:(c0 + i + 1) * 128], identb[:, :])
            kb.cp(dst_fn(c0, n), V(p, p.t[:, 0:n * 128].rearrange("p (c t) -> p c t", c=n)), eng=ev())

    def seqv(buf, ch, nseq, W, lo, L):
        if nseq == 1:
            return buf[:, ch, lo:lo + L]
        return V(buf, buf.t[:, ch, :].rearrange("p (s w) -> p s w", s=nseq)[:, :, lo:lo + L])

    def psv(p, nseq, L):
        if nseq == 1:
            return p[:, 0:L]
        return V(p, p.t[:, 0:nseq * L].rearrange("p (s w) -> p s w", s=nseq))

    def branch_out(l, b, yT, T, first):
        for blk in range(4):
            so = wget("wout%d_%d" % (b, blk))
            sg = wget("merge%d_%d" % (b, blk), hold=1)
            for sub in range(4):
                cb = blk * 4 + sub
                po = proj_fm(so, 8, sub * 128, 128, yT, T)
                pg = proj_fm(sg, 16, sub * 128, 128, xnT, T)
                g = gtmp[rot_g[0] % 2]
                rot_g[0] += 1
                kb.act(g[:, 0:T], pg[:, 0:T], AF.Sigmoid)
                if first:
                    kb.tt(hacc[:, cb, 0:T], g[:, 0:T], po[:, 0:T], ALU.mult)
                else:
                    kb.tt(g[:, 0:T], g[:, 0:T], po[:, 0:T], ALU.mult)
                    kb.tt(hacc[:, cb, 0:T], hacc[:, cb, 0:T], g[:, 0:T], ALU.add, eng="dve")

    rot_g = [0]
    gtmp = [None, None]

    def alloc_gtmp():
        gtmp[0] = arF.alloc("gt0", [128, 512])
        gtmp[1] = arF.alloc("gt1", [128, 512])

    for (kind, ci) in chunks:
        isP = kind == "P"
        T = 512 if isP else 128
        NT = T // 128
        nseq = 1 if isP else 16
        L = T // nseq
        last_chunk = isP and ci == max(c_[1] for c_ in chunks if c_[0] == 'P')
        arF.reset(); arB.reset()
        src = xp[ci * 512:(ci + 1) * 512, :] if isP else xs
        kb.dma(x[:, 0:NT, :], src.rearrange("(j p) d -> p j d", p=128), q="sp")

        for l in range(2):
            arF.reset(); arB.reset()
            rowbuf = arF.alloc("rowbuf", [128, D])
            kb.dma(rowbuf[:, :], norm_w[l].partition_broadcast(128), q="sp")
            xnb = arB.alloc("xnb", [128, D])
            for j in range(NT):
                r = rstd_of(x[:, j, :], D, 1e-6, xnb[:, :])
                kb.stt(xnb[:, :], x[:, j, :], r, rowbuf[:, :], ALU.mult, ALU.mult)
                transpose_to_fm(xnb, 16, lambda c0, n, j=j: xnT[:, c0:c0 + n, j * 128:(j + 1) * 128])
            first = True

            if 2 in branches:
                arF.reset(); arB.reset(); alloc_gtmp()
                Wc = 2 + L
                cbuf = arF.alloc("cbuf", [128, 8, nseq * Wc])
                acc = arF.alloc("acc", [128, 8, T])
                yT = arB.alloc("yT", [128, 8, T])
                if not isP:
                    stgc = arF.alloc("stgc", [128, 1024])
                    load_T(st_csc[l], 32, 1024,
                           lambda c: V(cbuf, cbuf.t[:, c, :].rearrange("p (s w) -> p s w", s=16)[:, :, 0:2]), stgc, grp=16)
                for blk in range(2):
                    s_ = wget("scc%d" % blk)
                    for sub in range(4):
                        ch = blk * 4 + sub
                        p = proj_fm(s_, 16, sub * 128, 128, xnT, T)
                        kb.cp(seqv(cbuf, ch, nseq, Wc, 2, L), psv(p, nseq, L), eng="act")
                        if isP:
                            kb.cp(cbuf[:, ch, 0:2], tailC[l][:, ch, :], eng="act")
                for blk in range(2):
                    s_ = wget("scv%d" % blk)
                    for sub in range(4):
                        ch = blk * 4 + sub
                        p = proj_fm(s_, 16, sub * 128, 128, xnT, T)
                        kb.tt(seqv(cbuf, ch, nseq, Wc, 2, L), seqv(cbuf, ch, nseq, Wc, 2, L), psv(p, nseq, L), ALU.mult)
                        if isP:
                            kb.cp(tailC[l][:, ch, :], cbuf[:, ch, L:L + 2], eng="act")
                        a3 = V(acc, acc.t[:, ch, :].rearrange("p (s w) -> p s w", s=nseq)) if nseq > 1 else acc[:, ch, :]
                        w0 = l * 37 + 34
                        kb.ts(a3, seqv(cbuf, ch, nseq, Wc, 0, L), pcB[:, ch, w0:w0 + 1], None, ALU.mult)
                        for k in (1, 2):
                            kb.stt(a3, seqv(cbuf, ch, nseq, Wc, k, L), pcB[:, ch, w0 + k:w0 + k + 1], a3, ALU.mult, ALU.add)
                if not isP:
                    stgo = arF.alloc("stgo", [128, 1024]); tmpo = arF.alloc("tmpo", [128, 128])
                    store_T(lambda c: V(cbuf, cbuf.t[:, c, :].rearrange("p (s w) -> p s w", s=16)[:, :, 8:10]),
                            32, 1024, s_csc[l], stgo, tmpo, grp=16)
                for blk in range(2):
                    s_ = wget("scb%d" % blk)
                    for sub in range(4):
                        ch = blk * 4 + sub
                        p = proj_fm(s_, 16, sub * 128, 128, xnT, T)
                        kb.tt(acc[:, ch, :], acc[:, ch, :], p[:, 0:T], ALU.mult)
                for blk in range(2):
                    s_ = wget("scg%d" % blk)
                    for sub in range(4):
                        ch = blk * 4 + sub
                        p = proj_fm(s_, 16, sub * 128, 128, xnT, T)
                        g = gtmp[rot_g[0] % 2]; rot_g[0] += 1
                        kb.act(g[:, 0:T], p[:, 0:T], AF.Silu)
                        kb.tt(yT[:, ch, :], acc[:, ch, :], g[:, 0:T], ALU.mult)
                if isP and ci == 0 and l == 0:
                    dump(0, V(acc, acc.t[:, :, :].rearrange("p a b -> p (a b)")), 4096)
                    dump(1, V(yT, yT.t[:, :, :].rearrange("p a b -> p (a b)")), 4096)
                branch_out(l, 2, yT, T, first)
                if isP and ci == 0 and l == 0:
                    dump(2, V(hacc, hacc.t[:, 0:8, :].rearrange("p a b -> p (a b)")), 4096)
                first = False

            if 1 in branches:
                arF.reset(); arB.reset(); alloc_gtmp()
                Wb = 30 + L
                ubuf = arF.alloc("ubuf", [128, 8, nseq * Wb])
                acc = arF.alloc("acc", [128, 8, T])
                yT = arB.alloc("yT", [128, 8, T])
                if isP:
                    ubf = arB.alloc("ubf", [128, Wb])
                    dg = [arB.alloc("dg%d" % i, [128, 128]) for i in range(4)]
                if not isP:
                    stgc = arF.alloc("stgc", [128, 1024])
                    for rb in range(4):
                        load_T(st_ccf[l][rb * 120:(rb + 1) * 120, :], 120, 1024,
                               lambda c, rb=rb: V(ubuf, ubuf.t[:, c, :].rearrange("p (s w) -> p s w", s=16)[:, rb * 4:(rb + 1) * 4, 0:30]),
                               stgc, grp=4)
                for blk in range(2):
                    s_ = wget("cfb%d" % blk)
                    for sub in range(4):
                        ch = blk * 4 + sub
                        p = proj_fm(s_, 16, sub * 128, 128, xnT, T)
                        kb.act(seqv(ubuf, ch, nseq, Wb, 30, L), psv(p, nseq, L), AF.Sigmoid)
                        if isP:
                            kb.cp(ubuf[:, ch, 0:30], tailB[l][:, ch, :], eng="act")
                w0 = l * 37
                for blk in range(2):
                    s_ = wget("cfa%d" % blk)
                    for sub in range(4):
                        ch = blk * 4 + sub
                        p = proj_fm(s_, 16, sub * 128, 128, xnT, T)
                        kb.tt(seqv(ubuf, ch, nseq, Wb, 30, L), seqv(ubuf, ch, nseq, Wb, 30, L), psv(p, nseq, L), ALU.mult)
                        if isP:
                            kb.cp(tailB[l][:, ch, :], ubuf[:, ch, L:L + 30], eng="act")
                        if isP:
                            kb.cp(ubf[:, 0:Wb], ubuf[:, ch, 0:Wb], eng="act")
                            pc = bank()
                            for k in range(31):
                                dgk = dg[k % 4]
                                kb.ts(dgk[:, :], identb[:, :], pcB[:, ch, w0 + k:w0 + k + 1], None, ALU.mult)
                                kb.mm(pc[:, 0:T], dgk[:, :], ubf[:, k:k + L], start=(k == 0), stop=(k == 30), sig=True)
                            kb.act(acc[:, ch, :], pc[:, 0:T], AF.Identity, bias=pcB[:, ch, w0 + 31:w0 + 32])
                        else:
                            a3 = V(acc, acc.t[:, ch, :].rearrange("p (s w) -> p s w", s=nseq))
                            kb.ts(a3, seqv(ubuf, ch, nseq, Wb, 0, L), pcB[:, ch, w0:w0 + 1], pcB[:, ch, w0 + 31:w0 + 32], ALU.mult, ALU.add)
                            for k in range(1, 31):
                                kb.stt(a3, seqv(ubuf, ch, nseq, Wb, k, L), pcB[:, ch, w0 + k:w0 + k + 1], a3, ALU.mult, ALU.add)
                if not isP:
                    stgo = arF.alloc("stgo", [128, 1024]); tmpo = arF.alloc("tmpo", [128, 128])
                    for rb in range(4):
                        store_T(lambda c, rb=rb: V(ubuf, ubuf.t[:, c, :].rearrange("p (s w) -> p s w", s=16)[:, rb * 4:(rb + 1) * 4, 8:38]),
                                120, 1024, s_ccf[l][rb * 120:(rb + 1) * 120, :], stgo, tmpo, grp=4)
                sq = gtmp[0]; mu = arF.alloc("mu", [128, T]); rs = arF.alloc("rs", [128, T])
                onesN = cview(C_ONES, 128)
                pm = bank()
                for ch in range(8):
                    kb.mm(pm[:, 0:T], onesN, acc[:, ch, :], start=(ch == 0), stop=(ch == 7))
                kb.ts(mu[:, :], pm[:, 0:T], 1.0 / 1024, None, ALU.mult)
                pv = bank()
                for ch in range(8):
                    kb.act(sq[:, 0:T], acc[:, ch, :], AF.Square)
                    kb.mm(pv[:, 0:T], onesN, sq[:, 0:T], start=(ch == 0), stop=(ch == 7), sig=True)
                kb.tt(sq[:, 0:T], mu[:, :], mu[:, :], ALU.mult)
                kb.stt(rs[:, :], pv[:, 0:T], 1.0 / 1024, sq[:, 0:T], ALU.mult, ALU.subtract)
                kb.ts(rs[:, :], rs[:, :], 1e-5, None, ALU.add)
                kb.act(rs[:, :], rs[:, :], AF.Sqrt)
                kb.recip(rs[:, :], rs[:, :])
                for ch in range(8):
                    kb.tt(acc[:, ch, :], acc[:, ch, :], mu[:, :], ALU.subtract)
                    kb.tt(acc[:, ch, :], acc[:, ch, :], rs[:, :], ALU.mult)
                    kb.act(acc[:, ch, :], acc[:, ch, :], AF.Silu, scale=pcB[:, ch, w0 + 32:w0 + 33], bias=pcB[:, ch, w0 + 33:w0 + 34])
                for blk in range(2):
                    s_ = wget("cfg%d" % blk)
                    for sub in range(4):
                        ch = blk * 4 + sub
                        p = proj_fm(s_, 16, sub * 128, 128, xnT, T)
                        g = gtmp[rot_g[0] % 2]; rot_g[0] += 1
                        kb.act(g[:, 0:T], p[:, 0:T], AF.Silu)
                        kb.tt(yT[:, ch, :], acc[:, ch, :], g[:, 0:T], ALU.mult)
                branch_out(l, 1, yT, T, first)
                first = False


            if 0 in branches:
                arF.reset(); arB.reset()
                Wa = 3 + L
                xbcT = arB.alloc("xbcT", [128, 16, T])
                zs = arB.alloc("zs", [128, NT, 1024])
                x_tok = arB.alloc("x_tok", [128, 1024]); xdt = arB.alloc("xdt", [128, 1024]); xdec = arB.alloc("xdec", [128, 1024])
                B_tok = arB.alloc("B_tok", [128, 512])
                if isP:
                    MTb = arB.alloc("MTb", [128, 2048])
                    hTbuf, hToff = MTb, 1024
                else:
                    Bm = arB.alloc("Bm", [128, 512])
                    MT = arB.alloc("MT", [128, 4, 128]); hT_bf = arB.alloc("hT_bf", [128, 1024])
                    hTbuf, hToff = hT_bf, 0
                cbufA = [arF.alloc("cbA%d" % i, [128, nseq * Wa]) for i in range(2)]
                accA = arF.alloc("accA", [128, T])
                dtt = arF.alloc("dtt", [128, NT, 16]); dat = arF.alloc("dat", [128, NT, 16])
                sp_ = arF.alloc("sp_", [128, 32]); sc = arF.alloc("sc", [128, 64])
                rowb = arF.alloc("rowb", [128, 1024])
                kb.dma(rowb[:, :], ssm_norm_w[l].partition_broadcast(128), q="sp")
                tmpo = arF.alloc("tmpo", [128, 128])
                if not isP:
                    stgA = arF.alloc("stgA", [128, 1024]); hist = arF.alloc("hist", [128, 16, 48])
                    for half in range(2):
                        load_T(st_cssm[l][:, half * 1024:(half + 1) * 1024], 48, 1024,
                               lambda c, half=half: hist[:, half * 8 + c, :], stgA)
                w0 = l * 5
                for blk in range(4):
                    s_ = wget("xbc%d" % blk)
                    for sub in range(4):
                        ch = blk * 4 + sub
                        p = proj_fm(s_, 16, sub * 128, 128, xnT, T)
                        cb = cbufA[ch % 2]

                        def cbv(lo, n, cb=cb):
                            if isP:
                                return cb[:, lo:lo + n]
                            return V(cb, cb.t[:, :].rearrange("p (s w) -> p s w", s=16)[:, :, lo:lo + n])
                        kb.cp(cbv(3, L), psv(p, nseq, L), eng="act")
                        if isP:
                            kb.cp(cb[:, 0:3], tailA[l][:, ch, :], eng="act")
                            kb.cp(tailA[l][:, ch, :], cb[:, L:L + 3], eng="act")
                        else:
                            hv = V(hist, hist.t[:, ch, :].rearrange("p (s w) -> p s w", s=16))
                            kb.cp(cbv(0, 3), hv, eng="act")
                            kb.cp(hv, cbv(8, 3), eng="act")
                        a3 = accA[:, 0:T] if isP else V(accA, accA.t[:, 0:T].rearrange("p (s w) -> p s w", s=16))
                        kb.ts(a3, cbv(0, L), pcA[:, ch, w0:w0 + 1], None, ALU.mult)
                        for k in range(1, 4):
                            kb.stt(a3, cbv(k, L), pcA[:, ch, w0 + k:w0 + k + 1], a3, ALU.mult, ALU.add)
                        kb.act(xbcT[:, ch, :], accA[:, 0:T], AF.Silu, bias=pcA[:, ch, w0 + 4:w0 + 5])
                if not isP:
                    for half in range(2):
                        store_T(lambda c, half=half: hist[:, half * 8 + c, :], 48, 1024,
                                s_cssm[l][:, half * 1024:(half + 1) * 1024], stgA, tmpo)
                for blk in range(2):
                    s_ = wget("z%d" % blk)
                    for j in range(NT):
                        p = proj_tm(s_, 16, 0, 512, j)
                        kb.act(zs[:, j, blk * 512:(blk + 1) * 512], p[:, 0:512], AF.Silu)
                s_ = wget("dt")
                one_col = cview(C_ONES, 1)
                for j in range(NT):
                    p = proj_tm(s_, 16, 0, 16, j)
                    kb.tt(sp_[:, 0:16], p[:, 0:16], r16[:, 0, l, :], ALU.add)
                    kb.act(sp_[:, 16:32], sp_[:, 0:16], AF.Abs)
                    kb.act(sp_[:, 16:32], sp_[:, 16:32], AF.Exp, scale=-1.0)
                    kb.act(sp_[:, 16:32], sp_[:, 16:32], AF.Ln, bias=one_col)
                    kb.ts(sp_[:, 0:16], sp_[:, 0:16], 0.0, None, ALU.max)
                    kb.tt(dtt[:, j, :], sp_[:, 0:16], sp_[:, 16:32], ALU.add)
                    kb.tt(dat[:, j, :], dtt[:, j, :], r16[:, 1, l, :], ALU.mult)
                NG = 16 if isP else 4
                rhs_cs = arF.alloc("rhs_cs", [128, NG, 128]); cbtm = arF.alloc("cbtm", [128, NG // 4, 128])
                dE = arF.alloc("dE", [128, NG, 128]); pyo_sb = arF.alloc("pyo_sb", [128, 8, 128])
                t1 = arF.alloc("t1", [128, 1024]); ssrep = arF.alloc("ssrep", [128, 128])
                cdecT = arF.alloc("cdecT", [128, 8, 16])
                if not isP:
                    h0nat = arF.alloc("h0nat", [128, 8, 128]); hnew = arF.alloc("hnew", [128, 8, 128])
                tri = cview(C_TRIP if isP else C_TRIS, 128)
                same = cview(C_ONES if isP else C_SAMES, 128)
                ones_m = cview(C_ONES, 128)
                lastsel = cview(C_LASTP, 1) if isP else cview(C_LASTS, 16)
                lrot = [0]

                def lbank():
                    lrot[0] ^= 1
                    return pf[2 + lrot[0]]
                FB = arB.F
                FF = arF.F
                for c in range(NT):
                    t0 = c * 128
                    pbk = bbank()
                    for i in range(8):
                        kb.tr(pbk[:, i * 128:(i + 1) * 128], xbcT[:, i, t0:t0 + 128], identb[:, :])
                    kb.cp(x_tok[:, :], pbk[:, 0:1024], eng="act")
                    pbk = bbank()
                    for i in range(4):
                        kb.tr(pbk[:, i * 128:(i + 1) * 128], xbcT[:, 8 + i, t0:t0 + 128], identb[:, :])
                    kb.cp(B_tok[:, :], pbk[:, 0:512], eng="dve")
                    p = lbank()
                    kb.mm(p[:, 0:16], tri, dat[:, c, :])
                    kb.cp(sc[:, 0:16], p[:, 0:16], eng="dve")
                    p = lbank()
                    kb.mm(p[:, 0:16], same, dat[:, c, :])
                    kb.cp(sc[:, 48:64], p[:, 0:16], eng="dve")
                    kb.tt(sc[:, 16:32], sc[:, 48:64], sc[:, 0:16], ALU.subtract)
                    kb.act(sc[:, 16:32], sc[:, 16:32], AF.Exp)
                    kb.act(sc[:, 32:48], sc[:, 0:16], AF.Exp)
                    x3 = V(x_tok, x_tok.t[:, :].rearrange("p (h d) -> p h d", h=16))
                    xdt3 = V(xdt, xdt.t[:, :].rearrange("p (h d) -> p h d", h=16))
                    xdec3 = V(xdec, xdec.t[:, :].rearrange("p (h d) -> p h d", h=16))
                    kb.tt(xdt3, x3, dtt.pat(c * 16, [[FF, 128], [1, 16], [0, 64]]), ALU.mult)
                    kb.tt(xdec3, xdt3, sc.pat(16, [[FF, 128], [1, 16], [0, 64]]), ALU.mult)
                    p = lbank()
                    for c8 in range(8):
                        kb.cp(V(ssrep, ssrep.t[:, :].rearrange("p (h d) -> p h d", h=2)),
                              sc.pat(48 + 2 * c8, [[FF, 128], [1, 2], [0, 64]]), eng="dve")
                        kb.mm(p[:, c8 * 16:c8 * 16 + nseq], ssrep[:, :], lastsel)
                    kb.act(cdecT[:, :, 0:nseq], V(p, p.t[:, 0:128].rearrange("p (a b) -> p a b", a=8)[:, :, 0:nseq]), AF.Exp)
                    if isP:
                        tri_off = C_TRIP
                        kb.tt(rhs_cs[:, :, :], dat.pat(c * 16, [[FF, 128], [1, 16], [0, 128]]),
                              V(cst, bass.AP(cst.t, tri_off, [[C_END, 128], [0, 16], [1, 128]])), ALU.mult)
                        for g in range(4):
                            kb.mm(pf[g][:, 0:512], ones_m,
                                  V(rhs_cs, rhs_cs.t[:, 4 * g:4 * g + 4, :].rearrange("p a b -> p (a b)")))
                        for h in range(16):
                            kb.ts(dE[:, h, :], pf[h // 4][:, (h % 4) * 128:(h % 4 + 1) * 128], sc[:, h:h + 1], 0.0,
                                  ALU.subtract, ALU.min)
                        kb.act(dE[:, :, :], dE[:, :, :], AF.Exp)
                        for g in range(4):
                            kb.mm(pw[:, g * 128:(g + 1) * 128], xbcT[:, 8 + g, t0:t0 + 128], xbcT[:, 12 + g, t0:t0 + 128])
                        kb.tt(cbtm[:, :, :], V(pw, pw.t[:, 0:512].rearrange("p (a b) -> p a b", a=4)),
                              V(cst, bass.AP(cst.t, tri_off, [[C_END, 128], [0, 4], [1, 128]])), ALU.mult)
                        kb.tt(V(MTb, MTb.t[:, :].rearrange("p (g h l) -> p g h l", g=4, h=4)),
                              V(dE, dE.t[:, :, :].rearrange("p (g h) l -> p g h l", g=4)),
                              cbtm.pat(0, [[FF, 128], [128, 4], [0, 4], [1, 128]]), ALU.mult)
                        for h in range(16):
                            kb.mm(pw[:, h * 64:(h + 1) * 64], MTb[:, h * 128:(h + 1) * 128], xdt[:, h * 64:(h + 1) * 64])
                    for g in range(0 if isP else 4):
                        kb.tt(rhs_cs[:, :, :], dat.pat(c * 16 + 4 * g, [[FF, 128], [1, 4], [0, 128]]),
                              V(cst, bass.AP(cst.t, (C_TRIP if isP else C_TRIS), [[C_END, 128], [0, 4], [1, 128]])), ALU.mult)
                        pr = lbank()
                        kb.mm(pr[:, 0:512], ones_m, V(rhs_cs, rhs_cs.t[:, :, :].rearrange("p a b -> p (a b)")))
                        for hl in range(4):
                            h = 4 * g + hl
                            kb.ts(dE[:, hl, :], pr[:, hl * 128:(hl + 1) * 128], sc[:, h:h + 1], 0.0, ALU.subtract, ALU.min)
                        kb.act(dE[:, :, :], dE[:, :, :], AF.Exp)
                        pc = lbank()
                        kb.mm(pc[:, 0:128], xbcT[:, 8 + g, t0:t0 + 128], xbcT[:, 12 + g, t0:t0 + 128])
                        kb.tt(cbtm[:, 0, :], pc[:, 0:128], tri, ALU.mult)
                        kb.tt(MT[:, :, :], dE[:, :, :], cbtm.pat(0, [[FF, 128], [0, 4], [1, 128]]), ALU.mult)
                        for hl in range(4):
                            h = 4 * g + hl
                            kb.mm(pw[:, h * 64:(h + 1) * 64], MT[:, hl, :], xdt[:, h * 64:(h + 1) * 64])
                    pyo = [pf[0], pf[1]]
                    for s in range(nseq):
                        if isP:
                            h0 = hnat[l]
                        else:
                            h0 = h0nat
                            kb.dma(h0nat[:, :, :], st_ssm[l, s].rearrange("(c q) n -> q c n", q=128), q="sp")
                        for half in range(2):
                            pq = lbank()
                            for i in range(4):
                                kb.tr(pq[:, i * 128:(i + 1) * 128], h0[:, half * 4 + i, :], identf)
                            kb.cp(hTbuf[:, hToff + half * 512:hToff + (half + 1) * 512], pq[:, 0:512], eng=ev())
                        for c8 in range(8):
                            g = c8 // 2
                            Lt = 128 if isP else 8
                            kb.mm(pyo[c8 // 4][:, (c8 % 4) * 128 + s * Lt:(c8 % 4) * 128 + (s + 1) * Lt],
                                  hTbuf[:, hToff + c8 * 128:hToff + (c8 + 1) * 128], xbcT[:, 12 + g, t0 + s * Lt:t0 + (s + 1) * Lt])
                        if not isP:
                            kb.ts(Bm[:, :], B_tok[:, :], cview(C_SELS + s, 1), None, ALU.mult)
                        Bsrc = B_tok if isP else Bm
                        for c8 in range(8):
                            g = c8 // 2
                            pst = lbank()
                            kb.mm(pst[:, 0:128], xdec[:, c8 * 128:(c8 + 1) * 128], Bsrc[:, g * 128:(g + 1) * 128])
                            dst = hnat[l][:, c8, :] if isP else hnew[:, c8, :]
                            kb.stt(dst, h0[:, c8, :], cdecT[:, c8, s:s + 1], pst[:, 0:128], ALU.mult, ALU.add)
                        if not isP:
                            kb.dma(s_ssm[l, s].rearrange("(c q) n -> q c n", q=128), hnew[:, :, :], q="sp")
                    for half in range(2):
                        kb.cp(pyo_sb[:, half * 4:(half + 1) * 4, :],
                              V(pyo[half], pyo[half].t[:, 0:512].rearrange("p (a b) -> p a b", a=4)), eng=ev())
                    for half in range(2):
                        pq = pyo[half]
                        for i in range(4):
                            kb.tr(pq[:, i * 128:(i + 1) * 128], pyo_sb[:, half * 4 + i, :], identf)
                        kb.tt(V(t1, t1.t[:, half * 512:(half + 1) * 512].rearrange("p (h d) -> p h d", h=8)),
                              V(pq, pq.t[:, 0:512].rearrange("p (h d) -> p h d", h=8)),
                              sc.pat(32 + half * 8, [[FF, 128], [1, 8], [0, 64]]), ALU.mult)
                    kb.tt(t1[:, :], t1[:, :], pw[:, 0:1024], ALU.add)
                    t2 = V(pyo_sb, pyo_sb.t[:, :, :].rearrange("p a b -> p (a b)"))
                    kb.tt(V(pyo_sb, pyo_sb.t[:, :, :].rearrange("p a (h d) -> p (a h) d", h=2)), x3,
                          V(r16, bass.AP(r16.t, (2 * 2 + l) * 16, [[128, 128], [1, 16], [0, 64]])), ALU.mult)
                    kb.tt(t1[:, :], t1[:, :], t2, ALU.add)
                    kb.tt(t1[:, :], t1[:, :], zs[:, c, :], ALU.mult)
                    r = rstd_of(t1[:, :], 1024, 1e-5, xdt[:, :])
                    kb.stt(xdt[:, :], t1[:, :], r, rowb[:, :], ALU.mult, ALU.mult)
                    transpose_to_fm(xdt, 8, lambda c0, n, t0=t0: xbcT[:, c0:c0 + n, t0:t0 + 128])
                arF.reset(); alloc_gtmp()
                branch_out(l, 0, xbcT, T, first)
                first = False

            if 3 in branches:
                arF.reset(); arB.reset(); alloc_gtmp()
                NK = 2 if isP else 17
                gsil = arB.alloc("gsil", [128, NT, 1024])
                QT = arB.alloc("QT", [128, 8, T])
                KT = arB.alloc("KT", [128, 2, (128 + T) if isP else 17 * 128])
                vtok = arB.alloc("vtok", [128, (1 + NT) if isP else 17, 256])
                Ogb = arB.alloc("Ogb", [128, 1024])
                GH = 4 if isP else 1
                NSET = 2 if isP else 1
                Pbfs = [arB.alloc("Pbf%d" % i, [128, GH * NK * 128]) for i in range(NSET)]
                PTs = [arB.alloc("PT%d" % i, [128, GH * NK, 128]) for i in range(NSET)]
                Pbf, PT = Pbfs[0], PTs[0]
                yT = QT if isP else arB.alloc("yT", [128, 8, T])
                kvf = arF.alloc("kvf", [128, 512])
                Ssbs = [arF.alloc("Ssb%d" % i, [128, GH * NK * 128]) for i in range(NSET)]
                Ssb = Ssbs[0]
                sm = arF.alloc("sm", [128, 8])
                sm4s = [arF.alloc("sm4%d" % i, [128, 24]) for i in range(NSET)]
                Otmps = [arF.alloc("Otmp%d" % i, [128, 256]) for i in range(NSET)]
                koff = 128 if isP else 16 * 128
                voff = 1 if isP else 16
                if not isP:
                    maskS = arF.alloc("maskS", [128, NMASKS])
                    if "m" not in DFLAGS:
                        kb.dma(maskS[:, :], masks_d, q="sp")
                    ckt = [arF.alloc("ckt%d" % i, [128, 256]) for i in range(2)]
                    for s in range(16 if "c" not in DFLAGS else 0):
                        kb.dma(ckt[s % 2][:, :], ck[l, s], q="sp")
                        kb.dma(vtok[:, s, :], cv[l, s], q="pool")
                        for c in range(2):
                            p = bank()
                            kb.tr(p[:, 0:128], ckt[s % 2][:, c * 128:(c + 1) * 128], identf)
                            kb.cp(KT[:, c, s * 128:(s + 1) * 128], p[:, 0:128], eng=ev())
                    if "d" not in DFLAGS:
                        kb.dma(s_k[l][:, 0:120, :], ck[l][:, 8:128, :], q="sp")
                        kb.dma(s_v[l][:, 0:120, :], cv[l][:, 8:128, :], q="sp")
                elif "p" not in DFLAGS:
                    for c in range(2):
                        kb.cp(KT[:, c, 0:128], KTprev[l][:, c, :], eng="pool")
                    kb.cp(vtok[:, 0, :], Vprev[l][:, :], eng="pool")
                for blk in range(2):
                    s_ = wget("ag%d" % blk)
                    for j in range(NT if "1" not in DFLAGS else 0):
                        p = proj_tm(s_, 16, 0, 512, j)
                        kb.act(gsil[:, j, blk * 512:(blk + 1) * 512], p[:, 0:512], AF.Silu)
                qtok = Pbf
                for m_ in range(2):
                    s_ = wget("q%d" % m_)
                    for j in range(NT if "2" not in DFLAGS else 0):
                        p = proj_tm(s_, 16, 0, 512, j)
                        kb.cp(V(qtok, qtok.t[:, 0:512].rearrange("p (r hf d) -> p r hf d", r=4, hf=2)),
                              V(p, p.t[:, 0:512].rearrange("p (hf r d) -> p r hf d", hf=2, r=4)), eng=ev())
                        transpose_to_fm(qtok, 4, lambda c0, n, j=j, m_=m_: QT[:, 4 * m_ + c0:4 * m_ + c0 + n, j * 128:(j + 1) * 128])
                s_ = wget("kv")
                for j in range(NT if "3" not in DFLAGS else 0):
                    p = proj_tm(s_, 16, 0, 512, j)
                    kb.cp(vtok[:, voff + j, :], p[:, 256:512], eng="act")
                    if (last_chunk and j == NT - 1) or not isP:
                        kb.cp(kvf[:, :], p[:, 0:512], eng="dve")
                for c in range(2 if "4" not in DFLAGS else 0):
                    p = proj_fm(s_, 16, c * 128, 128, xnT, T)
                    kb.cp(KT[:, c, koff:koff + T], p[:, 0:T], eng=ev())
                if last_chunk and "k" not in DFLAGS:
                    kb.dma(p_k[l], kvf[:, 0:256], q="sp")
                    kb.dma(p_v[l], kvf[:, 256:512], q="sp")
                if not isP and "o" not in DFLAGS:
                    for s in range(16):
                        kb.dma(s_k[l][s, 120:128, :], kvf[s * 8:(s + 1) * 8, 0:256], q="sp")
                        kb.dma(s_v[l][s, 120:128, :], kvf[s * 8:(s + 1) * 8, 256:512], q="sp")
                if "a" in DFLAGS:
                    kb.memset(V(yT, yT.t[:, :, :]), 0.0)
                FFd = arF.F

                def att_cfg(j):
                    if ci == 0 and j == 0:
                        return 128, [1], cview(C_MASKP + 128, 128)
                    return j * 128, [j, j + 1], cview(C_MASKP, 256)

                def st_scores(it, bs):
                    j, g = it
                    k0, vidx, mask = att_cfg(j)
                    ncol = len(vidx) * 128
                    m_, half = g // 2, g % 2
                    base = 64 * half
                    for r in range(4):
                        p = bank()
                        kb.mm(p[:, 0:ncol], QT[base:base + 64, 4 * m_ + r, j * 128:(j + 1) * 128],
                              KT[base:base + 64, m_, k0:k0 + ncol])
                        kb.stt(Ssbs[bs][:, r * ncol:(r + 1) * ncol], p[:, 0:ncol], 0.125, mask, ALU.mult, ALU.add)

                def st_stats(it, bs):
                    j, g = it
                    k0, vidx, mask = att_cfg(j)
                    ncol = len(vidx) * 128
                    sm4 = sm4s[bs]
                    Ss3 = V(Ssbs[bs], Ssbs[bs].t[:, 0:4 * ncol].rearrange("p (r c) -> p r c", r=4))
                    sk4 = r16[:, 3, l, 4 * g:4 * g + 4]
                    kb.red(sm4[:, 0:4], Ss3, ALU.max)
                    kb.tt(sm4[:, 0:4], sm4[:, 0:4], sk4, ALU.max)
                    kb.ts(sm4[:, 4:8], sm4[:, 0:4], -1.0, None, ALU.mult)
                    kb.tt(sm4[:, 12:16], sk4, sm4[:, 4:8], ALU.add)

                def st_exp(it, bs):
                    j, g = it
                    k0, vidx, mask = att_cfg(j)
                    ncol = len(vidx) * 128
                    sm4 = sm4s[bs]
                    for r in range(4):
                        kb.act(Pbfs[bs][:, r * ncol:(r + 1) * ncol], Ssbs[bs][:, r * ncol:(r + 1) * ncol], AF.Exp,
                               bias=sm4[:, 4 + r:5 + r], accum=sm4[:, 8 + r:9 + r])
                    kb.act(sm4[:, 12:16], sm4[:, 12:16], AF.Exp)
                    kb.tt(sm4[:, 16:20], sm4[:, 8:12], sm4[:, 12:16], ALU.add)
                    kb.recip(sm4[:, 20:24], sm4[:, 16:20])

                def st_tr(it, bs):
                    j, g = it
                    k0, vidx, mask = att_cfg(j)
                    nk = len(vidx)
                    pbk = bbank()
                    for q_ in range(4 * nk):
                        kb.tr(pbk[:, q_ * 128:(q_ + 1) * 128], Pbfs[bs][:, q_ * 128:(q_ + 1) * 128], identb[:, :])
                    kb.cp(PTs[bs][:, 0:4 * nk, :], V(pbk, pbk.t[:, 0:4 * nk * 128].rearrange("p (c t) -> p c t", c=4 * nk)), eng=ev())

                def st_pv(it, bs):
                    j, g = it
                    k0, vidx, mask = att_cfg(j)
                    nk = len(vidx)
                    po = bank()
                    for r in range(4):
                        for i, vi in enumerate(vidx):
                            kb.mm(po[:, r * 64:(r + 1) * 64], PTs[bs][:, r * nk + i, :], vtok[:, vi, g * 64:(g + 1) * 64],
                                  start=(i == 0), stop=(i == nk - 1))
                    kb.tt(V(Otmps[bs], Otmps[bs].t[:, :].rearrange("p (r d) -> p r d", r=4)),
                          V(po, po.t[:, 0:256].rearrange("p (r d) -> p r d", r=4)),
                          sm4s[bs].pat(20, [[FFd, 128], [1, 4], [0, 64]]), ALU.mult)
                    kb.tt(Ogb[:, g * 256:(g + 1) * 256], Otmps[bs][:, :], gsil[:, j, g * 256:(g + 1) * 256], ALU.mult)
                    if g == 3:
                        transpose_to_fm(Ogb, 8, lambda c0, n, j=j: yT[:, c0:c0 + n, j * 128:(j + 1) * 128])

                if isP and "a" not in DFLAGS:
                    items = [(j, g) for j in range(NT) for g in range(4)]
                    stages = [st_scores, st_stats, st_exp, st_tr, st_pv]
                    for i in range(0, len(items), 2):
                        pair = items[i:i + 2]
                        for s_i in range(len(stages)):
                            for k_, it in enumerate(pair):
                                stages[s_i](it, (i + k_) % 2)
                for j in range(NT if ("a" not in DFLAGS and not isP) else 0):
                    k0, vidx, mask = 0, list(range(17)), maskS[:, :]
                    nk = len(vidx)
                    ncol = nk * 128
                    for h in range(0 if isP else 16):
                        m_, half, r = h // 8, (h % 8) // 4, h % 4
                        qch, base, g = 4 * m_ + r, 64 * half, h // 4
                        for c0 in range(0, ncol, 512):
                            n = min(512, ncol - c0)
                            p = bank()
                            kb.mm(p[:, 0:n], QT[base:base + 64, qch, j * 128:(j + 1) * 128],
                                  KT[base:base + 64, m_, k0 + c0:k0 + c0 + n])
                            kb.stt(Ssb[:, c0:c0 + n], p[:, 0:n], 0.125, V(mask.buf, mask.ap[:, c0:c0 + n]), ALU.mult, ALU.add)
                        sk = r16[:, 3, l, h:h + 1]
                        kb.red(sm[:, 0:1], Ssb[:, 0:ncol], ALU.max)
                        kb.tt(sm[:, 0:1], sm[:, 0:1], sk, ALU.max)
                        kb.ts(sm[:, 1:2], sm[:, 0:1], -1.0, None, ALU.mult)
                        kb.act(Pbf[:, 0:ncol], Ssb[:, 0:ncol], AF.Exp, bias=sm[:, 1:2], accum=sm[:, 2:3])
                        kb.act(sm[:, 3:4], sk, AF.Exp, bias=sm[:, 1:2])
                        kb.tt(sm[:, 4:5], sm[:, 2:3], sm[:, 3:4], ALU.add)
                        kb.recip(sm[:, 5:6], sm[:, 4:5])
                        for kk in range(0, nk, 8):
                            n = min(8, nk - kk)
                            pbk = bbank()
                            for i in range(n):
                                kb.tr(pbk[:, i * 128:(i + 1) * 128], Pbf[:, (kk + i) * 128:(kk + i + 1) * 128], identb[:, :])
                            kb.cp(PT[:, kk:kk + n, :], V(pbk, pbk.t[:, 0:n * 128].rearrange("p (c t) -> p c t", c=n)), eng=ev())
                        po = bank()
                        for i, vi in enumerate(vidx):
                            kb.mm(po[:, 0:64], PT[:, i, :], vtok[:, vi, g * 64:(g + 1) * 64], start=(i == 0), stop=(i == nk - 1))
                        kb.stt(Ogb[:, h * 64:(h + 1) * 64], po[:, 0:64], sm[:, 5:6], gsil[:, j, h * 64:(h + 1) * 64], ALU.mult, ALU.mult)
                    transpose_to_fm(Ogb, 8, lambda c0, n, j=j: yT[:, c0:c0 + n, j * 128:(j + 1) * 128])
                if isP and "p" not in DFLAGS:
                    for c in range(2):
                        kb.cp(KTprev[l][:, c, :], KT[:, c, T:T + 128], eng="pool")
                    kb.cp(Vprev[l][:, :], vtok[:, NT, :], eng="pool")
                branch_out(l, 3, yT, T, first)
                first = False

            arF.reset(); arB.reset()
            if first:
                kb.memset(V(hacc, hacc.t.ap()), 0.0, eng="pool")
            for blk in range(4):
                s_ = wget("wo%d" % blk)
                for j in range(NT):
                    p = bank()
                    for k in range(16):
                        kb.mm(p[:, 0:512], hacc[:, k, j * 128:(j + 1) * 128], s_[:, k, 0:512], start=(k == 0), stop=(k == 15))
                    kb.tt(x[:, j, blk * 512:(blk + 1) * 512], x[:, j, blk * 512:(blk + 1) * 512], p[:, :], ALU.add)
            if isP and ci == 0 and l == 0:
                dump(3, V(x, x.t[:, 0, :]), 2048)
            if last_chunk:
                stgo = arF.alloc("stgo", [128, 1024]); tmpo = arF.alloc("tmpo", [128, 128])
                if 2 in branches:
                    store_T(lambda c: tailC[l][:, c, :], 2, 1024, p_csc[l], stgo, tmpo)
                if 1 in branches:
                    store_T(lambda c: tailB[l][:, c, :], 30, 1024, p_ccf[l], stgo, tmpo)
                if 0 in branches:
                    for half in range(2):
                        store_T(lambda c, half=half: tailA[l][:, half * 8 + c, :], 3, 1024,
                                p_cssm[l][:, half * 1024:(half + 1) * 1024], stgo, tmpo)
                    kb.dma(p_ssm[l].rearrange("(c q) n -> q c n", q=128), hnat[l][:, :, :], q="sp")

        arF.reset(); arB.reset()
        rowbuf = arF.alloc("rowbuf", [128, D])
        kb.dma(rowbuf[:, :], final_norm_w[0].partition_broadcast(128), q="sp")
        yo = [arF.alloc("yo%d" % i, [128, D]) for i in range(2)]
        for j in range(NT):
            r = rstd_of(x[:, j, :], D, 1e-6, yo[j % 2][:, :])
            kb.stt(yo[j % 2][:, :], x[:, j, :], r, rowbuf[:, :], ALU.mult, ALU.mult)
            dst = y_p[ci * 512 + j * 128: ci * 512 + (j + 1) * 128, :] if isP else y_s
            kb.dma(dst, yo[j % 2][:, :], q="sp")

    assert wst["next"] == len(plan), (wst["next"], len(plan))
    nc = kb.finish()
    print("instructions:", kb.n_inst, "channels:", kb.nchan)
    return nc


_NC_CACHE = {}
BRANCHES = (0, 1, 2, 3)
DBG_CORES = 0
DFLAGS = ""
CHUNKS = None


def kernel(**inp):
    f32 = lambda a: np.ascontiguousarray(np.asarray(a, dtype=np.float32))
    if "nc" not in _NC_CACHE:
        _NC_CACHE["nc"] = build(BRANCHES)
    nc = _NC_CACHE["nc"]
    consts, masks = make_consts()
    shared = {k: f32(inp[k]) for k in (
        "norm_w", "w_in", "ssm_conv_w", "ssm_conv_b", "ssm_dt_bias", "ssm_a_log", "ssm_d", "ssm_norm_w",
        "w_out_ssm", "cf_conv_w", "cf_conv_b", "cf_ln_w", "cf_ln_b", "w_out_cf", "sc_conv_w", "w_out_sc",
        "att_sinks", "w_out_att", "w_o")}
    shared["final_norm_w"] = f32(inp["final_norm_w"]).reshape(1, D)
    shared["consts"] = consts
    shared["masks"] = masks
    xp_ = f32(inp["x_prompt"]); xs_ = f32(inp["x_sample"])
    in_maps = []
    for c in range(8):
        sl = slice(16 * c, 16 * c + 16)
        m = dict(shared)
        m["xp"] = xp_[c % 4]
        m["xs"] = xs_[sl].reshape(128, D)
        m["st_ssm"] = f32(inp["state_ssm"][:, sl]).reshape(2, 16, 1024, 128)
        m["st_cssm"] = f32(inp["state_conv_ssm"][:, sl]).reshape(2, 48, 2048)
        m["st_ccf"] = f32(inp["state_conv_cf"][:, sl]).reshape(2, 480, 1024)
        m["st_csc"] = f32(inp["state_conv_sc"][:, sl]).reshape(2, 32, 1024)
        m["ck"] = f32(inp["cache_k"][:, sl]).reshape(2, 16, 128, 256)
        m["cv"] = f32(inp["cache_v"][:, sl]).reshape(2, 16, 128, 256)
        in_maps.append(m)
    if DBG_CORES:
        return run_bass_kernel_spmd(nc, in_maps[:DBG_CORES], core_ids=list(range(DBG_CORES))).results
    res = run_bass_kernel_spmd(nc, in_maps, core_ids=list(range(8)))
    R = res.results
    cat = lambda key, cores, ax: np.concatenate([R[c][key] for c in cores], axis=ax)
    pc = [0, 1, 2, 3]
    ac = list(range(8))
    y_prompt = np.stack([R[c]["y_p"] for c in pc], 0)
    y_sample = cat("y_s", ac, 0).reshape(128, 8, D)
    p_ssm = np.stack([R[c]["p_ssm"] for c in pc], 1).reshape(2, 4, 16, 64, 128)
    p_cssm = np.stack([R[c]["p_cssm"] for c in pc], 1)
    p_ccf = np.stack([R[c]["p_ccf"] for c in pc], 1)
    p_csc = np.stack([R[c]["p_csc"] for c in pc], 1)
    p_k = np.stack([R[c]["p_k"] for c in pc], 1).reshape(2, 4, 128, 4, 64)
    p_v = np.stack([R[c]["p_v"] for c in pc], 1).reshape(2, 4, 128, 4, 64)
    s_ssm = cat("s_ssm", ac, 1).reshape(2, 128, 16, 64, 128)
    s_cssm = cat("s_cssm", ac, 1).reshape(2, 128, 3, 2048)
    s_ccf = cat("s_ccf", ac, 1).reshape(2, 128, 30, 1024)
    s_csc = cat("s_csc", ac, 1).reshape(2, 128, 2, 1024)
    s_k = cat("s_k", ac, 1).reshape(2, 128, 128, 4, 64)
    s_v = cat("s_v", ac, 1).reshape(2, 128, 128, 4, 64)
    return (y_prompt, y_sample, p_ssm, p_cssm, p_ccf, p_csc, p_k, p_v,
            s_ssm, s_cssm, s_ccf, s_csc, s_k, s_v)
```

```python
import contextlib
import numpy as np
import concourse.bass as bass
import concourse.mybir as mybir

F32 = mybir.dt.float32
BF16 = mybir.dt.bfloat16
AF = mybir.ActivationFunctionType
ALU = mybir.AluOpType
AX = mybir.AxisListType


class Res:
    __slots__ = ("name", "last_write", "reads", "chan", "psum")

    def __init__(self, name):
        self.name = name
        self.psum = False
        self.last_write = None
        self.reads = {}
        self.chan = None


class V:
    __slots__ = ("buf", "ap")

    def __init__(self, buf, ap):
        self.buf = buf
        self.ap = ap


class Buf:
    def __init__(self, kb, name, t, space):
        self.kb = kb
        self.name = name
        self.t = t
        self.space = space
        self.res = Res(name)

    def __getitem__(self, idx):
        return V(self, self.t[idx])

    def pat(self, offset, pattern):
        return V(self, bass.AP(self.t, offset, [list(p) for p in pattern]))


class EngState:
    def __init__(self, name, handle, sem):
        self.name = name
        self.h = handle
        self.sem = sem
        self.count = 0
        self.waited = {}
        self.thunks = []


class KB:
    def __init__(self):
        self.nc = bass.Bass("TRN2", target_bir_lowering=False)
        self.stack = contextlib.ExitStack()
        nc = self.nc
        self.sems = {}
        self.engs = {}
        for name, h in (("pe", nc.tensor), ("act", nc.scalar), ("dve", nc.vector),
                        ("pool", nc.gpsimd), ("sp", nc.sync)):
            sem = self.stack.enter_context(nc.semaphore("sem_" + name))
            self.engs[name] = EngState(name, h, sem)
            self.sems[name] = sem
        self.nchan = 0
        self.chans = {}
        self.n_inst = 0

    def sb(self, name, shape, dtype):
        t = self.stack.enter_context(self.nc.sbuf_tensor(name, list(shape), dtype))
        return Buf(self, name, t, "sb")

    def ps(self, name, shape, dtype):
        t = self.stack.enter_context(self.nc.psum_tensor(name, list(shape), dtype))
        b = Buf(self, name, t, "ps")
        b.res.psum = True
        return b

    def dram(self, name, shape, dtype, kind):
        return self.nc.dram_tensor(name, list(shape), dtype, kind=kind)

    def new_chan(self, key):
        if key in self.chans:
            return key
        sem = self.stack.enter_context(self.nc.semaphore("ch_" + key))
        self.nchan += 1
        assert self.nchan < 130, "too many dma channels"
        self.chans[key] = [sem, 0]
        self.sems[key] = sem
        return key

    def _deps(self, reads, writes):
        deps = {}

        def add(k, v):
            if deps.get(k, 0) < v:
                deps[k] = v
        for r in reads:
            if r.last_write is not None:
                add(*r.last_write)
            if getattr(r, "psum", False):
                for k, v in r.reads.items():
                    add(k, v)
        for w in writes:
            if w.last_write is not None:
                add(*w.last_write)
            for k, v in w.reads.items():
                add(k, v)
        return deps

    def _emit_waits(self, es, deps):
        for k, v in deps.items():
            if k == es.name and k in ("pe", "sp"):
                continue
            if es.waited.get(k, 0) >= v:
                continue
            if k in self.engs and v > self.engs[k].count:
                raise RuntimeError("wait on a not-yet-signaled %s op (value %d > %d): signal that matmul" % (k, v, self.engs[k].count))
            es.waited[k] = v
            sem = self.sems[k]
            es.thunks.append(lambda h=es.h, sem=sem, v=v: h.wait_ge(sem, v))

    def op(self, eng, fn, reads, writes, sig=True):
        es = self.engs[eng]
        reads = [r.buf.res for r in reads if isinstance(r, V)]
        writes = [w.buf.res for w in writes if isinstance(w, V)]
        deps = self._deps(reads, writes)
        self._emit_waits(es, deps)
        self.n_inst += 1
        import sys as _sys
        fr = _sys._getframe(2)
        tag = []
        while fr is not None and len(tag) < 5:
            tag.append(fr.f_lineno)
            fr = fr.f_back
        if sig:
            es.count += 1
            val = es.count

            def th(h=es.h, sem=es.sem, tag=tag):
                try:
                    fn(h).then_inc(sem, 1)
                except Exception:
                    print("FAILED OP at lines", tag)
                    raise
        else:
            val = es.count + 1

            def th(h=es.h, tag=tag):
                try:
                    fn(h)
                except Exception:
                    print("FAILED OP at lines", tag)
                    raise
        es.thunks.append(th)
        for r in reads:
            if r.reads.get(eng, 0) < val:
                r.reads[eng] = val
        for w in writes:
            w.last_write = (eng, val)
            w.reads = {}

    def dma(self, out, in_, q="sp", chan_res=None, **kw):
        es = self.engs[q]
        reads = [in_.buf.res] if isinstance(in_, V) else []
        writes = [out.buf.res] if isinstance(out, V) else []
        owner = (writes + reads)
        if owner:
            res = owner[0]
            if writes:
                if res.chan is None:
                    res.chan = self.new_chan(res.name)
                ck = res.chan
            else:
                ck = self.new_chan(res.name + "_rd")
        else:
            ck = "misc"
            if ck not in self.chans:
                self.new_chan(ck)
        deps = self._deps(reads, writes)
        self._emit_waits(es, deps)
        ch = self.chans[ck]
        ch[1] += 16
        val = ch[1]
        sem = ch[0]
        o = out.ap if isinstance(out, V) else out
        i = in_.ap if isinstance(in_, V) else in_
        self.n_inst += 1
        es.thunks.append(lambda h=es.h, o=o, i=i, sem=sem, kw=kw:
                         h.dma_start(out=o, in_=i, **kw).then_inc(sem, 16))
        for r in reads:
            r.reads[ck] = val
        for w in writes:
            w.last_write = (ck, val)
            w.reads = {}

    @staticmethod
    def _a(x):
        return x.ap if isinstance(x, V) else x

    def mm(self, out, lhsT, rhs, start=True, stop=True, sig=None, **kw):
        if sig is None:
            sig = stop
        o, l, r = out.ap, lhsT.ap, rhs.ap
        self.op("pe", lambda h: h.matmul(o, l, r, start=start, stop=stop, **kw),
                [lhsT, rhs], [out], sig=sig)

    def tr(self, out, in_, ident, sig=True):
        o, i, d = out.ap, in_.ap, ident.ap
        self.op("pe", lambda h: h.transpose(o, i, d), [in_, ident], [out], sig=sig)

    def act(self, out, in_, func, bias=None, scale=None, accum=None, eng="act"):
        o, i = out.ap, in_.ap
        kw = {}
        if bias is not None:
            kw["bias"] = self._a(bias)
        if scale is not None:
            kw["scale"] = self._a(scale)
        if accum is not None:
            kw["accum_out"] = accum.ap
        self.op(eng, lambda h: h.activation(o, i, func, **kw),
                [in_, bias, scale], [out, accum])

    def tt(self, out, a, b, op, eng="dve"):
        o, x, y = out.ap, a.ap, b.ap
        self.op(eng, lambda h: h.tensor_tensor(o, x, y, op), [a, b], [out])

    def ts(self, out, a, s1, s2, op0, op1=None, eng="dve", accum=None):
        o, x = out.ap, a.ap
        p1, p2 = self._a(s1), self._a(s2)
        kw = {}
        if accum is not None:
            kw["accum_out"] = accum.ap
        if op1 is None:
            self.op(eng, lambda h: h.tensor_scalar(o, x, p1, None, op0, **kw), [a, s1], [out, accum])
        else:
            self.op(eng, lambda h: h.tensor_scalar(o, x, p1, p2, op0, op1, **kw), [a, s1, s2], [out, accum])

    def stt(self, out, a, scalar, b, op0, op1, accum=None):
        o, x, y = out.ap, a.ap, b.ap
        s = self._a(scalar)
        kw = {}
        if accum is not None:
            kw["accum_out"] = accum.ap
        self.op("dve", lambda h: h.scalar_tensor_tensor(o, x, s, y, op0, op1, **kw),
                [a, scalar, b], [out, accum])

    def cp(self, out, in_, eng="dve"):
        o, i = out.ap, in_.ap
        if eng == "act":
            self.op(eng, lambda h: h.copy(o, i), [in_], [out])
        else:
            self.op(eng, lambda h: h.tensor_copy(o, i), [in_], [out])

    def memset(self, out, val, eng="dve"):
        o = out.ap
        self.op(eng, lambda h: h.memset(o, val), [], [out])

    def red(self, out, in_, op, eng="dve", axis=None):
        o, i = out.ap, in_.ap
        ax = AX.X if axis is None else axis
        self.op(eng, lambda h: h.tensor_reduce(o, i, ax, op), [in_], [out])

    def recip(self, out, in_):
        o, i = out.ap, in_.ap
        self.op("dve", lambda h: h.reciprocal(o, i), [in_], [out])

    def barrier(self):
        for name, es in self.engs.items():
            deps = {n: e.count for n, e in self.engs.items() if n != name and e.count > 0}
            for k, (sem, cnt) in self.chans.items():
                if cnt > 0 and not k.startswith("wslot"):
                    deps[k] = cnt
            self._emit_waits(es, deps)

    def finish(self):
        sp = self.engs["sp"]
        for k, (sem, cnt) in self.chans.items():
            if cnt > 0:
                sp.thunks.append(lambda h=sp.h, sem=sem, cnt=cnt: h.wait_ge(sem, cnt))
        for name, es in self.engs.items():
            if name != "sp" and es.count > 0:
                sp.thunks.append(lambda h=sp.h, sem=es.sem, c=es.count: h.wait_ge(sem, c))
        with self.nc.Block() as block:
            @block.tensor
            def _(e):
                for t in self.engs["pe"].thunks:
                    t()

            @block.scalar
            def _(e):
                for t in self.engs["act"].thunks:
                    t()

            @block.vector
            def _(e):
                for t in self.engs["dve"].thunks:
                    t()

            @block.gpsimd
            def _(e):
                for t in self.engs["pool"].thunks:
                    t()

            @block.sync
            def _(e):
                for t in self.engs["sp"].thunks:
                    t()
        self.stack.close()
        return self.nc


class SubBuf(Buf):
    def __init__(self, arena, name, off, shape):
        self.kb = arena.kb
        self.name = name
        self.space = "sb"
        self.res = Res(name)
        self.arena = arena
        self.off = off
        self.shape = list(shape)
        n = int(np.prod(shape[1:]))
        base = arena.buf.t[0:shape[0], off:off + n]
        if len(shape) == 2:
            self.t = base
        elif len(shape) == 3:
            self.t = base.rearrange("p (a b) -> p a b", a=shape[1])
        elif len(shape) == 4:
            self.t = base.rearrange("p (a b c) -> p a b c", a=shape[1], b=shape[2])
        else:
            raise ValueError(shape)

    def pat(self, offset, pattern):
        F = self.arena.F
        pat = [list(p) for p in pattern]
        assert pat[0][0] in (F, 0), (pat, F)
        return V(self, bass.AP(self.arena.buf.t, self.off + offset, pat))


class Arena:
    def __init__(self, kb, name, ncols, dtype):
        self.kb = kb
        self.name = name
        self.F = ncols
        self.buf = kb.sb(name, [128, ncols], dtype)
        self.top = 0
        self.n = 0

    def reset(self):
        kb = self.kb
        fence = {n: e.count for n, e in kb.engs.items() if e.count > 0}
        for k, (sem, cnt) in kb.chans.items():
            if cnt > 0 and not k.startswith("wslot"):
                fence[k] = cnt
        self.fence = fence
        self.top = 0

    def alloc(self, name, shape):
        n = int(np.prod(shape[1:]))
        n = (n + 3) // 4 * 4
        off = self.top
        assert off + n <= self.F, ("arena overflow", self.name, name, off + n, self.F)
        self.top += n
        sbuf = SubBuf(self, "ar_" + self.name + "_" + name, off, shape)
        sbuf.res.reads = dict(getattr(self, "fence", {}))
        return sbuf


from concourse.bass_utils import run_bass_kernel_spmd

D = 2048
INC = 21008
Z0, XBC0, DT0, CFA0, CFB0, CFG0 = 0, 1024, 3072, 3088, 4112, 5136
SCB0, SCC0, SCV0, SCG0 = 6160, 7184, 8208, 9232
Q0, K0, V0, AG0, MG0 = 10256, 11280, 11536, 11792, 12816
NSLOT = 3
NEG = -30000.0

C_ID, C_TRIP, C_TRIS, C_SAMES, C_SELS, C_LASTP, C_LASTS, C_MASKP, C_ONES, C_END = (
    0, 128, 256, 384, 512, 528, 544, 560, 816, 944)
NMASKS = 17 * 128


def make_consts():
    c = np.zeros((128, C_END), np.float32)
    c[:, C_ID:C_ID + 128] = np.eye(128)
    s = np.arange(128)[:, None]
    l = np.arange(128)[None, :]
    c[:, C_TRIP:C_TRIP + 128] = (s <= l)
    same = (s // 8) == (l // 8)
    c[:, C_TRIS:C_TRIS + 128] = (s <= l) & same
    c[:, C_SAMES:C_SAMES + 128] = same
    c[:, C_SELS:C_SELS + 16] = (s // 8) == np.arange(16)[None, :]
    c[127, C_LASTP] = 1.0
    c[:, C_LASTS:C_LASTS + 16] = (s == (np.arange(16)[None, :] * 8 + 7))
    q = np.arange(128)[:, None]
    k = np.arange(128)[None, :]
    mp = np.zeros((128, 256), np.float32)
    mp[:, 0:128] = np.where(k > q, 0.0, NEG)
    mp[:, 128:256] = np.where(k <= q, 0.0, NEG)
    c[:, C_MASKP:C_MASKP + 256] = mp
    c[:, C_ONES:C_ONES + 128] = 1.0
    ms = np.full((128, NMASKS), NEG, np.float32)
    for qq in range(128):
        sq, i = qq // 8, qq % 8
        ms[qq, sq * 128 + i + 1: sq * 128 + 128] = 0.0
        ms[qq, 16 * 128 + sq * 8: 16 * 128 + sq * 8 + i + 1] = 0.0
    return c, ms


def build(branches=(0, 1, 2, 3)):
    kb = KB()
    nc = kb.nc

    def din(name, shape):
        return kb.dram(name, shape, F32, "ExternalInput").ap()

    def dout(name, shape):
        return kb.dram(name, shape, F32, "ExternalOutput").ap()

    xp = din("xp", [2048, D]); xs = din("xs", [128, D])
    st_ssm = din("st_ssm", [2, 16, 1024, 128]); st_cssm = din("st_cssm", [2, 48, 2048])
    st_ccf = din("st_ccf", [2, 480, 1024]); st_csc = din("st_csc", [2, 32, 1024])
    ck = din("ck", [2, 16, 128, 256]); cv = din("cv", [2, 16, 128, 256])
    norm_w = din("norm_w", [2, D]); w_in = din("w_in", [2, D, INC])
    ssm_conv_w = din("ssm_conv_w", [2, 4, 2048]); ssm_conv_b = din("ssm_conv_b", [2, 2048])
    ssm_dt_bias = din("ssm_dt_bias", [2, 16]); ssm_a_log = din("ssm_a_log", [2, 16]); ssm_d = din("ssm_d", [2, 16])
    ssm_norm_w = din("ssm_norm_w", [2, 1024]); w_out_ssm = din("w_out_ssm", [2, 1024, D])
    cf_conv_w = din("cf_conv_w", [2, 31, 1024]); cf_conv_b = din("cf_conv_b", [2, 1024])
    cf_ln_w = din("cf_ln_w", [2, 1024]); cf_ln_b = din("cf_ln_b", [2, 1024]); w_out_cf = din("w_out_cf", [2, 1024, D])
    sc_conv_w = din("sc_conv_w", [2, 3, 1024]); w_out_sc = din("w_out_sc", [2, 1024, D])
    att_sinks = din("att_sinks", [2, 16]); w_out_att = din("w_out_att", [2, 1024, D])
    w_o = din("w_o", [2, D, D]); final_norm_w = din("final_norm_w", [1, D])
    consts_d = din("consts", [128, C_END]); masks_d = din("masks", [128, NMASKS])
    w_outs = [w_out_ssm, w_out_cf, w_out_sc, w_out_att]

    y_p = dout("y_p", [2048, D]); y_s = dout("y_s", [128, D])
    p_ssm = dout("p_ssm", [2, 1024, 128]); p_cssm = dout("p_cssm", [2, 3, 2048])
    p_ccf = dout("p_ccf", [2, 30, 1024]); p_csc = dout("p_csc", [2, 2, 1024])
    p_k = dout("p_k", [2, 128, 256]); p_v = dout("p_v", [2, 128, 256])
    s_ssm = dout("s_ssm", [2, 16, 1024, 128]); s_cssm = dout("s_cssm", [2, 48, 2048])
    s_ccf = dout("s_ccf", [2, 480, 1024]); s_csc = dout("s_csc", [2, 32, 1024])
    s_k = dout("s_k", [2, 16, 128, 256]); s_v = dout("s_v", [2, 16, 128, 256])
    dbg_d = dout("dbg", [128, 6, 4096]) if DBG_CORES else None

    def dump(i, view, n):
        if dbg_d is not None:
            kb.dma(dbg_d[:, i, 0:n], view, q="pool")

    x = kb.sb("x", [128, 4, D], F32)
    xnT = kb.sb("xnT", [128, 16, 512], BF16)
    hacc = kb.sb("hacc", [128, 16, 512], BF16)
    slots = [kb.sb("wslot%d" % i, [128, 16, 512], BF16) for i in range(NSLOT)]
    cst = kb.sb("cst", [128, C_END], F32)
    identb = kb.sb("identb", [128, 128], BF16)
    r16 = kb.sb("r16", [128, 4, 2, 16], F32)
    pcA = kb.sb("pcA", [128, 16, 10], F32)
    pcB = kb.sb("pcB", [128, 8, 74], F32)
    ss = kb.sb("ss", [128, 8], F32)
    hnat = [kb.sb("hnat%d" % l, [128, 8, 128], F32) for l in range(2)]
    tailA = [kb.sb("tailA%d" % l, [128, 16, 3], F32) for l in range(2)]
    tailB = [kb.sb("tailB%d" % l, [128, 8, 30], F32) for l in range(2)]
    tailC = [kb.sb("tailC%d" % l, [128, 8, 2], F32) for l in range(2)]
    KTprev = [kb.sb("KTprev%d" % l, [128, 2, 128], BF16) for l in range(2)]
    Vprev = [kb.sb("Vprev%d" % l, [128, 256], BF16) for l in range(2)]
    arF = Arena(kb, "F", 10496, F32)
    arB = Arena(kb, "B", 17920, BF16)

    pf = [kb.ps("pf%d" % i, [128, 512], F32) for i in range(4)]
    pw = kb.ps("pw", [128, 1024], F32)
    pb = [kb.ps("pb%d" % i, [128, 1024], BF16) for i in range(2)]
    rot = {"f": 0, "b": 0, "e": 0}

    def bank():
        rot["f"] = (rot["f"] + 1) % 4
        return pf[rot["f"]]

    def bbank():
        rot["b"] = (rot["b"] + 1) % 2
        return pb[rot["b"]]

    def ev():
        rot["e"] ^= 1
        return "act" if rot["e"] else "dve"

    identf = V(cst, cst.t[:, C_ID:C_ID + 128])

    def cview(c0, n, rows=128):
        return V(cst, cst.t[0:rows, c0:c0 + n])

    kb.dma(cst[:, :], consts_d, q="sp")
    kb.cp(identb[:, :], identf)
    for buf in hnat + tailA + tailB + tailC:
        kb.memset(V(buf, buf.t.ap()), 0.0, eng="pool")
    for buf in KTprev + Vprev:
        kb.memset(V(buf, buf.t.ap()), 0.0, eng="pool")
    for i, src in enumerate((ssm_dt_bias, ssm_a_log, ssm_d, att_sinks)):
        kb.dma(V(r16, r16.t[:, i, :, :].rearrange("p l h -> p (l h)")),
               src.rearrange("l h -> (l h)").partition_broadcast(128), q="sp")
    kb.act(r16[:, 1, :, :], r16[:, 1, :, :], AF.Exp)
    kb.ts(r16[:, 1, :, :], r16[:, 1, :, :], -1.0, None, ALU.mult)

    def grpv(buf, R, grp):
        if grp == 1:
            return buf[:, 0:R]
        return V(buf, buf.t[:, 0:R].rearrange("p (s w) -> p s w", s=grp))

    def load_T(dram2d, R, C, dst_fn, stage, grp=1):
        kb.dma(stage[0:R, 0:C], dram2d, q="sp")
        for c in range(C // 128):
            p = bank()
            kb.tr(p[:, 0:R], stage[0:R, c * 128:(c + 1) * 128], V(cst, cst.t[0:R, C_ID:C_ID + R]))
            kb.cp(dst_fn(c), grpv(p, R, grp), eng=ev())

    def store_T(src_fn, R, C, dram2d, stage, tmp, grp=1):
        for c in range(C // 128):
            kb.cp(grpv(tmp, R, grp), src_fn(c), eng="dve")
            p = bank()
            kb.tr(p[0:R, 0:128], tmp[:, 0:R], identf)
            kb.cp(stage[0:R, c * 128:(c + 1) * 128], p[0:R, 0:128], eng="act")
        kb.dma(dram2d, stage[0:R, 0:C], q="sp")

    stg = arF.alloc("stg", [128, 2048])
    for l in range(2):
        kb.dma(stg[l * 5:l * 5 + 4, :], ssm_conv_w[l], q="sp")
        kb.dma(stg[l * 5 + 4:l * 5 + 5, :], ssm_conv_b[l:l + 1, :], q="sp")
    for c in range(16):
        p = bank()
        kb.tr(p[:, 0:10], stg[0:10, c * 128:(c + 1) * 128], V(cst, cst.t[0:10, C_ID:C_ID + 10]))
        kb.cp(pcA[:, c, :], p[:, 0:10], eng=ev())
    stg2 = arF.alloc("stg2", [128, 1024])
    for l in range(2):
        o = l * 37
        kb.dma(stg2[o:o + 31, :], cf_conv_w[l], q="sp")
        kb.dma(stg2[o + 31:o + 32, :], cf_conv_b[l:l + 1, :], q="sp")
        kb.dma(stg2[o + 32:o + 33, :], cf_ln_w[l:l + 1, :], q="sp")
        kb.dma(stg2[o + 33:o + 34, :], cf_ln_b[l:l + 1, :], q="sp")
        kb.dma(stg2[o + 34:o + 37, :], sc_conv_w[l], q="sp")
    for c in range(8):
        p = bank()
        kb.tr(p[:, 0:74], stg2[0:74, c * 128:(c + 1) * 128], V(cst, cst.t[0:74, C_ID:C_ID + 74]))
        kb.cp(pcB[:, c, :], p[:, 0:74], eng=ev())

    def layer_plan(l):
        P = []

        def win(name, c0, n):
            P.append((name, w_in[l].rearrange("(k p) c -> p k c", p=128)[:, :, c0:c0 + n], 16, n))

        def outp(b):
            for blk in range(4):
                win("merge%d_%d" % (b, blk), MG0 + b * 2048 + blk * 512, 512)
                P.append(("wout%d_%d" % (b, blk),
                          w_outs[b][l].rearrange("(k p) c -> p k c", p=128)[:, :, blk * 512:(blk + 1) * 512], 8, 512))
        if 2 in branches:
            for nm, c0 in (("scc", SCC0), ("scv", SCV0), ("scb", SCB0), ("scg", SCG0)):
                for b in range(2):
                    win("%s%d" % (nm, b), c0 + 512 * b, 512)
            outp(2)
        if 1 in branches:
            for b in range(2):
                win("cfb%d" % b, CFB0 + 512 * b, 512)
            for b in range(2):
                win("cfa%d" % b, CFA0 + 512 * b, 512)
            for b in range(2):
                win("cfg%d" % b, CFG0 + 512 * b, 512)
            outp(1)
        if 0 in branches:
            for b in range(4):
                win("xbc%d" % b, XBC0 + 512 * b, 512)
            for b in range(2):
                win("z%d" % b, Z0 + 512 * b, 512)
            win("dt", DT0, 16)
            outp(0)
        if 3 in branches:
            for b in range(2):
                win("ag%d" % b, AG0 + 512 * b, 512)
            for b in range(2):
                win("q%d" % b, Q0 + 512 * b, 512)
            win("kv", K0, 512)
            outp(3)
        for blk in range(4):
            P.append(("wo%d" % blk, w_o[l].rearrange("(k p) c -> p k c", p=128)[:, :, blk * 512:(blk + 1) * 512], 16, 512))
        return P

    chunks = CHUNKS if CHUNKS else [("P", i) for i in range(4)] + [("S", 0)]
    plan = []
    for ch in chunks:
        for l in range(2):
            plan += layer_plan(l)
    wst = {"issued": 0, "next": 0}
    NPL = len(layer_plan(0))
    wscr = kb.dram("wscr", [2 * NPL, 128, 8192], BF16, "Internal").ap() if len(chunks) > 1 else None

    def w_issue(j):
        name, src, kc, n = plan[j]
        slot = slots[j % NSLOT]
        bi = j % (2 * NPL)
        if wscr is not None:
            scr = wscr[bi][:, 0:kc * n].rearrange("p (k c) -> p k c", k=kc)
            if j >= 2 * NPL:
                kb.dma(slot[:, 0:kc, 0:n], scr, q="pool")
                return
            kb.dma(slot[:, 0:kc, 0:n], src, q="pool")
            kb.dma(scr, slot[:, 0:kc, 0:n], q="sp")
            return
        if isinstance(src, tuple):
            _, l, m = src
            base = w_in[l].rearrange("(k p) c -> p k c", p=128)
            for half in range(2):
                for r in range(4):
                    c0 = Q0 + m * 512 + half * 256 + r * 64
                    d0 = r * 128 + half * 64
                    kb.dma(slot[:, :, d0:d0 + 64], base[:, :, c0:c0 + 64], q="pool")
        else:
            kb.dma(slot[:, 0:kc, 0:n], src, q="pool")

    def wget(name, hold=0):
        j = wst["next"]
        assert plan[j][0] == name, (plan[j][0], name)
        while wst["issued"] < min(len(plan), j + NSLOT - hold):
            w_issue(wst["issued"])
            wst["issued"] += 1
        wst["next"] += 1
        return slots[j % NSLOT]

    def proj_fm(slot, kc, col_lo, ncols, rhs, T):
        p = bank()
        for k in range(kc):
            kb.mm(p[0:ncols, 0:T], slot[:, k, col_lo:col_lo + ncols], rhs[:, k, 0:T], start=(k == 0), stop=(k == kc - 1))
        return p

    def proj_tm(slot, kc, col_lo, ncols, j, p=None):
        if p is None:
            p = bank()
        for k in range(kc):
            kb.mm(p[:, 0:ncols], xnT[:, k, j * 128:(j + 1) * 128], slot[:, k, col_lo:col_lo + ncols],
                  start=(k == 0), stop=(k == kc - 1))
        return p

    def rstd_of(src, C, eps, junk):
        kb.act(junk, src, AF.Square, accum=ss[:, 0:1])
        kb.ts(ss[:, 1:2], ss[:, 0:1], 1.0 / C, eps, ALU.mult, ALU.add)
        kb.act(ss[:, 2:3], ss[:, 1:2], AF.Sqrt)
        kb.recip(ss[:, 3:4], ss[:, 2:3])
        return ss[:, 3:4]

    def transpose_to_fm(src_bf, ncol_chunks, dst_fn):
        for c0 in range(0, ncol_chunks, 8):
            n = min(8, ncol_chunks - c0)
            p = bbank()
            for i in range(n):
                kb.tr(p[:, i * 128:(i + 1) * 128], src_bf[:, (c0 + i) * 128:(c0 + i + 1) * 128], identb[:, :])
            kb.cp(dst_fn(c0, n), V(p, p.t[:, 0:n * 128].rearrange("p (c t) -> p c t", c=n)), eng=ev())

    def seqv(buf, ch, nseq, W, lo, L):
        if nseq == 1:
            return buf[:, ch, lo:lo + L]
        return V(buf, buf.t[:, ch, :].rearrange("p (s w) -> p s w", s=nseq)[:, :, lo:lo + L])

    def psv(p, nseq, L):
        if nseq == 1:
            return p[:, 0:L]
        return V(p, p.t[:, 0:nseq * L].rearrange("p (s w) -> p s w", s=nseq))

    def branch_out(l, b, yT, T, first):
        for blk in range(4):
            sg = wget("merge%d_%d" % (b, blk))
            so = wget("wout%d_%d" % (b, blk), hold=1)
            for sub in range(4):
                cb = blk * 4 + sub
                pg = proj_fm(sg, 16, sub * 128, 128, xnT, T)
                po = proj_fm(so, 8, sub * 128, 128, yT, T)
                g = gtmp[rot_g[0] % 2]
                rot_g[0] += 1
                kb.act(g[:, 0:T], pg[:, 0:T], AF.Sigmoid)
                if first:
                    kb.tt(hacc[:, cb, 0:T], g[:, 0:T], po[:, 0:T], ALU.mult)
                else:
                    kb.tt(g[:, 0:T], g[:, 0:T], po[:, 0:T], ALU.mult)
                    kb.tt(hacc[:, cb, 0:T], hacc[:, cb, 0:T], g[:, 0:T], ALU.add, eng="dve")

    rot_g = [0]
    gtmp = [None, None]

    def alloc_gtmp():
        gtmp[0] = arF.alloc("gt0", [128, 512])
        gtmp[1] = arF.alloc("gt1", [128, 512])

    for (kind, ci) in chunks:
        isP = kind == "P"
        T = 512 if isP else 128
        NT = T // 128
        nseq = 1 if isP else 16
        L = T // nseq
        last_chunk = isP and ci == max(c_[1] for c_ in chunks if c_[0] == 'P')
        arF.reset(); arB.reset()
        src = xp[ci * 512:(ci + 1) * 512, :] if isP else xs
        kb.dma(x[:, 0:NT, :], src.rearrange("(j p) d -> p j d", p=128), q="sp")

        for l in range(2):
            arF.reset(); arB.reset()
            rowbuf = arF.alloc("rowbuf", [128, D])
            kb.dma(rowbuf[:, :], norm_w[l].partition_broadcast(128), q="sp")
            xnb = arB.alloc("xnb", [128, D])
            for j in range(NT):
                r = rstd_of(x[:, j, :], D, 1e-6, xnb[:, :])
                kb.stt(xnb[:, :], x[:, j, :], r, rowbuf[:, :], ALU.mult, ALU.mult)
                transpose_to_fm(xnb, 16, lambda c0, n, j=j: xnT[:, c0:c0 + n, j * 128:(j + 1) * 128])
            first = True

            if 2 in branches:
                arF.reset(); arB.reset(); alloc_gtmp()
                Wc = 2 + L
                cbuf = arF.alloc("cbuf", [128, 8, nseq * Wc])
                acc = arF.alloc("acc", [128, 8, T])
                yT = arB.alloc("yT", [128, 8, T])
                if not isP:
                    stgc = arF.alloc("stgc", [128, 1024])
                    load_T(st_csc[l], 32, 1024,
                           lambda c: V(cbuf, cbuf.t[:, c, :].rearrange("p (s w) -> p s w", s=16)[:, :, 0:2]), stgc, grp=16)
                for blk in range(2):
                    s_ = wget("scc%d" % blk)
                    for sub in range(4):
                        ch = blk * 4 + sub
                        p = proj_fm(s_, 16, sub * 128, 128, xnT, T)
                        kb.cp(seqv(cbuf, ch, nseq, Wc, 2, L), psv(p, nseq, L), eng="act")
                        if isP:
                            kb.cp(cbuf[:, ch, 0:2], tailC[l][:, ch, :], eng="act")
                for blk in range(2):
                    s_ = wget("scv%d" % blk)
                    for sub in range(4):
                        ch = blk * 4 + sub
                        p = proj_fm(s_, 16, sub * 128, 128, xnT, T)
                        kb.tt(seqv(cbuf, ch, nseq, Wc, 2, L), seqv(cbuf, ch, nseq, Wc, 2, L), psv(p, nseq, L), ALU.mult)
                        if isP:
                            kb.cp(tailC[l][:, ch, :], cbuf[:, ch, L:L + 2], eng="act")
                        a3 = V(acc, acc.t[:, ch, :].rearrange("p (s w) -> p s w", s=nseq)) if nseq > 1 else acc[:, ch, :]
                        w0 = l * 37 + 34
                        kb.ts(a3, seqv(cbuf, ch, nseq, Wc, 0, L), pcB[:, ch, w0:w0 + 1], None, ALU.mult)
                        for k in (1, 2):
                            kb.stt(a3, seqv(cbuf, ch, nseq, Wc, k, L), pcB[:, ch, w0 + k:w0 + k + 1], a3, ALU.mult, ALU.add)
                if not isP:
                    stgo = arF.alloc("stgo", [128, 1024]); tmpo = arF.alloc("tmpo", [128, 128])
                    store_T(lambda c: V(cbuf, cbuf.t[:, c, :].rearrange("p (s w) -> p s w", s=16)[:, :, 8:10]),
                            32, 1024, s_csc[l], stgo, tmpo, grp=16)
                for blk in range(2):
                    s_ = wget("scb%d" % blk)
                    for sub in range(4):
                        ch = blk * 4 + sub
                        p = proj_fm(s_, 16, sub * 128, 128, xnT, T)
                        kb.tt(acc[:, ch, :], acc[:, ch, :], p[:, 0:T], ALU.mult)
                for blk in range(2):
                    s_ = wget("scg%d" % blk)
                    for sub in range(4):
                        ch = blk * 4 + sub
                        p = proj_fm(s_, 16, sub * 128, 128, xnT, T)
                        g = gtmp[rot_g[0] % 2]; rot_g[0] += 1
                        kb.act(g[:, 0:T], p[:, 0:T], AF.Silu)
                        kb.tt(yT[:, ch, :], acc[:, ch, :], g[:, 0:T], ALU.mult)
                if isP and ci == 0 and l == 0:
                    dump(0, V(acc, acc.t[:, :, :].rearrange("p a b -> p (a b)")), 4096)
                    dump(1, V(yT, yT.t[:, :, :].rearrange("p a b -> p (a b)")), 4096)
                branch_out(l, 2, yT, T, first)
                if isP and ci == 0 and l == 0:
                    dump(2, V(hacc, hacc.t[:, 0:8, :].rearrange("p a b -> p (a b)")), 4096)
                first = False

            if 1 in branches:
                arF.reset(); arB.reset(); alloc_gtmp()
                Wb = 30 + L
                ubuf = arF.alloc("ubuf", [128, 8, nseq * Wb])
                acc = arF.alloc("acc", [128, 8, T])
                yT = arB.alloc("yT", [128, 8, T])
                if isP:
                    ubf = arB.alloc("ubf", [128, Wb])
                    dg = [arB.alloc("dg%d" % i, [128, 128]) for i in range(4)]
                if not isP:
                    stgc = arF.alloc("stgc", [128, 1024])
                    for rb in range(4):
                        load_T(st_ccf[l][rb * 120:(rb + 1) * 120, :], 120, 1024,
                               lambda c, rb=rb: V(ubuf, ubuf.t[:, c, :].rearrange("p (s w) -> p s w", s=16)[:, rb * 4:(rb + 1) * 4, 0:30]),
                               stgc, grp=4)
                for blk in range(2):
                    s_ = wget("cfb%d" % blk)
                    for sub in range(4):
                        ch = blk * 4 + sub
                        p = proj_fm(s_, 16, sub * 128, 128, xnT, T)
                        kb.act(seqv(ubuf, ch, nseq, Wb, 30, L), psv(p, nseq, L), AF.Sigmoid)
                        if isP:
                            kb.cp(ubuf[:, ch, 0:30], tailB[l][:, ch, :], eng="act")
                w0 = l * 37
                for blk in range(2):
                    s_ = wget("cfa%d" % blk)
                    for sub in range(4):
                        ch = blk * 4 + sub
                        p = proj_fm(s_, 16, sub * 128, 128, xnT, T)
                        kb.tt(seqv(ubuf, ch, nseq, Wb, 30, L), seqv(ubuf, ch, nseq, Wb, 30, L), psv(p, nseq, L), ALU.mult)
                        if isP:
                            kb.cp(tailB[l][:, ch, :], ubuf[:, ch, L:L + 30], eng="act")
                        if isP:
                            kb.cp(ubf[:, 0:Wb], ubuf[:, ch, 0:Wb], eng="act")
                            pc = bank()
                            for k in range(31):
                                dgk = dg[k % 4]
                                kb.ts(dgk[:, :], identb[:, :], pcB[:, ch, w0 + k:w0 + k + 1], None, ALU.mult)
                                kb.mm(pc[:, 0:T], dgk[:, :], ubf[:, k:k + L], start=(k == 0), stop=(k == 30), sig=True)
                            kb.act(acc[:, ch, :], pc[:, 0:T], AF.Identity, bias=pcB[:, ch, w0 + 31:w0 + 32])
                        else:
                            a3 = V(acc, acc.t[:, ch, :].rearrange("p (s w) -> p s w", s=nseq))
                            kb.ts(a3, seqv(ubuf, ch, nseq, Wb, 0, L), pcB[:, ch, w0:w0 + 1], pcB[:, ch, w0 + 31:w0 + 32], ALU.mult, ALU.add)
                            for k in range(1, 31):
                                kb.stt(a3, seqv(ubuf, ch, nseq, Wb, k, L), pcB[:, ch, w0 + k:w0 + k + 1], a3, ALU.mult, ALU.add)
                if not isP:
                    stgo = arF.alloc("stgo", [128, 1024]); tmpo = arF.alloc("tmpo", [128, 128])
                    for rb in range(4):
                        store_T(lambda c, rb=rb: V(ubuf, ubuf.t[:, c, :].rearrange("p (s w) -> p s w", s=16)[:, rb * 4:(rb + 1) * 4, 8:38]),
                                120, 1024, s_ccf[l][rb * 120:(rb + 1) * 120, :], stgo, tmpo, grp=4)
                sq = gtmp[0]; mu = arF.alloc("mu", [128, T]); rs = arF.alloc("rs", [128, T])
                onesN = cview(C_ONES, 128)
                pm = bank()
                for ch in range(8):
                    kb.mm(pm[:, 0:T], onesN, acc[:, ch, :], start=(ch == 0), stop=(ch == 7))
                kb.ts(mu[:, :], pm[:, 0:T], 1.0 / 1024, None, ALU.mult)
                pv = bank()
                for ch in range(8):
                    kb.act(sq[:, 0:T], acc[:, ch, :], AF.Square)
                    kb.mm(pv[:, 0:T], onesN, sq[:, 0:T], start=(ch == 0), stop=(ch == 7), sig=True)
                kb.tt(sq[:, 0:T], mu[:, :], mu[:, :], ALU.mult)
                kb.stt(rs[:, :], pv[:, 0:T], 1.0 / 1024, sq[:, 0:T], ALU.mult, ALU.subtract)
                kb.ts(rs[:, :], rs[:, :], 1e-5, None, ALU.add)
                kb.act(rs[:, :], rs[:, :], AF.Sqrt)
                kb.recip(rs[:, :], rs[:, :])
                for ch in range(8):
                    kb.tt(acc[:, ch, :], acc[:, ch, :], mu[:, :], ALU.subtract)
                    kb.tt(acc[:, ch, :], acc[:, ch, :], rs[:, :], ALU.mult)
                    kb.act(acc[:, ch, :], acc[:, ch, :], AF.Silu, scale=pcB[:, ch, w0 + 32:w0 + 33], bias=pcB[:, ch, w0 + 33:w0 + 34])
                for blk in range(2):
                    s_ = wget("cfg%d" % blk)
                    for sub in range(4):
                        ch = blk * 4 + sub
                        p = proj_fm(s_, 16, sub * 128, 128, xnT, T)
                        g = gtmp[rot_g[0] % 2]; rot_g[0] += 1
                        kb.act(g[:, 0:T], p[:, 0:T], AF.Silu)
                        kb.tt(yT[:, ch, :], acc[:, ch, :], g[:, 0:T], ALU.mult)
                branch_out(l, 1, yT, T, first)
                first = False


            if 0 in branches:
                arF.reset(); arB.reset()
                Wa = 3 + L
                xbcT = arB.alloc("xbcT", [128, 16, T])
                zs = arB.alloc("zs", [128, NT, 1024])
                x_tok = arB.alloc("x_tok", [128, 1024]); xdt = arB.alloc("xdt", [128, 1024]); xdec = arB.alloc("xdec", [128, 1024])
                B_tok = arB.alloc("B_tok", [128, 512])
                if isP:
                    MTb = arB.alloc("MTb", [128, 2048])
                    hTbuf, hToff = MTb, 1024
                else:
                    Bm = arB.alloc("Bm", [128, 512])
                    MT = arB.alloc("MT", [128, 4, 128]); hT_bf = arB.alloc("hT_bf", [128, 1024])
                    hTbuf, hToff = hT_bf, 0
                cbufA = [arF.alloc("cbA%d" % i, [128, nseq * Wa]) for i in range(2)]
                accA = arF.alloc("accA", [128, T])
                dtt = arF.alloc("dtt", [128, NT, 16]); dat = arF.alloc("dat", [128, NT, 16])
                sp_ = arF.alloc("sp_", [128, 32]); sc = arF.alloc("sc", [128, 64])
                rowb = arF.alloc("rowb", [128, 1024])
                kb.dma(rowb[:, :], ssm_norm_w[l].partition_broadcast(128), q="sp")
                tmpo = arF.alloc("tmpo", [128, 128])
                if not isP:
                    stgA = arF.alloc("stgA", [128, 1024]); hist = arF.alloc("hist", [128, 16, 48])
                    for half in range(2):
                        load_T(st_cssm[l][:, half * 1024:(half + 1) * 1024], 48, 1024,
                               lambda c, half=half: hist[:, half * 8 + c, :], stgA)
                w0 = l * 5
                for blk in range(4):
                    s_ = wget("xbc%d" % blk)
                    for sub in range(4):
                        ch = blk * 4 + sub
                        p = proj_fm(s_, 16, sub * 128, 128, xnT, T)
                        cb = cbufA[ch % 2]

                        def cbv(lo, n, cb=cb):
                            if isP:
                                return cb[:, lo:lo + n]
                            return V(cb, cb.t[:, :].rearrange("p (s w) -> p s w", s=16)[:, :, lo:lo + n])
                        kb.cp(cbv(3, L), psv(p, nseq, L), eng="act")
                        if isP:
                            kb.cp(cb[:, 0:3], tailA[l][:, ch, :], eng="act")
                            kb.cp(tailA[l][:, ch, :], cb[:, L:L + 3], eng="act")
                        else:
                            hv = V(hist, hist.t[:, ch, :].rearrange("p (s w) -> p s w", s=16))
                            kb.cp(cbv(0, 3), hv, eng="act")
                            kb.cp(hv, cbv(8, 3), eng="act")
                        a3 = accA[:, 0:T] if isP else V(accA, accA.t[:, 0:T].rearrange("p (s w) -> p s w", s=16))
                        kb.ts(a3, cbv(0, L), pcA[:, ch, w0:w0 + 1], None, ALU.mult)
                        for k in range(1, 4):
                            kb.stt(a3, cbv(k, L), pcA[:, ch, w0 + k:w0 + k + 1], a3, ALU.mult, ALU.add)
                        kb.act(xbcT[:, ch, :], accA[:, 0:T], AF.Silu, bias=pcA[:, ch, w0 + 4:w0 + 5])
                if not isP:
                    for half in range(2):
                        store_T(lambda c, half=half: hist[:, half * 8 + c, :], 48, 1024,
                                s_cssm[l][:, half * 1024:(half + 1) * 1024], stgA, tmpo)
                for blk in range(2):
                    s_ = wget("z%d" % blk)
                    for j in range(NT):
                        p = proj_tm(s_, 16, 0, 512, j)
                        kb.act(zs[:, j, blk * 512:(blk + 1) * 512], p[:, 0:512], AF.Silu)
                s_ = wget("dt")
                one_col = cview(C_ONES, 1)
                for j in range(NT):
                    p = proj_tm(s_, 16, 0, 16, j)
                    kb.tt(sp_[:, 0:16], p[:, 0:16], r16[:, 0, l, :], ALU.add)
                    kb.act(sp_[:, 16:32], sp_[:, 0:16], AF.Abs)
                    kb.act(sp_[:, 16:32], sp_[:, 16:32], AF.Exp, scale=-1.0)
                    kb.act(sp_[:, 16:32], sp_[:, 16:32], AF.Ln, bias=one_col)
                    kb.ts(sp_[:, 0:16], sp_[:, 0:16], 0.0, None, ALU.max)
                    kb.tt(dtt[:, j, :], sp_[:, 0:16], sp_[:, 16:32], ALU.add)
                    kb.tt(dat[:, j, :], dtt[:, j, :], r16[:, 1, l, :], ALU.mult)
                NG = 16 if isP else 4
                rhs_cs = arF.alloc("rhs_cs", [128, NG, 128]); cbtm = arF.alloc("cbtm", [128, NG // 4, 128])
                dE = arF.alloc("dE", [128, NG, 128]); pyo_sb = arF.alloc("pyo_sb", [128, 8, 128])
                t1 = arF.alloc("t1", [128, 1024]); ssrep = arF.alloc("ssrep", [128, 128])
                cdecT = arF.alloc("cdecT", [128, 8, 16])
                if not isP:
                    h0nat = arF.alloc("h0nat", [128, 8, 128]); hnew = arF.alloc("hnew", [128, 8, 128])
                tri = cview(C_TRIP if isP else C_TRIS, 128)
                same = cview(C_ONES if isP else C_SAMES, 128)
                ones_m = cview(C_ONES, 128)
                lastsel = cview(C_LASTP, 1) if isP else cview(C_LASTS, 16)
                lrot = [0]

                def lbank():
                    lrot[0] ^= 1
                    return pf[2 + lrot[0]]
                FB = arB.F
                FF = arF.F
                for c in range(NT):
                    t0 = c * 128
                    pbk = bbank()
                    for i in range(8):
                        kb.tr(pbk[:, i * 128:(i + 1) * 128], xbcT[:, i, t0:t0 + 128], identb[:, :])
                    kb.cp(x_tok[:, :], pbk[:, 0:1024], eng="act")
                    pbk = bbank()
                    for i in range(4):
                        kb.tr(pbk[:, i * 128:(i + 1) * 128], xbcT[:, 8 + i, t0:t0 + 128], identb[:, :])
                    kb.cp(B_tok[:, :], pbk[:, 0:512], eng="dve")
                    p = lbank()
                    kb.mm(p[:, 0:16], tri, dat[:, c, :])
                    kb.cp(sc[:, 0:16], p[:, 0:16], eng="dve")
                    p = lbank()
                    kb.mm(p[:, 0:16], same, dat[:, c, :])
                    kb.cp(sc[:, 48:64], p[:, 0:16], eng="dve")
                    kb.tt(sc[:, 16:32], sc[:, 48:64], sc[:, 0:16], ALU.subtract)
                    kb.act(sc[:, 16:32], sc[:, 16:32], AF.Exp)
                    kb.act(sc[:, 32:48], sc[:, 0:16], AF.Exp)
                    x3 = V(x_tok, x_tok.t[:, :].rearrange("p (h d) -> p h d", h=16))
                    xdt3 = V(xdt, xdt.t[:, :].rearrange("p (h d) -> p h d", h=16))
                    xdec3 = V(xdec, xdec.t[:, :].rearrange("p (h d) -> p h d", h=16))
                    kb.tt(xdt3, x3, dtt.pat(c * 16, [[FF, 128], [1, 16], [0, 64]]), ALU.mult)
                    kb.tt(xdec3, xdt3, sc.pat(16, [[FF, 128], [1, 16], [0, 64]]), ALU.mult)
                    p = lbank()
                    for c8 in range(8):
                        kb.cp(V(ssrep, ssrep.t[:, :].rearrange("p (h d) -> p h d", h=2)),
                              sc.pat(48 + 2 * c8, [[FF, 128], [1, 2], [0, 64]]), eng="dve")
                        kb.mm(p[:, c8 * 16:c8 * 16 + nseq], ssrep[:, :], lastsel)
                    kb.act(cdecT[:, :, 0:nseq], V(p, p.t[:, 0:128].rearrange("p (a b) -> p a b", a=8)[:, :, 0:nseq]), AF.Exp)
                    if isP:
                        tri_off = C_TRIP
                        kb.tt(rhs_cs[:, :, :], dat.pat(c * 16, [[FF, 128], [1, 16], [0, 128]]),
                              V(cst, bass.AP(cst.t, tri_off, [[C_END, 128], [0, 16], [1, 128]])), ALU.mult)
                        for g in range(4):
                            kb.mm(pf[g][:, 0:512], ones_m,
                                  V(rhs_cs, rhs_cs.t[:, 4 * g:4 * g + 4, :].rearrange("p a b -> p (a b)")))
                        for h in range(16):
                            kb.ts(dE[:, h, :], pf[h // 4][:, (h % 4) * 128:(h % 4 + 1) * 128], sc[:, h:h + 1], 0.0,
                                  ALU.subtract, ALU.min)
                        kb.act(dE[:, :, :], dE[:, :, :], AF.Exp)
                        for g in range(4):
                            kb.mm(pw[:, g * 128:(g + 1) * 128], xbcT[:, 8 + g, t0:t0 + 128], xbcT[:, 12 + g, t0:t0 + 128])
                        kb.tt(cbtm[:, :, :], V(pw, pw.t[:, 0:512].rearrange("p (a b) -> p a b", a=4)),
                              V(cst, bass.AP(cst.t, tri_off, [[C_END, 128], [0, 4], [1, 128]])), ALU.mult)
                        kb.tt(V(MTb, MTb.t[:, :].rearrange("p (g h l) -> p g h l", g=4, h=4)),
                              V(dE, dE.t[:, :, :].rearrange("p (g h) l -> p g h l", g=4)),
                              cbtm.pat(0, [[FF, 128], [128, 4], [0, 4], [1, 128]]), ALU.mult)
                        for h in range(16):
                            kb.mm(pw[:, h * 64:(h + 1) * 64], MTb[:, h * 128:(h + 1) * 128], xdt[:, h * 64:(h + 1) * 64])
                    for g in range(0 if isP else 4):
                        kb.tt(rhs_cs[:, :, :], dat.pat(c * 16 + 4 * g, [[FF, 128], [1, 4], [0, 128]]),
                              V(cst, bass.AP(cst.t, (C_TRIP if isP else C_TRIS), [[C_END, 128], [0, 4], [1, 128]])), ALU.mult)
                        pr = lbank()
                        kb.mm(pr[:, 0:512], ones_m, V(rhs_cs, rhs_cs.t[:, :, :].rearrange("p a b -> p (a b)")))
                        for hl in range(4):
                            h = 4 * g + hl
                            kb.ts(dE[:, hl, :], pr[:, hl * 128:(hl + 1) * 128], sc[:, h:h + 1], 0.0, ALU.subtract, ALU.min)
                        kb.act(dE[:, :, :], dE[:, :, :], AF.Exp)
                        pc = lbank()
                        kb.mm(pc[:, 0:128], xbcT[:, 8 + g, t0:t0 + 128], xbcT[:, 12 + g, t0:t0 + 128])
                        kb.tt(cbtm[:, 0, :], pc[:, 0:128], tri, ALU.mult)
                        kb.tt(MT[:, :, :], dE[:, :, :], cbtm.pat(0, [[FF, 128], [0, 4], [1, 128]]), ALU.mult)
                        for hl in range(4):
                            h = 4 * g + hl
                            kb.mm(pw[:, h * 64:(h + 1) * 64], MT[:, hl, :], xdt[:, h * 64:(h + 1) * 64])
                    pyo = [pf[0], pf[1]]
                    for s in range(nseq):
                        if isP:
                            h0 = hnat[l]
                        else:
                            h0 = h0nat
                            kb.dma(h0nat[:, :, :], st_ssm[l, s].rearrange("(c q) n -> q c n", q=128), q="sp")
                        for half in range(2):
                            pq = lbank()
                            for i in range(4):
                                kb.tr(pq[:, i * 128:(i + 1) * 128], h0[:, half * 4 + i, :], identf)
                            kb.cp(hTbuf[:, hToff + half * 512:hToff + (half + 1) * 512], pq[:, 0:512], eng=ev())
                        for c8 in range(8):
                            g = c8 // 2
                            Lt = 128 if isP else 8
                            kb.mm(pyo[c8 // 4][:, (c8 % 4) * 128 + s * Lt:(c8 % 4) * 128 + (s + 1) * Lt],
                                  hTbuf[:, hToff + c8 * 128:hToff + (c8 + 1) * 128], xbcT[:, 12 + g, t0 + s * Lt:t0 + (s + 1) * Lt])
                        if not isP:
                            kb.ts(Bm[:, :], B_tok[:, :], cview(C_SELS + s, 1), None, ALU.mult)
                        Bsrc = B_tok if isP else Bm
                        for c8 in range(8):
                            g = c8 // 2
                            pst = lbank()
                            kb.mm(pst[:, 0:128], xdec[:, c8 * 128:(c8 + 1) * 128], Bsrc[:, g * 128:(g + 1) * 128])
                            dst = hnat[l][:, c8, :] if isP else hnew[:, c8, :]
                            kb.stt(dst, h0[:, c8, :], cdecT[:, c8, s:s + 1], pst[:, 0:128], ALU.mult, ALU.add)
                        if not isP:
                            kb.dma(s_ssm[l, s].rearrange("(c q) n -> q c n", q=128), hnew[:, :, :], q="sp")
                    for half in range(2):
                        kb.cp(pyo_sb[:, half * 4:(half + 1) * 4, :],
                              V(pyo[half], pyo[half].t[:, 0:512].rearrange("p (a b) -> p a b", a=4)), eng=ev())
                    for half in range(2):
                        pq = pyo[half]
                        for i in range(4):
                            kb.tr(pq[:, i * 128:(i + 1) * 128], pyo_sb[:, half * 4 + i, :], identf)
                        kb.tt(V(t1, t1.t[:, half * 512:(half + 1) * 512].rearrange("p (h d) -> p h d", h=8)),
                              V(pq, pq.t[:, 0:512].rearrange("p (h d) -> p h d", h=8)),
                              sc.pat(32 + half * 8, [[FF, 128], [1, 8], [0, 64]]), ALU.mult)
                    kb.tt(t1[:, :], t1[:, :], pw[:, 0:1024], ALU.add)
                    t2 = V(pyo_sb, pyo_sb.t[:, :, :].rearrange("p a b -> p (a b)"))
                    kb.tt(V(pyo_sb, pyo_sb.t[:, :, :].rearrange("p a (h d) -> p (a h) d", h=2)), x3,
                          V(r16, bass.AP(r16.t, (2 * 2 + l) * 16, [[128, 128], [1, 16], [0, 64]])), ALU.mult)
                    kb.tt(t1[:, :], t1[:, :], t2, ALU.add)
                    kb.tt(t1[:, :], t1[:, :], zs[:, c, :], ALU.mult)
                    r = rstd_of(t1[:, :], 1024, 1e-5, xdt[:, :])
                    kb.stt(xdt[:, :], t1[:, :], r, rowb[:, :], ALU.mult, ALU.mult)
                    transpose_to_fm(xdt, 8, lambda c0, n, t0=t0: xbcT[:, c0:c0 + n, t0:t0 + 128])
                arF.reset(); alloc_gtmp()
                branch_out(l, 0, xbcT, T, first)
                first = False

            if 3 in branches:
                arF.reset(); arB.reset(); alloc_gtmp()
                NK = 2 if isP else 17
                gsil = arB.alloc("gsil", [128, NT, 1024])
                QT = arB.alloc("QT", [128, 8, T])
                KT = arB.alloc("KT", [128, 2, (128 + T) if isP else 17 * 128])
                vtok = arB.alloc("vtok", [128, (1 + NT) if isP else 17, 256])
                Ogb = arB.alloc("Ogb", [128, 1024])
                GH = 4 if isP else 1
                NSET = 2 if isP else 1
                Pbfs = [arB.alloc("Pbf%d" % i, [128, GH * NK * 128]) for i in range(NSET)]
                PTs = [arB.alloc("PT%d" % i, [128, GH * NK, 128]) for i in range(NSET)]
                Pbf, PT = Pbfs[0], PTs[0]
                yT = QT if isP else arB.alloc("yT", [128, 8, T])
                kvf = arF.alloc("kvf", [128, 512])
                Ssbs = [arF.alloc("Ssb%d" % i, [128, GH * NK * 128]) for i in range(NSET)]
                Ssb = Ssbs[0]
                sm = arF.alloc("sm", [128, 8])
                sm4s = [arF.alloc("sm4%d" % i, [128, 24]) for i in range(NSET)]
                Otmps = [arF.alloc("Otmp%d" % i, [128, 256]) for i in range(NSET)]
                koff = 128 if isP else 16 * 128
                voff = 1 if isP else 16
                if not isP:
                    maskS = arF.alloc("maskS", [128, NMASKS])
                    if "m" not in DFLAGS:
                        kb.dma(maskS[:, :], masks_d, q="sp")
                    ckt = [arF.alloc("ckt%d" % i, [128, 256]) for i in range(2)]
                    for s in range(16 if "c" not in DFLAGS else 0):
                        kb.dma(ckt[s % 2][:, :], ck[l, s], q="sp")
                        kb.dma(vtok[:, s, :], cv[l, s], q="pool")
                        for c in range(2):
                            p = bank()
                            kb.tr(p[:, 0:128], ckt[s % 2][:, c * 128:(c + 1) * 128], identf)
                            kb.cp(KT[:, c, s * 128:(s + 1) * 128], p[:, 0:128], eng=ev())
                    if "d" not in DFLAGS:
                        kb.dma(s_k[l][:, 0:120, :], ck[l][:, 8:128, :], q="sp")
                        kb.dma(s_v[l][:, 0:120, :], cv[l][:, 8:128, :], q="sp")
                elif "p" not in DFLAGS:
                    for c in range(2):
                        kb.cp(KT[:, c, 0:128], KTprev[l][:, c, :], eng="pool")
                    kb.cp(vtok[:, 0, :], Vprev[l][:, :], eng="pool")
                for blk in range(2):
                    s_ = wget("ag%d" % blk)
                    for j in range(NT if "1" not in DFLAGS else 0):
                        p = proj_tm(s_, 16, 0, 512, j)
                        kb.act(gsil[:, j, blk * 512:(blk + 1) * 512], p[:, 0:512], AF.Silu)
                qtok = Pbf
                for m_ in range(2):
                    s_ = wget("q%d" % m_)
                    for j in range(NT if "2" not in DFLAGS else 0):
                        p = proj_tm(s_, 16, 0, 512, j)
                        kb.cp(V(qtok, qtok.t[:, 0:512].rearrange("p (r hf d) -> p r hf d", r=4, hf=2)),
                              V(p, p.t[:, 0:512].rearrange("p (hf r d) -> p r hf d", hf=2, r=4)), eng=ev())
                        transpose_to_fm(qtok, 4, lambda c0, n, j=j, m_=m_: QT[:, 4 * m_ + c0:4 * m_ + c0 + n, j * 128:(j + 1) * 128])
                s_ = wget("kv")
                for j in range(NT if "3" not in DFLAGS else 0):
                    p = proj_tm(s_, 16, 0, 512, j)
                    kb.cp(vtok[:, voff + j, :], p[:, 256:512], eng="act")
                    if (last_chunk and j == NT - 1) or not isP:
                        kb.cp(kvf[:, :], p[:, 0:512], eng="dve")
                for c in range(2 if "4" not in DFLAGS else 0):
                    p = proj_fm(s_, 16, c * 128, 128, xnT, T)
                    kb.cp(KT[:, c, koff:koff + T], p[:, 0:T], eng=ev())
                if last_chunk and "k" not in DFLAGS:
                    kb.dma(p_k[l], kvf[:, 0:256], q="sp")
                    kb.dma(p_v[l], kvf[:, 256:512], q="sp")
                if not isP and "o" not in DFLAGS:
                    for s in range(16):
                        kb.dma(s_k[l][s, 120:128, :], kvf[s * 8:(s + 1) * 8, 0:256], q="sp")
                        kb.dma(s_v[l][s, 120:128, :], kvf[s * 8:(s + 1) * 8, 256:512], q="sp")
                if "a" in DFLAGS:
                    kb.memset(V(yT, yT.t[:, :, :]), 0.0)
                FFd = arF.F

                def att_cfg(j):
                    if ci == 0 and j == 0:
                        return 128, [1], cview(C_MASKP + 128, 128)
                    return j * 128, [j, j + 1], cview(C_MASKP, 256)

                def st_scores(it, bs):
                    j, g = it
                    k0, vidx, mask = att_cfg(j)
                    ncol = len(vidx) * 128
                    m_, half = g // 2, g % 2
                    base = 64 * half
                    for r in range(4):
                        p = bank()
                        kb.mm(p[:, 0:ncol], QT[base:base + 64, 4 * m_ + r, j * 128:(j + 1) * 128],
                              KT[base:base + 64, m_, k0:k0 + ncol])
                        kb.stt(Ssbs[bs][:, r * ncol:(r + 1) * ncol], p[:, 0:ncol], 0.125, mask, ALU.mult, ALU.add)

                def st_stats(it, bs):
                    j, g = it
                    k0, vidx, mask = att_cfg(j)
                    ncol = len(vidx) * 128
                    sm4 = sm4s[bs]
                    Ss3 = V(Ssbs[bs], Ssbs[bs].t[:, 0:4 * ncol].rearrange("p (r c) -> p r c", r=4))
                    sk4 = r16[:, 3, l, 4 * g:4 * g + 4]
                    kb.red(sm4[:, 0:4], Ss3, ALU.max)
                    kb.tt(sm4[:, 0:4], sm4[:, 0:4], sk4, ALU.max)
                    kb.ts(sm4[:, 4:8], sm4[:, 0:4], -1.0, None, ALU.mult)
                    kb.tt(sm4[:, 12:16], sk4, sm4[:, 4:8], ALU.add)

                def st_exp(it, bs):
                    j, g = it
                    k0, vidx, mask = att_cfg(j)
                    ncol = len(vidx) * 128
                    sm4 = sm4s[bs]
                    for r in range(4):
                        kb.act(Pbfs[bs][:, r * ncol:(r + 1) * ncol], Ssbs[bs][:, r * ncol:(r + 1) * ncol], AF.Exp,
                               bias=sm4[:, 4 + r:5 + r], accum=sm4[:, 8 + r:9 + r])
                    kb.act(sm4[:, 12:16], sm4[:, 12:16], AF.Exp)
                    kb.tt(sm4[:, 16:20], sm4[:, 8:12], sm4[:, 12:16], ALU.add)
                    kb.recip(sm4[:, 20:24], sm4[:, 16:20])

                def st_tr(it, bs):
                    j, g = it
                    k0, vidx, mask = att_cfg(j)
                    nk = len(vidx)
                    pbk = bbank()
                    for q_ in range(4 * nk):
                        kb.tr(pbk[:, q_ * 128:(q_ + 1) * 128], Pbfs[bs][:, q_ * 128:(q_ + 1) * 128], identb[:, :])
                    kb.cp(PTs[bs][:, 0:4 * nk, :], V(pbk, pbk.t[:, 0:4 * nk * 128].rearrange("p (c t) -> p c t", c=4 * nk)), eng=ev())

                def st_pv(it, bs):
                    j, g = it
                    k0, vidx, mask = att_cfg(j)
                    nk = len(vidx)
                    po = bank()
                    for r in range(4):
                        for i, vi in enumerate(vidx):
                            kb.mm(po[:, r * 64:(r + 1) * 64], PTs[bs][:, r * nk + i, :], vtok[:, vi, g * 64:(g + 1) * 64],
                                  start=(i == 0), stop=(i == nk - 1))
                    kb.tt(V(Otmps[bs], Otmps[bs].t[:, :].rearrange("p (r d) -> p r d", r=4)),
                          V(po, po.t[:, 0:256].rearrange("p (r d) -> p r d", r=4)),
                          sm4s[bs].pat(20, [[FFd, 128], [1, 4], [0, 64]]), ALU.mult)
                    kb.tt(Ogb[:, g * 256:(g + 1) * 256], Otmps[bs][:, :], gsil[:, j, g * 256:(g + 1) * 256], ALU.mult)
                    if g == 3:
                        transpose_to_fm(Ogb, 8, lambda c0, n, j=j: yT[:, c0:c0 + n, j * 128:(j + 1) * 128])

                if isP and "a" not in DFLAGS:
                    items = [(j, g) for j in range(NT) for g in range(4)]
                    stages = [st_scores, st_stats, st_exp, st_tr, st_pv]
                    for i in range(0, len(items), 2):
                        pair = items[i:i + 2]
                        for s_i in range(len(stages)):
                            for k_, it in enumerate(pair):
                                stages[s_i](it, (i + k_) % 2)
                for j in range(NT if ("a" not in DFLAGS and not isP) else 0):
                    k0, vidx, mask = 0, list(range(17)), maskS[:, :]
                    nk = len(vidx)
                    ncol = nk * 128
                    for h in range(0 if isP else 16):
                        m_, half, r = h // 8, (h % 8) // 4, h % 4
                        qch, base, g = 4 * m_ + r, 64 * half, h // 4
                        for c0 in range(0, ncol, 512):
                            n = min(512, ncol - c0)
                            p = bank()
                            kb.mm(p[:, 0:n], QT[base:base + 64, qch, j * 128:(j + 1) * 128],
                                  KT[base:base + 64, m_, k0 + c0:k0 + c0 + n])
                            kb.stt(Ssb[:, c0:c0 + n], p[:, 0:n], 0.125, V(mask.buf, mask.ap[:, c0:c0 + n]), ALU.mult, ALU.add)
                        sk = r16[:, 3, l, h:h + 1]
                        kb.red(sm[:, 0:1], Ssb[:, 0:ncol], ALU.max)
                        kb.tt(sm[:, 0:1], sm[:, 0:1], sk, ALU.max)
                        kb.ts(sm[:, 1:2], sm[:, 0:1], -1.0, None, ALU.mult)
                        kb.act(Pbf[:, 0:ncol], Ssb[:, 0:ncol], AF.Exp, bias=sm[:, 1:2], accum=sm[:, 2:3])
                        kb.act(sm[:, 3:4], sk, AF.Exp, bias=sm[:, 1:2])
                        kb.tt(sm[:, 4:5], sm[:, 2:3], sm[:, 3:4], ALU.add)
                        kb.recip(sm[:, 5:6], sm[:, 4:5])
                        for kk in range(0, nk, 8):
                            n = min(8, nk - kk)
                            pbk = bbank()
                            for i in range(n):
                                kb.tr(pbk[:, i * 128:(i + 1) * 128], Pbf[:, (kk + i) * 128:(kk + i + 1) * 128], identb[:, :])
                            kb.cp(PT[:, kk:kk + n, :], V(pbk, pbk.t[:, 0:n * 128].rearrange("p (c t) -> p c t", c=n)), eng=ev())
                        po = bank()
                        for i, vi in enumerate(vidx):
                            kb.mm(po[:, 0:64], PT[:, i, :], vtok[:, vi, g * 64:(g + 1) * 64], start=(i == 0), stop=(i == nk - 1))
                        kb.stt(Ogb[:, h * 64:(h + 1) * 64], po[:, 0:64], sm[:, 5:6], gsil[:, j, h * 64:(h + 1) * 64], ALU.mult, ALU.mult)
                    transpose_to_fm(Ogb, 8, lambda c0, n, j=j: yT[:, c0:c0 + n, j * 128:(j + 1) * 128])
                if isP and "p" not in DFLAGS:
                    for c in range(2):
                        kb.cp(KTprev[l][:, c, :], KT[:, c, T:T + 128], eng="pool")
                    kb.cp(Vprev[l][:, :], vtok[:, NT, :], eng="pool")
                branch_out(l, 3, yT, T, first)
                first = False

            arF.reset(); arB.reset()
            if first:
                kb.memset(V(hacc, hacc.t.ap()), 0.0, eng="pool")
            for blk in range(4):
                s_ = wget("wo%d" % blk)
                for j in range(NT):
                    p = bank()
                    for k in range(16):
                        kb.mm(p[:, 0:512], hacc[:, k, j * 128:(j + 1) * 128], s_[:, k, 0:512], start=(k == 0), stop=(k == 15))
                    kb.tt(x[:, j, blk * 512:(blk + 1) * 512], x[:, j, blk * 512:(blk + 1) * 512], p[:, :], ALU.add)
            if isP and ci == 0 and l == 0:
                dump(3, V(x, x.t[:, 0, :]), 2048)
            if last_chunk:
                stgo = arF.alloc("stgo", [128, 1024]); tmpo = arF.alloc("tmpo", [128, 128])
                if 2 in branches:
                    store_T(lambda c: tailC[l][:, c, :], 2, 1024, p_csc[l], stgo, tmpo)
                if 1 in branches:
                    store_T(lambda c: tailB[l][:, c, :], 30, 1024, p_ccf[l], stgo, tmpo)
                if 0 in branches:
                    for half in range(2):
                        store_T(lambda c, half=half: tailA[l][:, half * 8 + c, :], 3, 1024,
                                p_cssm[l][:, half * 1024:(half + 1) * 1024], stgo, tmpo)
                    kb.dma(p_ssm[l].rearrange("(c q) n -> q c n", q=128), hnat[l][:, :, :], q="sp")

        arF.reset(); arB.reset()
        rowbuf = arF.alloc("rowbuf", [128, D])
        kb.dma(rowbuf[:, :], final_norm_w[0].partition_broadcast(128), q="sp")
        yo = [arF.alloc("yo%d" % i, [128, D]) for i in range(2)]
        for j in range(NT):
            r = rstd_of(x[:, j, :], D, 1e-6, yo[j % 2][:, :])
            kb.stt(yo[j % 2][:, :], x[:, j, :], r, rowbuf[:, :], ALU.mult, ALU.mult)
            dst = y_p[ci * 512 + j * 128: ci * 512 + (j + 1) * 128, :] if isP else y_s
            kb.dma(dst, yo[j % 2][:, :], q="sp")

    assert wst["next"] == len(plan), (wst["next"], len(plan))
    nc = kb.finish()
    print("instructions:", kb.n_inst, "channels:", kb.nchan)
    return nc


_NC_CACHE = {}
BRANCHES = (0, 1, 2, 3)
DBG_CORES = 0
DFLAGS = ""
CHUNKS = None


def kernel(**inp):
    f32 = lambda a: np.ascontiguousarray(np.asarray(a, dtype=np.float32))
    if "nc" not in _NC_CACHE:
        _NC_CACHE["nc"] = build(BRANCHES)
    nc = _NC_CACHE["nc"]
    consts, masks = make_consts()
    shared = {k: f32(inp[k]) for k in (
        "norm_w", "w_in", "ssm_conv_w", "ssm_conv_b", "ssm_dt_bias", "ssm_a_log", "ssm_d", "ssm_norm_w",
        "w_out_ssm", "cf_conv_w", "cf_conv_b", "cf_ln_w", "cf_ln_b", "w_out_cf", "sc_conv_w", "w_out_sc",
        "att_sinks", "w_out_att", "w_o")}
    shared["final_norm_w"] = f32(inp["final_norm_w"]).reshape(1, D)
    shared["consts"] = consts
    shared["masks"] = masks
    xp_ = f32(inp["x_prompt"]); xs_ = f32(inp["x_sample"])
    in_maps = []
    for c in range(8):
        sl = slice(16 * c, 16 * c + 16)
        m = dict(shared)
        m["xp"] = xp_[c % 4]
        m["xs"] = xs_[sl].reshape(128, D)
        m["st_ssm"] = f32(inp["state_ssm"][:, sl]).reshape(2, 16, 1024, 128)
        m["st_cssm"] = f32(inp["state_conv_ssm"][:, sl]).reshape(2, 48, 2048)
        m["st_ccf"] = f32(inp["state_conv_cf"][:, sl]).reshape(2, 480, 1024)
        m["st_csc"] = f32(inp["state_conv_sc"][:, sl]).reshape(2, 32, 1024)
        m["ck"] = f32(inp["cache_k"][:, sl]).reshape(2, 16, 128, 256)
        m["cv"] = f32(inp["cache_v"][:, sl]).reshape(2, 16, 128, 256)
        in_maps.append(m)
    if DBG_CORES:
        return run_bass_kernel_spmd(nc, in_maps[:DBG_CORES], core_ids=list(range(DBG_CORES))).results
    res = run_bass_kernel_spmd(nc, in_maps, core_ids=list(range(8)))
    R = res.results
    cat = lambda key, cores, ax: np.concatenate([R[c][key] for c in cores], axis=ax)
    pc = [0, 1, 2, 3]
    ac = list(range(8))
    y_prompt = np.stack([R[c]["y_p"] for c in pc], 0)
    y_sample = cat("y_s", ac, 0).reshape(128, 8, D)
    p_ssm = np.stack([R[c]["p_ssm"] for c in pc], 1).reshape(2, 4, 16, 64, 128)
    p_cssm = np.stack([R[c]["p_cssm"] for c in pc], 1)
    p_ccf = np.stack([R[c]["p_ccf"] for c in pc], 1)
    p_csc = np.stack([R[c]["p_csc"] for c in pc], 1)
    p_k = np.stack([R[c]["p_k"] for c in pc], 1).reshape(2, 4, 128, 4, 64)
    p_v = np.stack([R[c]["p_v"] for c in pc], 1).reshape(2, 4, 128, 4, 64)
    s_ssm = cat("s_ssm", ac, 1).reshape(2, 128, 16, 64, 128)
    s_cssm = cat("s_cssm", ac, 1).reshape(2, 128, 3, 2048)
    s_ccf = cat("s_ccf", ac, 1).reshape(2, 128, 30, 1024)
    s_csc = cat("s_csc", ac, 1).reshape(2, 128, 2, 1024)
    s_k = cat("s_k", ac, 1).reshape(2, 128, 128, 4, 64)
    s_v = cat("s_v", ac, 1).reshape(2, 128, 128, 4, 64)
    return (y_prompt, y_sample, p_ssm, p_cssm, p_ccf, p_csc, p_k, p_v,
            s_ssm, s_cssm, s_ccf, s_csc, s_k, s_v)
```

```python
import contextlib
import numpy as np
import concourse.bass as bass
import concourse.mybir as mybir

F32 = mybir.dt.float32
BF16 = mybir.dt.bfloat16
AF = mybir.ActivationFunctionType
ALU = mybir.AluOpType
AX = mybir.AxisListType


class Res:
    __slots__ = ("name", "last_write", "reads", "chan", "psum")

    def __init__(self, name):
        self.name = name
        self.psum = False
        self.last_write = None
        self.reads = {}
        self.chan = None


class V:
    __slots__ = ("buf", "ap")

    def __init__(self, buf, ap):
        self.buf = buf
        self.ap = ap


class Buf:
    def __init__(self, kb, name, t, space):
        self.kb = kb
        self.name = name
        self.t = t
        self.space = space
        self.res = Res(name)

    def __getitem__(self, idx):
        return V(self, self.t[idx])

    def pat(self, offset, pattern):
        return V(self, bass.AP(self.t, offset, [list(p) for p in pattern]))


class EngState:
    def __init__(self, name, handle, sem):
        self.name = name
        self.h = handle
        self.sem = sem
        self.count = 0
        self.waited = {}
        self.thunks = []


class KB:
    def __init__(self):
        self.nc = bass.Bass("TRN2", target_bir_lowering=False)
        self.stack = contextlib.ExitStack()
        nc = self.nc
        self.sems = {}
        self.engs = {}
        for name, h in (("pe", nc.tensor), ("act", nc.scalar), ("dve", nc.vector),
                        ("pool", nc.gpsimd), ("sp", nc.sync)):
            sem = self.stack.enter_context(nc.semaphore("sem_" + name))
            self.engs[name] = EngState(name, h, sem)
            self.sems[name] = sem
        self.nchan = 0
        self.chans = {}
        self.n_inst = 0

    def sb(self, name, shape, dtype):
        t = self.stack.enter_context(self.nc.sbuf_tensor(name, list(shape), dtype))
        return Buf(self, name, t, "sb")

    def ps(self, name, shape, dtype):
        t = self.stack.enter_context(self.nc.psum_tensor(name, list(shape), dtype))
        b = Buf(self, name, t, "ps")
        b.res.psum = True
        return b

    def dram(self, name, shape, dtype, kind):
        return self.nc.dram_tensor(name, list(shape), dtype, kind=kind)

    def new_chan(self, key):
        if key in self.chans:
            return key
        sem = self.stack.enter_context(self.nc.semaphore("ch_" + key))
        self.nchan += 1
        assert self.nchan < 130, "too many dma channels"
        self.chans[key] = [sem, 0]
        self.sems[key] = sem
        return key

    def _deps(self, reads, writes):
        deps = {}

        def add(k, v):
            if deps.get(k, 0) < v:
                deps[k] = v
        for r in reads:
            if r.last_write is not None:
                add(*r.last_write)
            if getattr(r, "psum", False):
                for k, v in r.reads.items():
                    add(k, v)
        for w in writes:
            if w.last_write is not None:
                add(*w.last_write)
            for k, v in w.reads.items():
                add(k, v)
        return deps

    def _emit_waits(self, es, deps):
        for k, v in deps.items():
            if k == es.name and k in ("pe", "sp"):
                continue
            if es.waited.get(k, 0) >= v:
                continue
            if k in self.engs and v > self.engs[k].count:
                raise RuntimeError("wait on a not-yet-signaled %s op (value %d > %d): signal that matmul" % (k, v, self.engs[k].count))
            es.waited[k] = v
            sem = self.sems[k]
            es.thunks.append(lambda h=es.h, sem=sem, v=v: h.wait_ge(sem, v))

    def op(self, eng, fn, reads, writes, sig=True):
        es = self.engs[eng]
        reads = [r.buf.res for r in reads if isinstance(r, V)]
        writes = [w.buf.res for w in writes if isinstance(w, V)]
        deps = self._deps(reads, writes)
        self._emit_waits(es, deps)
        self.n_inst += 1
        import sys as _sys
        fr = _sys._getframe(2)
        tag = []
        while fr is not None and len(tag) < 5:
            tag.append(fr.f_lineno)
            fr = fr.f_back
        if sig:
            es.count += 1
            val = es.count

            def th(h=es.h, sem=es.sem, tag=tag):
                try:
                    fn(h).then_inc(sem, 1)
                except Exception:
                    print("FAILED OP at lines", tag)
                    raise
        else:
            val = es.count + 1

            def th(h=es.h, tag=tag):
                try:
                    fn(h)
                except Exception:
                    print("FAILED OP at lines", tag)
                    raise
        es.thunks.append(th)
        for r in reads:
            if r.reads.get(eng, 0) < val:
                r.reads[eng] = val
        for w in writes:
            w.last_write = (eng, val)
            w.reads = {}

    def dma(self, out, in_, q="sp", chan_res=None, **kw):
        es = self.engs[q]
        reads = [in_.buf.res] if isinstance(in_, V) else []
        writes = [out.buf.res] if isinstance(out, V) else []
        owner = (writes + reads)
        if owner:
            res = owner[0]
            if writes:
                if res.chan is None:
                    res.chan = self.new_chan(res.name)
                ck = res.chan
            else:
                ck = self.new_chan(res.name + "_rd")
        else:
            ck = "misc"
            if ck not in self.chans:
                self.new_chan(ck)
        deps = self._deps(reads, writes)
        self._emit_waits(es, deps)
        ch = self.chans[ck]
        ch[1] += 16
        val = ch[1]
        sem = ch[0]
        o = out.ap if isinstance(out, V) else out
        i = in_.ap if isinstance(in_, V) else in_
        self.n_inst += 1
        es.thunks.append(lambda h=es.h, o=o, i=i, sem=sem, kw=kw:
                         h.dma_start(out=o, in_=i, **kw).then_inc(sem, 16))
        for r in reads:
            r.reads[ck] = val
        for w in writes:
            w.last_write = (ck, val)
            w.reads = {}

    @staticmethod
    def _a(x):
        return x.ap if isinstance(x, V) else x

    def mm(self, out, lhsT, rhs, start=True, stop=True, sig=None, **kw):
        if sig is None:
            sig = stop
        o, l, r = out.ap, lhsT.ap, rhs.ap
        self.op("pe", lambda h: h.matmul(o, l, r, start=start, stop=stop, **kw),
                [lhsT, rhs], [out], sig=sig)

    def tr(self, out, in_, ident, sig=True):
        o, i, d = out.ap, in_.ap, ident.ap
        self.op("pe", lambda h: h.transpose(o, i, d), [in_, ident], [out], sig=sig)

    def act(self, out, in_, func, bias=None, scale=None, accum=None, eng="act"):
        o, i = out.ap, in_.ap
        kw = {}
        if bias is not None:
            kw["bias"] = self._a(bias)
        if scale is not None:
            kw["scale"] = self._a(scale)
        if accum is not None:
            kw["accum_out"] = accum.ap
        self.op(eng, lambda h: h.activation(o, i, func, **kw),
                [in_, bias, scale], [out, accum])

    def tt(self, out, a, b, op, eng="dve"):
        o, x, y = out.ap, a.ap, b.ap
        self.op(eng, lambda h: h.tensor_tensor(o, x, y, op), [a, b], [out])

    def ts(self, out, a, s1, s2, op0, op1=None, eng="dve", accum=None):
        o, x = out.ap, a.ap
        p1, p2 = self._a(s1), self._a(s2)
        kw = {}
        if accum is not None:
            kw["accum_out"] = accum.ap
        if op1 is None:
            self.op(eng, lambda h: h.tensor_scalar(o, x, p1, None, op0, **kw), [a, s1], [out, accum])
        else:
            self.op(eng, lambda h: h.tensor_scalar(o, x, p1, p2, op0, op1, **kw), [a, s1, s2], [out, accum])

    def stt(self, out, a, scalar, b, op0, op1, accum=None):
        o, x, y = out.ap, a.ap, b.ap
        s = self._a(scalar)
        kw = {}
        if accum is not None:
            kw["accum_out"] = accum.ap
        self.op("dve", lambda h: h.scalar_tensor_tensor(o, x, s, y, op0, op1, **kw),
                [a, scalar, b], [out, accum])

    def cp(self, out, in_, eng="dve"):
        o, i = out.ap, in_.ap
        if eng == "act":
            self.op(eng, lambda h: h.copy(o, i), [in_], [out])
        else:
            self.op(eng, lambda h: h.tensor_copy(o, i), [in_], [out])

    def memset(self, out, val, eng="dve"):
        o = out.ap
        self.op(eng, lambda h: h.memset(o, val), [], [out])

    def red(self, out, in_, op, eng="dve", axis=None):
        o, i = out.ap, in_.ap
        ax = AX.X if axis is None else axis
        self.op(eng, lambda h: h.tensor_reduce(o, i, ax, op), [in_], [out])

    def recip(self, out, in_):
        o, i = out.ap, in_.ap
        self.op("dve", lambda h: h.reciprocal(o, i), [in_], [out])

    def barrier(self):
        for name, es in self.engs.items():
            deps = {n: e.count for n, e in self.engs.items() if n != name and e.count > 0}
            for k, (sem, cnt) in self.chans.items():
                if cnt > 0 and not k.startswith("wslot"):
                    deps[k] = cnt
            self._emit_waits(es, deps)

    def finish(self):
        sp = self.engs["sp"]
        for k, (sem, cnt) in self.chans.items():
            if cnt > 0:
                sp.thunks.append(lambda h=sp.h, sem=sem, cnt=cnt: h.wait_ge(sem, cnt))
        for name, es in self.engs.items():
            if name != "sp" and es.count > 0:
                sp.thunks.append(lambda h=sp.h, sem=es.sem, c=es.count: h.wait_ge(sem, c))
        with self.nc.Block() as block:
            @block.tensor
            def _(e):
                for t in self.engs["pe"].thunks:
                    t()

            @block.scalar
            def _(e):
                for t in self.engs["act"].thunks:
                    t()

            @block.vector
            def _(e):
                for t in self.engs["dve"].thunks:
                    t()

            @block.gpsimd
            def _(e):
                for t in self.engs["pool"].thunks:
                    t()

            @block.sync
            def _(e):
                for t in self.engs["sp"].thunks:
                    t()
        self.stack.close()
        return self.nc


class SubBuf(Buf):
    def __init__(self, arena, name, off, shape):
        self.kb = arena.kb
        self.name = name
        self.space = "sb"
        self.res = Res(name)
        self.arena = arena
        self.off = off
        self.shape = list(shape)
        n = int(np.prod(shape[1:]))
        base = arena.buf.t[0:shape[0], off:off + n]
        if len(shape) == 2:
            self.t = base
        elif len(shape) == 3:
            self.t = base.rearrange("p (a b) -> p a b", a=shape[1])
        elif len(shape) == 4:
            self.t = base.rearrange("p (a b c) -> p a b c", a=shape[1], b=shape[2])
        else:
            raise ValueError(shape)

    def pat(self, offset, pattern):
        F = self.arena.F
        pat = [list(p) for p in pattern]
        assert pat[0][0] in (F, 0), (pat, F)
        return V(self, bass.AP(self.arena.buf.t, self.off + offset, pat))


class Arena:
    def __init__(self, kb, name, ncols, dtype):
        self.kb = kb
        self.name = name
        self.F = ncols
        self.buf = kb.sb(name, [128, ncols], dtype)
        self.top = 0
        self.n = 0

    def reset(self):
        kb = self.kb
        fence = {n: e.count for n, e in kb.engs.items() if e.count > 0}
        for k, (sem, cnt) in kb.chans.items():
            if cnt > 0 and not k.startswith("wslot"):
                fence[k] = cnt
        self.fence = fence
        self.top = 0

    def alloc(self, name, shape):
        n = int(np.prod(shape[1:]))
        n = (n + 3) // 4 * 4
        off = self.top
        assert off + n <= self.F, ("arena overflow", self.name, name, off + n, self.F)
        self.top += n
        sbuf = SubBuf(self, "ar_" + self.name + "_" + name, off, shape)
        sbuf.res.reads = dict(getattr(self, "fence", {}))
        return sbuf


from concourse.bass_utils import run_bass_kernel_spmd

D = 2048
INC = 21008
Z0, XBC0, DT0, CFA0, CFB0, CFG0 = 0, 1024, 3072, 3088, 4112, 5136
SCB0, SCC0, SCV0, SCG0 = 6160, 7184, 8208, 9232
Q0, K0, V0, AG0, MG0 = 10256, 11280, 11536, 11792, 12816
NSLOT = 3
NEG = -30000.0

C_ID, C_TRIP, C_TRIS, C_SAMES, C_SELS, C_LASTP, C_LASTS, C_MASKP, C_ONES, C_END = (
    0, 128, 256, 384, 512, 528, 544, 560, 816, 944)
NMASKS = 17 * 128


def make_consts():
    c = np.zeros((128, C_END), np.float32)
    c[:, C_ID:C_ID + 128] = np.eye(128)
    s = np.arange(128)[:, None]
    l = np.arange(128)[None, :]
    c[:, C_TRIP:C_TRIP + 128] = (s <= l)
    same = (s // 8) == (l // 8)
    c[:, C_TRIS:C_TRIS + 128] = (s <= l) & same
    c[:, C_SAMES:C_SAMES + 128] = same
    c[:, C_SELS:C_SELS + 16] = (s // 8) == np.arange(16)[None, :]
    c[127, C_LASTP] = 1.0
    c[:, C_LASTS:C_LASTS + 16] = (s == (np.arange(16)[None, :] * 8 + 7))
    q = np.arange(128)[:, None]
    k = np.arange(128)[None, :]
    mp = np.zeros((128, 256), np.float32)
    mp[:, 0:128] = np.where(k > q, 0.0, NEG)
    mp[:, 128:256] = np.where(k <= q, 0.0, NEG)
    c[:, C_MASKP:C_MASKP + 256] = mp
    c[:, C_ONES:C_ONES + 128] = 1.0
    ms = np.full((128, NMASKS), NEG, np.float32)
    for qq in range(128):
        sq, i = qq // 8, qq % 8
        ms[qq, sq * 128 + i + 1: sq * 128 + 128] = 0.0
        ms[qq, 16 * 128 + sq * 8: 16 * 128 + sq * 8 + i + 1] = 0.0
    return c, ms


def build(branches=(0, 1, 2, 3)):
    kb = KB()
    nc = kb.nc

    def din(name, shape):
        return kb.dram(name, shape, F32, "ExternalInput").ap()

    def dout(name, shape):
        return kb.dram(name, shape, F32, "ExternalOutput").ap()

    xp = din("xp", [2048, D]); xs = din("xs", [128, D])
    st_ssm = din("st_ssm", [2, 16, 1024, 128]); st_cssm = din("st_cssm", [2, 48, 2048])
    st_ccf = din("st_ccf", [2, 480, 1024]); st_csc = din("st_csc", [2, 32, 1024])
    ck = din("ck", [2, 16, 128, 256]); cv = din("cv", [2, 16, 128, 256])
    norm_w = din("norm_w", [2, D]); w_in = din("w_in", [2, D, INC])
    ssm_conv_w = din("ssm_conv_w", [2, 4, 2048]); ssm_conv_b = din("ssm_conv_b", [2, 2048])
    ssm_dt_bias = din("ssm_dt_bias", [2, 16]); ssm_a_log = din("ssm_a_log", [2, 16]); ssm_d = din("ssm_d", [2, 16])
    ssm_norm_w = din("ssm_norm_w", [2, 1024]); w_out_ssm = din("w_out_ssm", [2, 1024, D])
    cf_conv_w = din("cf_conv_w", [2, 31, 1024]); cf_conv_b = din("cf_conv_b", [2, 1024])
    cf_ln_w = din("cf_ln_w", [2, 1024]); cf_ln_b = din("cf_ln_b", [2, 1024]); w_out_cf = din("w_out_cf", [2, 1024, D])
    sc_conv_w = din("sc_conv_w", [2, 3, 1024]); w_out_sc = din("w_out_sc", [2, 1024, D])
    att_sinks = din("att_sinks", [2, 16]); w_out_att = din("w_out_att", [2, 1024, D])
    w_o = din("w_o", [2, D, D]); final_norm_w = din("final_norm_w", [1, D])
    consts_d = din("consts", [128, C_END]); masks_d = din("masks", [128, NMASKS])
    w_outs = [w_out_ssm, w_out_cf, w_out_sc, w_out_att]

    y_p = dout("y_p", [2048, D]); y_s = dout("y_s", [128, D])
    p_ssm = dout("p_ssm", [2, 1024, 128]); p_cssm = dout("p_cssm", [2, 3, 2048])
    p_ccf = dout("p_ccf", [2, 30, 1024]); p_csc = dout("p_csc", [2, 2, 1024])
    p_k = dout("p_k", [2, 128, 256]); p_v = dout("p_v", [2, 128, 256])
    s_ssm = dout("s_ssm", [2, 16, 1024, 128]); s_cssm = dout("s_cssm", [2, 48, 2048])
    s_ccf = dout("s_ccf", [2, 480, 1024]); s_csc = dout("s_csc", [2, 32, 1024])
    s_k = dout("s_k", [2, 16, 128, 256]); s_v = dout("s_v", [2, 16, 128, 256])
    dbg_d = dout("dbg", [128, 6, 4096]) if DBG_CORES else None

    def dump(i, view, n):
        if dbg_d is not None:
            kb.dma(dbg_d[:, i, 0:n], view, q="pool")

    x = kb.sb("x", [128, 4, D], F32)
    xnT = kb.sb("xnT", [128, 16, 512], BF16)
    hacc = kb.sb("hacc", [128, 16, 512], BF16)
    slots = [kb.sb("wslot%d" % i, [128, 16, 512], BF16) for i in range(NSLOT)]
    cst = kb.sb("cst", [128, C_END], F32)
    identb = kb.sb("identb", [128, 128], BF16)
    r16 = kb.sb("r16", [128, 4, 2, 16], F32)
    pcA = kb.sb("pcA", [128, 16, 10], F32)
    pcB = kb.sb("pcB", [128, 8, 74], F32)
    ss = kb.sb("ss", [128, 8], F32)
    hnat = [kb.sb("hnat%d" % l, [128, 8, 128], F32) for l in range(2)]
    tailA = [kb.sb("tailA%d" % l, [128, 16, 3], F32) for l in range(2)]
    tailB = [kb.sb("tailB%d" % l, [128, 8, 30], F32) for l in range(2)]
    tailC = [kb.sb("tailC%d" % l, [128, 8, 2], F32) for l in range(2)]
    KTprev = [kb.sb("KTprev%d" % l, [128, 2, 128], BF16) for l in range(2)]
    Vprev = [kb.sb("Vprev%d" % l, [128, 256], BF16) for l in range(2)]
    arF = Arena(kb, "F", 10496, F32)
    arB = Arena(kb, "B", 17920, BF16)

    pf = [kb.ps("pf%d" % i, [128, 512], F32) for i in range(4)]
    pw = kb.ps("pw", [128, 1024], F32)
    pb = [kb.ps("pb%d" % i, [128, 1024], BF16) for i in range(2)]
    rot = {"f": 0, "b": 0, "e": 0}

    def bank():
        rot["f"] = (rot["f"] + 1) % 4
        return pf[rot["f"]]

    def bbank():
        rot["b"] = (rot["b"] + 1) % 2
        return pb[rot["b"]]

    def ev():
        rot["e"] ^= 1
        return "act" if rot["e"] else "dve"

    identf = V(cst, cst.t[:, C_ID:C_ID + 128])

    def cview(c0, n, rows=128):
        return V(cst, cst.t[0:rows, c0:c0 + n])

    kb.dma(cst[:, :], consts_d, q="sp")
    kb.cp(identb[:, :], identf)
    for buf in hnat + tailA + tailB + tailC:
        kb.memset(V(buf, buf.t.ap()), 0.0, eng="pool")
    for buf in KTprev + Vprev:
        kb.memset(V(buf, buf.t.ap()), 0.0, eng="pool")
    for i, src in enumerate((ssm_dt_bias, ssm_a_log, ssm_d, att_sinks)):
        kb.dma(V(r16, r16.t[:, i, :, :].rearrange("p l h -> p (l h)")),
               src.rearrange("l h -> (l h)").partition_broadcast(128), q="sp")
    kb.act(r16[:, 1, :, :], r16[:, 1, :, :], AF.Exp)
    kb.ts(r16[:, 1, :, :], r16[:, 1, :, :], -1.0, None, ALU.mult)

    def grpv(buf, R, grp):
        if grp == 1:
            return buf[:, 0:R]
        return V(buf, buf.t[:, 0:R].rearrange("p (s w) -> p s w", s=grp))

    def load_T(dram2d, R, C, dst_fn, stage, grp=1):
        kb.dma(stage[0:R, 0:C], dram2d, q="sp")
        for c in range(C // 128):
            p = bank()
            kb.tr(p[:, 0:R], stage[0:R, c * 128:(c + 1) * 128], V(cst, cst.t[0:R, C_ID:C_ID + R]))
            kb.cp(dst_fn(c), grpv(p, R, grp), eng=ev())

    def store_T(src_fn, R, C, dram2d, stage, tmp, grp=1):
        for c in range(C // 128):
            kb.cp(grpv(tmp, R, grp), src_fn(c), eng="dve")
            p = bank()
            kb.tr(p[0:R, 0:128], tmp[:, 0:R], identf)
            kb.cp(stage[0:R, c * 128:(c + 1) * 128], p[0:R, 0:128], eng="act")
        kb.dma(dram2d, stage[0:R, 0:C], q="sp")

    stg = arF.alloc("stg", [128, 2048])
    for l in range(2):
        kb.dma(stg[l * 5:l * 5 + 4, :], ssm_conv_w[l], q="sp")
        kb.dma(stg[l * 5 + 4:l * 5 + 5, :], ssm_conv_b[l:l + 1, :], q="sp")
    for c in range(16):
        p = bank()
        kb.tr(p[:, 0:10], stg[0:10, c * 128:(c + 1) * 128], V(cst, cst.t[0:10, C_ID:C_ID + 10]))
        kb.cp(pcA[:, c, :], p[:, 0:10], eng=ev())
    stg2 = arF.alloc("stg2", [128, 1024])
    for l in range(2):
        o = l * 37
        kb.dma(stg2[o:o + 31, :], cf_conv_w[l], q="sp")
        kb.dma(stg2[o + 31:o + 32, :], cf_conv_b[l:l + 1, :], q="sp")
        kb.dma(stg2[o + 32:o + 33, :], cf_ln_w[l:l + 1, :], q="sp")
        kb.dma(stg2[o + 33:o + 34, :], cf_ln_b[l:l + 1, :], q="sp")
        kb.dma(stg2[o + 34:o + 37, :], sc_conv_w[l], q="sp")
    for c in range(8):
        p = bank()
        kb.tr(p[:, 0:74], stg2[0:74, c * 128:(c + 1) * 128], V(cst, cst.t[0:74, C_ID:C_ID + 74]))
        kb.cp(pcB[:, c, :], p[:, 0:74], eng=ev())

    def layer_plan(l):
        P = []

        def win(name, c0, n):
            P.append((name, w_in[l].rearrange("(k p) c -> p k c", p=128)[:, :, c0:c0 + n], 16, n))

        def outp(b):
            for blk in range(4):
                win("merge%d_%d" % (b, blk), MG0 + b * 2048 + blk * 512, 512)
                P.append(("wout%d_%d" % (b, blk),
                          w_outs[b][l].rearrange("(k p) c -> p k c", p=128)[:, :, blk * 512:(blk + 1) * 512], 8, 512))
        if 2 in branches:
            for nm, c0 in (("scc", SCC0), ("scv", SCV0), ("scb", SCB0), ("scg", SCG0)):
                for b in range(2):
                    win("%s%d" % (nm, b), c0 + 512 * b, 512)
            outp(2)
        if 1 in branches:
            for b in range(2):
                win("cfb%d" % b, CFB0 + 512 * b, 512)
            for b in range(2):
                win("cfa%d" % b, CFA0 + 512 * b, 512)
            for b in range(2):
                win("cfg%d" % b, CFG0 + 512 * b, 512)
            outp(1)
        if 0 in branches:
            for b in range(4):
                win("xbc%d" % b, XBC0 + 512 * b, 512)
            for b in range(2):
                win("z%d" % b, Z0 + 512 * b, 512)
            win("dt", DT0, 16)
            outp(0)
        if 3 in branches:
            for b in range(2):
                win("ag%d" % b, AG0 + 512 * b, 512)
            for b in range(2):
                win("q%d" % b, Q0 + 512 * b, 512)
            win("kv", K0, 512)
            outp(3)
        for blk in range(4):
            P.append(("wo%d" % blk, w_o[l].rearrange("(k p) c -> p k c", p=128)[:, :, blk * 512:(blk + 1) * 512], 16, 512))
        return P

    chunks = CHUNKS if CHUNKS else [("P", i) for i in range(4)] + [("S", 0)]
    plan = []
    for ch in chunks:
        for l in range(2):
            plan += layer_plan(l)
    wst = {"issued": 0, "next": 0}
    NPL = len(layer_plan(0))
    wscr = kb.dram("wscr", [2 * NPL, 128, 8192], BF16, "Internal").ap() if len(chunks) > 1 else None

    def w_issue(j):
        name, src, kc, n = plan[j]
        slot = slots[j % NSLOT]
        bi = j % (2 * NPL)
        if wscr is not None:
            scr = wscr[bi][:, 0:kc * n].rearrange("p (k c) -> p k c", k=kc)
            if j >= 2 * NPL:
                kb.dma(slot[:, 0:kc, 0:n], scr, q="pool")
                return
            kb.dma(slot[:, 0:kc, 0:n], src, q="pool")
            kb.dma(scr, slot[:, 0:kc, 0:n], q="sp")
            return
        if isinstance(src, tuple):
            _, l, m = src
            base = w_in[l].rearrange("(k p) c -> p k c", p=128)
            for half in range(2):
                for r in range(4):
                    c0 = Q0 + m * 512 + half * 256 + r * 64
                    d0 = r * 128 + half * 64
                    kb.dma(slot[:, :, d0:d0 + 64], base[:, :, c0:c0 + 64], q="pool")
        else:
            kb.dma(slot[:, 0:kc, 0:n], src, q="pool")

    def wget(name, hold=0):
        j = wst["next"]
        assert plan[j][0] == name, (plan[j][0], name)
        while wst["issued"] < min(len(plan), j + NSLOT - hold):
            w_issue(wst["issued"])
            wst["issued"] += 1
        wst["next"] += 1
        return slots[j % NSLOT]

    def proj_fm(slot, kc, col_lo, ncols, rhs, T):
        p = bank()
        for k in range(kc):
            kb.mm(p[0:ncols, 0:T], slot[:, k, col_lo:col_lo + ncols], rhs[:, k, 0:T], start=(k == 0), stop=(k == kc - 1))
        return p

    def proj_tm(slot, kc, col_lo, ncols, j, p=None):
        if p is None:
            p = bank()
        for k in range(kc):
            kb.mm(p[:, 0:ncols], xnT[:, k, j * 128:(j + 1) * 128], slot[:, k, col_lo:col_lo + ncols],
                  start=(k == 0), stop=(k == kc - 1))
        return p

    def rstd_of(src, C, eps, junk):
        kb.act(junk, src, AF.Square, accum=ss[:, 0:1])
        kb.ts(ss[:, 1:2], ss[:, 0:1], 1.0 / C, eps, ALU.mult, ALU.add)
        kb.act(ss[:, 2:3], ss[:, 1:2], AF.Sqrt)
        kb.recip(ss[:, 3:4], ss[:, 2:3])
        return ss[:, 3:4]

    def transpose_to_fm(src_bf, ncol_chunks, dst_fn):
        for c0 in range(0, ncol_chunks, 8):
            n = min(8, ncol_chunks - c0)
            p = bbank()
            for i in range(n):
                kb.tr(p[:, i * 128:(i + 1) * 128], src_bf[:, (c0 + i) * 128:(c0 + i + 1) * 128], identb[:, :])
            kb.cp(dst_fn(c0, n), V(p, p.t[:, 0:n * 128].rearrange("p (c t) -> p c t", c=n)), eng=ev())

    def seqv(buf, ch, nseq, W, lo, L):
        if nseq == 1:
            return buf[:, ch, lo:lo + L]
        return V(buf, buf.t[:, ch, :].rearrange("p (s w) -> p s w", s=nseq)[:, :, lo:lo + L])

    def psv(p, nseq, L):
        if nseq == 1:
            return p[:, 0:L]
        return V(p, p.t[:, 0:nseq * L].rearrange("p (s w) -> p s w", s=nseq))

    def branch_out(l, b, yT, T, first):
        for blk in range(4):
            sg = wget("merge%d_%d" % (b, blk))
            so = wget("wout%d_%d" % (b, blk), hold=1)
            for sub in range(4):
                cb = blk * 4 + sub
                pg = proj_fm(sg, 16, sub * 128, 128, xnT, T)
                po = proj_fm(so, 8, sub * 128, 128, yT, T)
                g = gtmp[rot_g[0] % 2]
                rot_g[0] += 1
                kb.act(g[:, 0:T], pg[:, 0:T], AF.Sigmoid)
                if first:
                    kb.tt(hacc[:, cb, 0:T], g[:, 0:T], po[:, 0:T], ALU.mult)
                else:
                    kb.tt(g[:, 0:T], g[:, 0:T], po[:, 0:T], ALU.mult)
                    kb.tt(hacc[:, cb, 0:T], hacc[:, cb, 0:T], g[:, 0:T], ALU.add, eng="dve")

    rot_g = [0]
    gtmp = [None, None]

    def alloc_gtmp():
        gtmp[0] = arF.alloc("gt0", [128, 512])
        gtmp[1] = arF.alloc("gt1", [128, 512])

    for (kind, ci) in chunks:
        isP = kind == "P"
        T = 512 if isP else 128
        NT = T // 128
        nseq = 1 if isP else 16
        L = T // nseq
        last_chunk = isP and ci == max(c_[1] for c_ in chunks if c_[0] == 'P')
        arF.reset(); arB.reset()
        src = xp[ci * 512:(ci + 1) * 512, :] if isP else xs
        kb.dma(x[:, 0:NT, :], src.rearrange("(j p) d -> p j d", p=128), q="sp")

        for l in range(2):
            arF.reset(); arB.reset()
            rowbuf = arF.alloc("rowbuf", [128, D])
            kb.dma(rowbuf[:, :], norm_w[l].partition_broadcast(128), q="sp")
            xnb = arB.alloc("xnb", [128, D])
            for j in range(NT):
                r = rstd_of(x[:, j, :], D, 1e-6, xnb[:, :])
                kb.stt(xnb[:, :], x[:, j, :], r, rowbuf[:, :], ALU.mult, ALU.mult)
                transpose_to_fm(xnb, 16, lambda c0, n, j=j: xnT[:, c0:c0 + n, j * 128:(j + 1) * 128])
            first = True

            if 2 in branches:
                arF.reset(); arB.reset(); alloc_gtmp()
                Wc = 2 + L
                cbuf = arF.alloc("cbuf", [128, 8, nseq * Wc])
                acc = arF.alloc("acc", [128, 8, T])
                yT = arB.alloc("yT", [128, 8, T])
                if not isP:
                    stgc = arF.alloc("stgc", [128, 1024])
                    load_T(st_csc[l], 32, 1024,
                           lambda c: V(cbuf, cbuf.t[:, c, :].rearrange("p (s w) -> p s w", s=16)[:, :, 0:2]), stgc, grp=16)
                for blk in range(2):
                    s_ = wget("scc%d" % blk)
                    for sub in range(4):
                        ch = blk * 4 + sub
                        p = proj_fm(s_, 16, sub * 128, 128, xnT, T)
                        kb.cp(seqv(cbuf, ch, nseq, Wc, 2, L), psv(p, nseq, L), eng="act")
                        if isP:
                            kb.cp(cbuf[:, ch, 0:2], tailC[l][:, ch, :], eng="act")
                for blk in range(2):
                    s_ = wget("scv%d" % blk)
                    for sub in range(4):
                        ch = blk * 4 + sub
                        p = proj_fm(s_, 16, sub * 128, 128, xnT, T)
                        kb.tt(seqv(cbuf, ch, nseq, Wc, 2, L), seqv(cbuf, ch, nseq, Wc, 2, L), psv(p, nseq, L), ALU.mult)
                        if isP:
                            kb.cp(tailC[l][:, ch, :], cbuf[:, ch, L:L + 2], eng="act")
                        a3 = V(acc, acc.t[:, ch, :].rearrange("p (s w) -> p s w", s=nseq)) if nseq > 1 else acc[:, ch, :]
                        w0 = l * 37 + 34
                        kb.ts(a3, seqv(cbuf, ch, nseq, Wc, 0, L), pcB[:, ch, w0:w0 + 1], None, ALU.mult)
                        for k in (1, 2):
                            kb.stt(a3, seqv(cbuf, ch, nseq, Wc, k, L), pcB[:, ch, w0 + k:w0 + k + 1], a3, ALU.mult, ALU.add)
                if not isP:
                    stgo = arF.alloc("stgo", [128, 1024]); tmpo = arF.alloc("tmpo", [128, 128])
                    store_T(lambda c: V(cbuf, cbuf.t[:, c, :].rearrange("p (s w) -> p s w", s=16)[:, :, 8:10]),
                            32, 1024, s_csc[l], stgo, tmpo, grp=16)
                for blk in range(2):
                    s_ = wget("scb%d" % blk)
                    for sub in range(4):
                        ch = blk * 4 + sub
                        p = proj_fm(s_, 16, sub * 128, 128, xnT, T)
                        kb.tt(acc[:, ch, :], acc[:, ch, :], p[:, 0:T], ALU.mult)
                for blk in range(2):
                    s_ = wget("scg%d" % blk)
                    for sub in range(4):
                        ch = blk * 4 + sub
                        p = proj_fm(s_, 16, sub * 128, 128, xnT, T)
                        g = gtmp[rot_g[0] % 2]; rot_g[0] += 1
                        kb.act(g[:, 0:T], p[:, 0:T], AF.Silu)
                        kb.tt(yT[:, ch, :], acc[:, ch, :], g[:, 0:T], ALU.mult)
                if isP and ci == 0 and l == 0:
                    dump(0, V(acc, acc.t[:, :, :].rearrange("p a b -> p (a b)")), 4096)
                    dump(1, V(yT, yT.t[:, :, :].rearrange("p a b -> p (a b)")), 4096)
                branch_out(l, 2, yT, T, first)
                if isP and ci == 0 and l == 0:
                    dump(2, V(hacc, hacc.t[:, 0:8, :].rearrange("p a b -> p (a b)")), 4096)
                first = False

            if 1 in branches:
                arF.reset(); arB.reset(); alloc_gtmp()
                Wb = 30 + L
                ubuf = arF.alloc("ubuf", [128, 8, nseq * Wb])
                acc = arF.alloc("acc", [128, 8, T])
                yT = arB.alloc("yT", [128, 8, T])
                if True:
                    ubf = arB.alloc("ubf", [128, nseq * Wb])
                    dg = [arB.alloc("dg%d" % i, [128, 128]) for i in range(4)]
                if not isP:
                    stgc = arF.alloc("stgc", [128, 1024])
                    for rb in range(4):
                        load_T(st_ccf[l][rb * 120:(rb + 1) * 120, :], 120, 1024,
                               lambda c, rb=rb: V(ubuf, ubuf.t[:, c, :].rearrange("p (s w) -> p s w", s=16)[:, rb * 4:(rb + 1) * 4, 0:30]),
                               stgc, grp=4)
                for blk in range(2):
                    s_ = wget("cfb%d" % blk)
                    for sub in range(4):
                        ch = blk * 4 + sub
                        p = proj_fm(s_, 16, sub * 128, 128, xnT, T)
                        kb.act(seqv(ubuf, ch, nseq, Wb, 30, L), psv(p, nseq, L), AF.Sigmoid)
                        if isP:
                            kb.cp(ubuf[:, ch, 0:30], tailB[l][:, ch, :], eng="act")
                w0 = l * 37
                for blk in range(2):
                    s_ = wget("cfa%d" % blk)
                    for sub in range(4):
                        ch = blk * 4 + sub
                        p = proj_fm(s_, 16, sub * 128, 128, xnT, T)
                        kb.tt(seqv(ubuf, ch, nseq, Wb, 30, L), seqv(ubuf, ch, nseq, Wb, 30, L), psv(p, nseq, L), ALU.mult)
                        if isP:
                            kb.cp(tailB[l][:, ch, :], ubuf[:, ch, L:L + 30], eng="act")
                        if True:
                            kb.cp(ubf[:, 0:nseq * Wb], ubuf[:, ch, 0:nseq * Wb], eng="act")
                            pc = bank()
                            for k in range(31):
                                dgk = dg[k % 4]
                                kb.ts(dgk[:, :], identb[:, :], pcB[:, ch, w0 + k:w0 + k + 1], None, ALU.mult)
                                if isP:
                                    mv = ubf[:, k:k + L]
                                else:
                                    mv = V(ubf, ubf.t[:, 0:nseq * Wb].rearrange("p (s w) -> p s w", s=16)[:, :, k:k + L])
                                kb.mm(pc[:, 0:T], dgk[:, :], mv, start=(k == 0), stop=(k == 30), sig=True)
                            kb.act(acc[:, ch, :], pc[:, 0:T], AF.Identity, bias=pcB[:, ch, w0 + 31:w0 + 32])
                        else:
                            a3 = V(acc, acc.t[:, ch, :].rearrange("p (s w) -> p s w", s=nseq))
                            kb.ts(a3, seqv(ubuf, ch, nseq, Wb, 0, L), pcB[:, ch, w0:w0 + 1], pcB[:, ch, w0 + 31:w0 + 32], ALU.mult, ALU.add)
                            for k in range(1, 31):
                                kb.stt(a3, seqv(ubuf, ch, nseq, Wb, k, L), pcB[:, ch, w0 + k:w0 + k + 1], a3, ALU.mult, ALU.add)
                if not isP:
                    stgo = arF.alloc("stgo", [128, 1024]); tmpo = arF.alloc("tmpo", [128, 128])
                    for rb in range(4):
                        store_T(lambda c, rb=rb: V(ubuf, ubuf.t[:, c, :].rearrange("p (s w) -> p s w", s=16)[:, rb * 4:(rb + 1) * 4, 8:38]),
                                120, 1024, s_ccf[l][rb * 120:(rb + 1) * 120, :], stgo, tmpo, grp=4)
                sq = gtmp[0]; mu = arF.alloc("mu", [128, T]); rs = arF.alloc("rs", [128, T])
                onesN = cview(C_ONES, 128)
                pm = bank()
                for ch in range(8):
                    kb.mm(pm[:, 0:T], onesN, acc[:, ch, :], start=(ch == 0), stop=(ch == 7))
                kb.ts(mu[:, :], pm[:, 0:T], 1.0 / 1024, None, ALU.mult)
                pv = bank()
                for ch in range(8):
                    kb.act(sq[:, 0:T], acc[:, ch, :], AF.Square)
                    kb.mm(pv[:, 0:T], onesN, sq[:, 0:T], start=(ch == 0), stop=(ch == 7), sig=True)
                kb.tt(sq[:, 0:T], mu[:, :], mu[:, :], ALU.mult)
                kb.stt(rs[:, :], pv[:, 0:T], 1.0 / 1024, sq[:, 0:T], ALU.mult, ALU.subtract)
                kb.ts(rs[:, :], rs[:, :], 1e-5, None, ALU.add)
                kb.act(rs[:, :], rs[:, :], AF.Sqrt)
                kb.recip(rs[:, :], rs[:, :])
                for ch in range(8):
                    kb.tt(acc[:, ch, :], acc[:, ch, :], mu[:, :], ALU.subtract)
                    kb.tt(acc[:, ch, :], acc[:, ch, :], rs[:, :], ALU.mult)
                    kb.act(acc[:, ch, :], acc[:, ch, :], AF.Silu, scale=pcB[:, ch, w0 + 32:w0 + 33], bias=pcB[:, ch, w0 + 33:w0 + 34])
                for blk in range(2):
                    s_ = wget("cfg%d" % blk)
                    for sub in range(4):
                        ch = blk * 4 + sub
                        p = proj_fm(s_, 16, sub * 128, 128, xnT, T)
                        g = gtmp[rot_g[0] % 2]; rot_g[0] += 1
                        kb.act(g[:, 0:T], p[:, 0:T], AF.Silu)
                        kb.tt(yT[:, ch, :], acc[:, ch, :], g[:, 0:T], ALU.mult)
                branch_out(l, 1, yT, T, first)
                first = False


            if 0 in branches:
                arF.reset(); arB.reset()
                Wa = 3 + L
                xbcT = arB.alloc("xbcT", [128, 16, T])
                zs = arB.alloc("zs", [128, NT, 1024])
                x_tok = arB.alloc("x_tok", [128, 1024]); xdt = arB.alloc("xdt", [128, 1024]); xdec = arB.alloc("xdec", [128, 1024])
                B_tok = arB.alloc("B_tok", [128, 512])
                if isP:
                    MTb = arB.alloc("MTb", [128, 2048])
                    hTbuf, hToff = MTb, 1024
                else:
                    Bm = arB.alloc("Bm", [128, 512])
                    MT = arB.alloc("MT", [128, 4, 128]); hT_bf = arB.alloc("hT_bf", [128, 1024])
                    hTbuf, hToff = hT_bf, 0
                cbufA = [arF.alloc("cbA%d" % i, [128, nseq * Wa]) for i in range(2)]
                accA = arF.alloc("accA", [128, T])
                dtt = arF.alloc("dtt", [128, NT, 16]); dat = arF.alloc("dat", [128, NT, 16])
                sp_ = arF.alloc("sp_", [128, 32]); sc = arF.alloc("sc", [128, 64])
                rowb = arF.alloc("rowb", [128, 1024])
                kb.dma(rowb[:, :], ssm_norm_w[l].partition_broadcast(128), q="sp")
                tmpo = arF.alloc("tmpo", [128, 128])
                if not isP:
                    stgA = arF.alloc("stgA", [128, 1024]); hist = arF.alloc("hist", [128, 16, 48])
                    for half in range(2):
                        load_T(st_cssm[l][:, half * 1024:(half + 1) * 1024], 48, 1024,
                               lambda c, half=half: hist[:, half * 8 + c, :], stgA)
                w0 = l * 5
                for blk in range(4):
                    s_ = wget("xbc%d" % blk)
                    for sub in range(4):
                        ch = blk * 4 + sub
                        p = proj_fm(s_, 16, sub * 128, 128, xnT, T)
                        cb = cbufA[ch % 2]

                        def cbv(lo, n, cb=cb):
                            if isP:
                                return cb[:, lo:lo + n]
                            return V(cb, cb.t[:, :].rearrange("p (s w) -> p s w", s=16)[:, :, lo:lo + n])
                        kb.cp(cbv(3, L), psv(p, nseq, L), eng="act")
                        if isP:
                            kb.cp(cb[:, 0:3], tailA[l][:, ch, :], eng="act")
                            kb.cp(tailA[l][:, ch, :], cb[:, L:L + 3], eng="act")
                        else:
                            hv = V(hist, hist.t[:, ch, :].rearrange("p (s w) -> p s w", s=16))
                            kb.cp(cbv(0, 3), hv, eng="act")
                            kb.cp(hv, cbv(8, 3), eng="act")
                        a3 = accA[:, 0:T] if isP else V(accA, accA.t[:, 0:T].rearrange("p (s w) -> p s w", s=16))
                        kb.ts(a3, cbv(0, L), pcA[:, ch, w0:w0 + 1], None, ALU.mult)
                        for k in range(1, 4):
                            kb.stt(a3, cbv(k, L), pcA[:, ch, w0 + k:w0 + k + 1], a3, ALU.mult, ALU.add)
                        kb.act(xbcT[:, ch, :], accA[:, 0:T], AF.Silu, bias=pcA[:, ch, w0 + 4:w0 + 5])
                if not isP:
                    for half in range(2):
                        store_T(lambda c, half=half: hist[:, half * 8 + c, :], 48, 1024,
                                s_cssm[l][:, half * 1024:(half + 1) * 1024], stgA, tmpo)
                for blk in range(2):
                    s_ = wget("z%d" % blk)
                    for j in range(NT):
                        p = proj_tm(s_, 16, 0, 512, j)
                        kb.act(zs[:, j, blk * 512:(blk + 1) * 512], p[:, 0:512], AF.Silu)
                s_ = wget("dt")
                one_col = cview(C_ONES, 1)
                for j in range(NT):
                    p = proj_tm(s_, 16, 0, 16, j)
                    kb.tt(sp_[:, 0:16], p[:, 0:16], r16[:, 0, l, :], ALU.add)
                    kb.act(sp_[:, 16:32], sp_[:, 0:16], AF.Abs)
                    kb.act(sp_[:, 16:32], sp_[:, 16:32], AF.Exp, scale=-1.0)
                    kb.act(sp_[:, 16:32], sp_[:, 16:32], AF.Ln, bias=one_col)
                    kb.ts(sp_[:, 0:16], sp_[:, 0:16], 0.0, None, ALU.max)
                    kb.tt(dtt[:, j, :], sp_[:, 0:16], sp_[:, 16:32], ALU.add)
                    kb.tt(dat[:, j, :], dtt[:, j, :], r16[:, 1, l, :], ALU.mult)
                NG = 16 if isP else 4
                rhs_cs = arF.alloc("rhs_cs", [128, NG, 128]); cbtm = arF.alloc("cbtm", [128, NG // 4, 128])
                dE = arF.alloc("dE", [128, NG, 128]); pyo_sb = arF.alloc("pyo_sb", [128, 8, 128])
                t1 = arF.alloc("t1", [128, 1024]); ssrep = arF.alloc("ssrep", [128, 128])
                cdecT = arF.alloc("cdecT", [128, 8, 16])
                if not isP:
                    h0nat = arF.alloc("h0nat", [128, 8, 128]); hnew = arF.alloc("hnew", [128, 8, 128])
                tri = cview(C_TRIP if isP else C_TRIS, 128)
                same = cview(C_ONES if isP else C_SAMES, 128)
                ones_m = cview(C_ONES, 128)
                lastsel = cview(C_LASTP, 1) if isP else cview(C_LASTS, 16)
                lrot = [0]

                def lbank():
                    lrot[0] ^= 1
                    return pf[2 + lrot[0]]
                FB = arB.F
                FF = arF.F
                for c in range(NT):
                    t0 = c * 128
                    pbk = bbank()
                    for i in range(8):
                        kb.tr(pbk[:, i * 128:(i + 1) * 128], xbcT[:, i, t0:t0 + 128], identb[:, :])
                    kb.cp(x_tok[:, :], pbk[:, 0:1024], eng="act")
                    pbk = bbank()
                    for i in range(4):
                        kb.tr(pbk[:, i * 128:(i + 1) * 128], xbcT[:, 8 + i, t0:t0 + 128], identb[:, :])
                    kb.cp(B_tok[:, :], pbk[:, 0:512], eng="dve")
                    p = lbank()
                    kb.mm(p[:, 0:16], tri, dat[:, c, :])
                    kb.cp(sc[:, 0:16], p[:, 0:16], eng="dve")
                    p = lbank()
                    kb.mm(p[:, 0:16], same, dat[:, c, :])
                    kb.cp(sc[:, 48:64], p[:, 0:16], eng="dve")
                    kb.tt(sc[:, 16:32], sc[:, 48:64], sc[:, 0:16], ALU.subtract)
                    kb.act(sc[:, 16:32], sc[:, 16:32], AF.Exp)
                    kb.act(sc[:, 32:48], sc[:, 0:16], AF.Exp)
                    x3 = V(x_tok, x_tok.t[:, :].rearrange("p (h d) -> p h d", h=16))
                    xdt3 = V(xdt, xdt.t[:, :].rearrange("p (h d) -> p h d", h=16))
                    xdec3 = V(xdec, xdec.t[:, :].rearrange("p (h d) -> p h d", h=16))
                    kb.tt(xdt3, x3, dtt.pat(c * 16, [[FF, 128], [1, 16], [0, 64]]), ALU.mult)
                    kb.tt(xdec3, xdt3, sc.pat(16, [[FF, 128], [1, 16], [0, 64]]), ALU.mult)
                    p = lbank()
                    for c8 in range(8):
                        kb.cp(V(ssrep, ssrep.t[:, :].rearrange("p (h d) -> p h d", h=2)),
                              sc.pat(48 + 2 * c8, [[FF, 128], [1, 2], [0, 64]]), eng="dve")
                        kb.mm(p[:, c8 * 16:c8 * 16 + nseq], ssrep[:, :], lastsel)
                    kb.act(cdecT[:, :, 0:nseq], V(p, p.t[:, 0:128].rearrange("p (a b) -> p a b", a=8)[:, :, 0:nseq]), AF.Exp)
                    if isP:
                        tri_off = C_TRIP
                        kb.tt(rhs_cs[:, :, :], dat.pat(c * 16, [[FF, 128], [1, 16], [0, 128]]),
                              V(cst, bass.AP(cst.t, tri_off, [[C_END, 128], [0, 16], [1, 128]])), ALU.mult)
                        for g in range(4):
                            kb.mm(pf[g][:, 0:512], ones_m,
                                  V(rhs_cs, rhs_cs.t[:, 4 * g:4 * g + 4, :].rearrange("p a b -> p (a b)")))
                        for h in range(16):
                            kb.ts(dE[:, h, :], pf[h // 4][:, (h % 4) * 128:(h % 4 + 1) * 128], sc[:, h:h + 1], 0.0,
                                  ALU.subtract, ALU.min)
                        kb.act(dE[:, :, :], dE[:, :, :], AF.Exp)
                        for g in range(4):
                            kb.mm(pw[:, g * 128:(g + 1) * 128], xbcT[:, 8 + g, t0:t0 + 128], xbcT[:, 12 + g, t0:t0 + 128])
                        kb.tt(cbtm[:, :, :], V(pw, pw.t[:, 0:512].rearrange("p (a b) -> p a b", a=4)),
                              V(cst, bass.AP(cst.t, tri_off, [[C_END, 128], [0, 4], [1, 128]])), ALU.mult)
                        kb.tt(V(MTb, MTb.t[:, :].rearrange("p (g h l) -> p g h l", g=4, h=4)),
                              V(dE, dE.t[:, :, :].rearrange("p (g h) l -> p g h l", g=4)),
                              cbtm.pat(0, [[FF, 128], [128, 4], [0, 4], [1, 128]]), ALU.mult)
                        for h in range(16):
                            kb.mm(pw[:, h * 64:(h + 1) * 64], MTb[:, h * 128:(h + 1) * 128], xdt[:, h * 64:(h + 1) * 64])
                    for g in range(0 if isP else 4):
                        kb.tt(rhs_cs[:, :, :], dat.pat(c * 16 + 4 * g, [[FF, 128], [1, 4], [0, 128]]),
                              V(cst, bass.AP(cst.t, (C_TRIP if isP else C_TRIS), [[C_END, 128], [0, 4], [1, 128]])), ALU.mult)
                        pr = lbank()
                        kb.mm(pr[:, 0:512], ones_m, V(rhs_cs, rhs_cs.t[:, :, :].rearrange("p a b -> p (a b)")))
                        for hl in range(4):
                            h = 4 * g + hl
                            kb.ts(dE[:, hl, :], pr[:, hl * 128:(hl + 1) * 128], sc[:, h:h + 1], 0.0, ALU.subtract, ALU.min)
                        kb.act(dE[:, :, :], dE[:, :, :], AF.Exp)
                        pc = lbank()
                        kb.mm(pc[:, 0:128], xbcT[:, 8 + g, t0:t0 + 128], xbcT[:, 12 + g, t0:t0 + 128])
                        kb.tt(cbtm[:, 0, :], pc[:, 0:128], tri, ALU.mult)
                        kb.tt(MT[:, :, :], dE[:, :, :], cbtm.pat(0, [[FF, 128], [0, 4], [1, 128]]), ALU.mult)
                        for hl in range(4):
                            h = 4 * g + hl
                            kb.mm(pw[:, h * 64:(h + 1) * 64], MT[:, hl, :], xdt[:, h * 64:(h + 1) * 64])
                    pyo = [pf[0], pf[1]]
                    for s in range(nseq):
                        if isP:
                            h0 = hnat[l]
                        else:
                            h0 = h0nat
                            kb.dma(h0nat[:, :, :], st_ssm[l, s].rearrange("(c q) n -> q c n", q=128), q="sp")
                        for half in range(2):
                            pq = lbank()
                            for i in range(4):
                                kb.tr(pq[:, i * 128:(i + 1) * 128], h0[:, half * 4 + i, :], identf)
                            kb.cp(hTbuf[:, hToff + half * 512:hToff + (half + 1) * 512], pq[:, 0:512], eng=ev())
                        for c8 in range(8):
                            g = c8 // 2
                            Lt = 128 if isP else 8
                            kb.mm(pyo[c8 // 4][:, (c8 % 4) * 128 + s * Lt:(c8 % 4) * 128 + (s + 1) * Lt],
                                  hTbuf[:, hToff + c8 * 128:hToff + (c8 + 1) * 128], xbcT[:, 12 + g, t0 + s * Lt:t0 + (s + 1) * Lt])
                        if not isP:
                            kb.ts(Bm[:, :], B_tok[:, :], cview(C_SELS + s, 1), None, ALU.mult)
                        Bsrc = B_tok if isP else Bm
                        for c8 in range(8):
                            g = c8 // 2
                            pst = lbank()
                            kb.mm(pst[:, 0:128], xdec[:, c8 * 128:(c8 + 1) * 128], Bsrc[:, g * 128:(g + 1) * 128])
                            dst = hnat[l][:, c8, :] if isP else hnew[:, c8, :]
                            kb.stt(dst, h0[:, c8, :], cdecT[:, c8, s:s + 1], pst[:, 0:128], ALU.mult, ALU.add)
                        if not isP:
                            kb.dma(s_ssm[l, s].rearrange("(c q) n -> q c n", q=128), hnew[:, :, :], q="sp")
                    for half in range(2):
                        kb.cp(pyo_sb[:, half * 4:(half + 1) * 4, :],
                              V(pyo[half], pyo[half].t[:, 0:512].rearrange("p (a b) -> p a b", a=4)), eng=ev())
                    for half in range(2):
                        pq = pyo[half]
                        for i in range(4):
                            kb.tr(pq[:, i * 128:(i + 1) * 128], pyo_sb[:, half * 4 + i, :], identf)
                        kb.tt(V(t1, t1.t[:, half * 512:(half + 1) * 512].rearrange("p (h d) -> p h d", h=8)),
                              V(pq, pq.t[:, 0:512].rearrange("p (h d) -> p h d", h=8)),
                              sc.pat(32 + half * 8, [[FF, 128], [1, 8], [0, 64]]), ALU.mult)
                    kb.tt(t1[:, :], t1[:, :], pw[:, 0:1024], ALU.add)
                    t2 = V(pyo_sb, pyo_sb.t[:, :, :].rearrange("p a b -> p (a b)"))
                    kb.tt(V(pyo_sb, pyo_sb.t[:, :, :].rearrange("p a (h d) -> p (a h) d", h=2)), x3,
                          V(r16, bass.AP(r16.t, (2 * 2 + l) * 16, [[128, 128], [1, 16], [0, 64]])), ALU.mult)
                    kb.tt(t1[:, :], t1[:, :], t2, ALU.add)
                    kb.tt(t1[:, :], t1[:, :], zs[:, c, :], ALU.mult)
                    r = rstd_of(t1[:, :], 1024, 1e-5, xdt[:, :])
                    kb.stt(xdt[:, :], t1[:, :], r, rowb[:, :], ALU.mult, ALU.mult)
                    transpose_to_fm(xdt, 8, lambda c0, n, t0=t0: xbcT[:, c0:c0 + n, t0:t0 + 128])
                arF.reset(); alloc_gtmp()
                branch_out(l, 0, xbcT, T, first)
                first = False

            if 3 in branches:
                arF.reset(); arB.reset(); alloc_gtmp()
                NK = 2 if isP else 17
                gsil = arB.alloc("gsil", [128, NT, 1024])
                QT = arB.alloc("QT", [128, 8, T])
                KT = arB.alloc("KT", [128, 2, (128 + T) if isP else 17 * 128])
                vtok = arB.alloc("vtok", [128, (1 + NT) if isP else 17, 256])
                Ogb = arB.alloc("Ogb", [128, 1024])
                GH = 4 if isP else 1
                NSET = 2 if isP else 1
                Pbfs = [arB.alloc("Pbf%d" % i, [128, GH * NK * 128]) for i in range(NSET)]
                PTs = [arB.alloc("PT%d" % i, [128, GH * NK, 128]) for i in range(NSET)]
                Pbf, PT = Pbfs[0], PTs[0]
                yT = QT if isP else arB.alloc("yT", [128, 8, T])
                kvf = arF.alloc("kvf", [128, 512])
                Ssbs = [arF.alloc("Ssb%d" % i, [128, GH * NK * 128]) for i in range(NSET)]
                Ssb = Ssbs[0]
                sm = arF.alloc("sm", [128, 8])
                sm4s = [arF.alloc("sm4%d" % i, [128, 24]) for i in range(NSET)]
                Otmps = [arF.alloc("Otmp%d" % i, [128, 256]) for i in range(NSET)]
                koff = 128 if isP else 16 * 128
                voff = 1 if isP else 16
                if not isP:
                    maskS = arF.alloc("maskS", [128, NMASKS])
                    if "m" not in DFLAGS:
                        kb.dma(maskS[:, :], masks_d, q="sp")
                    ckt = [arF.alloc("ckt%d" % i, [128, 256]) for i in range(2)]
                    for s in range(16 if "c" not in DFLAGS else 0):
                        kb.dma(ckt[s % 2][:, :], ck[l, s], q="sp")
                        kb.dma(vtok[:, s, :], cv[l, s], q="pool")
                        for c in range(2):
                            p = bank()
                            kb.tr(p[:, 0:128], ckt[s % 2][:, c * 128:(c + 1) * 128], identf)
                            kb.cp(KT[:, c, s * 128:(s + 1) * 128], p[:, 0:128], eng=ev())
                    if "d" not in DFLAGS:
                        kb.dma(s_k[l][:, 0:120, :], ck[l][:, 8:128, :], q="sp")
                        kb.dma(s_v[l][:, 0:120, :], cv[l][:, 8:128, :], q="sp")
                elif "p" not in DFLAGS:
                    for c in range(2):
                        kb.cp(KT[:, c, 0:128], KTprev[l][:, c, :], eng="pool")
                    kb.cp(vtok[:, 0, :], Vprev[l][:, :], eng="pool")
                for blk in range(2):
                    s_ = wget("ag%d" % blk)
                    for j in range(NT if "1" not in DFLAGS else 0):
                        p = proj_tm(s_, 16, 0, 512, j)
                        kb.act(gsil[:, j, blk * 512:(blk + 1) * 512], p[:, 0:512], AF.Silu)
                qtok = Pbf
                for m_ in range(2):
                    s_ = wget("q%d" % m_)
                    for j in range(NT if "2" not in DFLAGS else 0):
                        p = proj_tm(s_, 16, 0, 512, j)
                        kb.cp(V(qtok, qtok.t[:, 0:512].rearrange("p (r hf d) -> p r hf d", r=4, hf=2)),
                              V(p, p.t[:, 0:512].rearrange("p (hf r d) -> p r hf d", hf=2, r=4)), eng=ev())
                        transpose_to_fm(qtok, 4, lambda c0, n, j=j, m_=m_: QT[:, 4 * m_ + c0:4 * m_ + c0 + n, j * 128:(j + 1) * 128])
                s_ = wget("kv")
                for j in range(NT if "3" not in DFLAGS else 0):
                    p = proj_tm(s_, 16, 0, 512, j)
                    kb.cp(vtok[:, voff + j, :], p[:, 256:512], eng="act")
                    if (last_chunk and j == NT - 1) or not isP:
                        kb.cp(kvf[:, :], p[:, 0:512], eng="dve")
                for c in range(2 if "4" not in DFLAGS else 0):
                    p = proj_fm(s_, 16, c * 128, 128, xnT, T)
                    kb.cp(KT[:, c, koff:koff + T], p[:, 0:T], eng=ev())
                if last_chunk and "k" not in DFLAGS:
                    kb.dma(p_k[l], kvf[:, 0:256], q="sp")
                    kb.dma(p_v[l], kvf[:, 256:512], q="sp")
                if not isP and "o" not in DFLAGS:
                    for s in range(16):
                        kb.dma(s_k[l][s, 120:128, :], kvf[s * 8:(s + 1) * 8, 0:256], q="sp")
                        kb.dma(s_v[l][s, 120:128, :], kvf[s * 8:(s + 1) * 8, 256:512], q="sp")
                if "a" in DFLAGS:
                    kb.memset(V(yT, yT.t[:, :, :]), 0.0)
                FFd = arF.F

                def att_cfg(j):
                    if ci == 0 and j == 0:
                        return 128, [1], cview(C_MASKP + 128, 128)
                    return j * 128, [j, j + 1], cview(C_MASKP, 256)

                def st_scores(it, bs):
                    j, g = it
                    k0, vidx, mask = att_cfg(j)
                    ncol = len(vidx) * 128
                    m_, half = g // 2, g % 2
                    base = 64 * half
                    for r in range(4):
                        p = bank()
                        kb.mm(p[:, 0:ncol], QT[base:base + 64, 4 * m_ + r, j * 128:(j + 1) * 128],
                              KT[base:base + 64, m_, k0:k0 + ncol])
                        kb.stt(Ssbs[bs][:, r * ncol:(r + 1) * ncol], p[:, 0:ncol], 0.125, mask, ALU.mult, ALU.add)

                def st_stats(it, bs):
                    j, g = it
                    k0, vidx, mask = att_cfg(j)
                    ncol = len(vidx) * 128
                    sm4 = sm4s[bs]
                    Ss3 = V(Ssbs[bs], Ssbs[bs].t[:, 0:4 * ncol].rearrange("p (r c) -> p r c", r=4))
                    sk4 = r16[:, 3, l, 4 * g:4 * g + 4]
                    kb.red(sm4[:, 0:4], Ss3, ALU.max)
                    kb.tt(sm4[:, 0:4], sm4[:, 0:4], sk4, ALU.max)
                    kb.ts(sm4[:, 4:8], sm4[:, 0:4], -1.0, None, ALU.mult)
                    kb.tt(sm4[:, 12:16], sk4, sm4[:, 4:8], ALU.add)

                def st_exp(it, bs):
                    j, g = it
                    k0, vidx, mask = att_cfg(j)
                    ncol = len(vidx) * 128
                    sm4 = sm4s[bs]
                    for r in range(4):
                        kb.act(Pbfs[bs][:, r * ncol:(r + 1) * ncol], Ssbs[bs][:, r * ncol:(r + 1) * ncol], AF.Exp,
                               bias=sm4[:, 4 + r:5 + r], accum=sm4[:, 8 + r:9 + r])
                    kb.act(sm4[:, 12:16], sm4[:, 12:16], AF.Exp)
                    kb.tt(sm4[:, 16:20], sm4[:, 8:12], sm4[:, 12:16], ALU.add)
                    kb.recip(sm4[:, 20:24], sm4[:, 16:20])

                def st_tr(it, bs):
                    j, g = it
                    k0, vidx, mask = att_cfg(j)
                    nk = len(vidx)
                    pbk = bbank()
                    for q_ in range(4 * nk):
                        kb.tr(pbk[:, q_ * 128:(q_ + 1) * 128], Pbfs[bs][:, q_ * 128:(q_ + 1) * 128], identb[:, :])
                    kb.cp(PTs[bs][:, 0:4 * nk, :], V(pbk, pbk.t[:, 0:4 * nk * 128].rearrange("p (c t) -> p c t", c=4 * nk)), eng=ev())

                def st_pv(it, bs):
                    j, g = it
                    k0, vidx, mask = att_cfg(j)
                    nk = len(vidx)
                    po = bank()
                    for r in range(4):
                        for i, vi in enumerate(vidx):
                            kb.mm(po[:, r * 64:(r + 1) * 64], PTs[bs][:, r * nk + i, :], vtok[:, vi, g * 64:(g + 1) * 64],
                                  start=(i == 0), stop=(i == nk - 1))
                    kb.tt(V(Otmps[bs], Otmps[bs].t[:, :].rearrange("p (r d) -> p r d", r=4)),
                          V(po, po.t[:, 0:256].rearrange("p (r d) -> p r d", r=4)),
                          sm4s[bs].pat(20, [[FFd, 128], [1, 4], [0, 64]]), ALU.mult)
                    kb.tt(Ogb[:, g * 256:(g + 1) * 256], Otmps[bs][:, :], gsil[:, j, g * 256:(g + 1) * 256], ALU.mult)
                    if g == 3:
                        transpose_to_fm(Ogb, 8, lambda c0, n, j=j: yT[:, c0:c0 + n, j * 128:(j + 1) * 128])

                if isP and "a" not in DFLAGS:
                    items = [(j, g) for j in range(NT) for g in range(4)]
                    stages = [st_scores, st_stats, st_exp, st_tr, st_pv]
                    for i in range(0, len(items), 2):
                        pair = items[i:i + 2]
                        for s_i in range(len(stages)):
                            for k_, it in enumerate(pair):
                                stages[s_i](it, (i + k_) % 2)
                for j in range(NT if ("a" not in DFLAGS and not isP) else 0):
                    k0, vidx, mask = 0, list(range(17)), maskS[:, :]
                    nk = len(vidx)
                    ncol = nk * 128
                    for h in range(0 if isP else 16):
                        m_, half, r = h // 8, (h % 8) // 4, h % 4
                        qch, base, g = 4 * m_ + r, 64 * half, h // 4
                        for c0 in range(0, ncol, 512):
                            n = min(512, ncol - c0)
                            p = bank()
                            kb.mm(p[:, 0:n], QT[base:base + 64, qch, j * 128:(j + 1) * 128],
                                  KT[base:base + 64, m_, k0 + c0:k0 + c0 + n])
                            kb.stt(Ssb[:, c0:c0 + n], p[:, 0:n], 0.125, V(mask.buf, mask.ap[:, c0:c0 + n]), ALU.mult, ALU.add)
                        sk = r16[:, 3, l, h:h + 1]
                        kb.red(sm[:, 0:1], Ssb[:, 0:ncol], ALU.max)
                        kb.tt(sm[:, 0:1], sm[:, 0:1], sk, ALU.max)
                        kb.ts(sm[:, 1:2], sm[:, 0:1], -1.0, None, ALU.mult)
                        kb.act(Pbf[:, 0:ncol], Ssb[:, 0:ncol], AF.Exp, bias=sm[:, 1:2], accum=sm[:, 2:3])
                        kb.act(sm[:, 3:4], sk, AF.Exp, bias=sm[:, 1:2])
                        kb.tt(sm[:, 4:5], sm[:, 2:3], sm[:, 3:4], ALU.add)
                        kb.recip(sm[:, 5:6], sm[:, 4:5])
                        for kk in range(0, nk, 8):
                            n = min(8, nk - kk)
                            pbk = bbank()
                            for i in range(n):
                                kb.tr(pbk[:, i * 128:(i + 1) * 128], Pbf[:, (kk + i) * 128:(kk + i + 1) * 128], identb[:, :])
                            kb.cp(PT[:, kk:kk + n, :], V(pbk, pbk.t[:, 0:n * 128].rearrange("p (c t) -> p c t", c=n)), eng=ev())
                        po = bank()
                        for i, vi in enumerate(vidx):
                            kb.mm(po[:, 0:64], PT[:, i, :], vtok[:, vi, g * 64:(g + 1) * 64], start=(i == 0), stop=(i == nk - 1))
                        kb.stt(Ogb[:, h * 64:(h + 1) * 64], po[:, 0:64], sm[:, 5:6], gsil[:, j, h * 64:(h + 1) * 64], ALU.mult, ALU.mult)
                    transpose_to_fm(Ogb, 8, lambda c0, n, j=j: yT[:, c0:c0 + n, j * 128:(j + 1) * 128])
                if isP and "p" not in DFLAGS:
                    for c in range(2):
                        kb.cp(KTprev[l][:, c, :], KT[:, c, T:T + 128], eng="pool")
                    kb.cp(Vprev[l][:, :], vtok[:, NT, :], eng="pool")
                branch_out(l, 3, yT, T, first)
                first = False

            arF.reset(); arB.reset()
            if first:
                kb.memset(V(hacc, hacc.t.ap()), 0.0, eng="pool")
            for blk in range(4):
                s_ = wget("wo%d" % blk)
                for j in range(NT):
                    p = bank()
                    for k in range(16):
                        kb.mm(p[:, 0:512], hacc[:, k, j * 128:(j + 1) * 128], s_[:, k, 0:512], start=(k == 0), stop=(k == 15))
                    kb.tt(x[:, j, blk * 512:(blk + 1) * 512], x[:, j, blk * 512:(blk + 1) * 512], p[:, :], ALU.add)
            if isP and ci == 0 and l == 0:
                dump(3, V(x, x.t[:, 0, :]), 2048)
            if last_chunk:
                stgo = arF.alloc("stgo", [128, 1024]); tmpo = arF.alloc("tmpo", [128, 128])
                if 2 in branches:
                    store_T(lambda c: tailC[l][:, c, :], 2, 1024, p_csc[l], stgo, tmpo)
                if 1 in branches:
                    store_T(lambda c: tailB[l][:, c, :], 30, 1024, p_ccf[l], stgo, tmpo)
                if 0 in branches:
                    for half in range(2):
                        store_T(lambda c, half=half: tailA[l][:, half * 8 + c, :], 3, 1024,
                                p_cssm[l][:, half * 1024:(half + 1) * 1024], stgo, tmpo)
                    kb.dma(p_ssm[l].rearrange("(c q) n -> q c n", q=128), hnat[l][:, :, :], q="sp")

        arF.reset(); arB.reset()
        rowbuf = arF.alloc("rowbuf", [128, D])
        kb.dma(rowbuf[:, :], final_norm_w[0].partition_broadcast(128), q="sp")
        yo = [arF.alloc("yo%d" % i, [128, D]) for i in range(2)]
        for j in range(NT):
            r = rstd_of(x[:, j, :], D, 1e-6, yo[j % 2][:, :])
            kb.stt(yo[j % 2][:, :], x[:, j, :], r, rowbuf[:, :], ALU.mult, ALU.mult)
            dst = y_p[ci * 512 + j * 128: ci * 512 + (j + 1) * 128, :] if isP else y_s
            kb.dma(dst, yo[j % 2][:, :], q="sp")

    assert wst["next"] == len(plan), (wst["next"], len(plan))
    nc = kb.finish()
    print("instructions:", kb.n_inst, "channels:", kb.nchan)
    return nc


_NC_CACHE = {}
BRANCHES = (0, 1, 2, 3)
DBG_CORES = 0
DFLAGS = ""
CHUNKS = None


def kernel(**inp):
    f32 = lambda a: np.ascontiguousarray(np.asarray(a, dtype=np.float32))
    if "nc" not in _NC_CACHE:
        _NC_CACHE["nc"] = build(BRANCHES)
    nc = _NC_CACHE["nc"]
    consts, masks = make_consts()
    shared = {k: f32(inp[k]) for k in (
        "norm_w", "w_in", "ssm_conv_w", "ssm_conv_b", "ssm_dt_bias", "ssm_a_log", "ssm_d", "ssm_norm_w",
        "w_out_ssm", "cf_conv_w", "cf_conv_b", "cf_ln_w", "cf_ln_b", "w_out_cf", "sc_conv_w", "w_out_sc",
        "att_sinks", "w_out_att", "w_o")}
    shared["final_norm_w"] = f32(inp["final_norm_w"]).reshape(1, D)
    shared["consts"] = consts
    shared["masks"] = masks
    xp_ = f32(inp["x_prompt"]); xs_ = f32(inp["x_sample"])
    in_maps = []
    for c in range(8):
        sl = slice(16 * c, 16 * c + 16)
        m = dict(shared)
        m["xp"] = xp_[c % 4]
        m["xs"] = xs_[sl].reshape(128, D)
        m["st_ssm"] = f32(inp["state_ssm"][:, sl]).reshape(2, 16, 1024, 128)
        m["st_cssm"] = f32(inp["state_conv_ssm"][:, sl]).reshape(2, 48, 2048)
        m["st_ccf"] = f32(inp["state_conv_cf"][:, sl]).reshape(2, 480, 1024)
        m["st_csc"] = f32(inp["state_conv_sc"][:, sl]).reshape(2, 32, 1024)
        m["ck"] = f32(inp["cache_k"][:, sl]).reshape(2, 16, 128, 256)
        m["cv"] = f32(inp["cache_v"][:, sl]).reshape(2, 16, 128, 256)
        in_maps.append(m)
    if DBG_CORES:
        return run_bass_kernel_spmd(nc, in_maps[:DBG_CORES], core_ids=list(range(DBG_CORES))).results
    res = run_bass_kernel_spmd(nc, in_maps, core_ids=list(range(8)))
    R = res.results
    cat = lambda key, cores, ax: np.concatenate([R[c][key] for c in cores], axis=ax)
    pc = [0, 1, 2, 3]
    ac = list(range(8))
    y_prompt = np.stack([R[c]["y_p"] for c in pc], 0)
    y_sample = cat("y_s", ac, 0).reshape(128, 8, D)
    p_ssm = np.stack([R[c]["p_ssm"] for c in pc], 1).reshape(2, 4, 16, 64, 128)
    p_cssm = np.stack([R[c]["p_cssm"] for c in pc], 1)
    p_ccf = np.stack([R[c]["p_ccf"] for c in pc], 1)
    p_csc = np.stack([R[c]["p_csc"] for c in pc], 1)
    p_k = np.stack([R[c]["p_k"] for c in pc], 1).reshape(2, 4, 128, 4, 64)
    p_v = np.stack([R[c]["p_v"] for c in pc], 1).reshape(2, 4, 128, 4, 64)
    s_ssm = cat("s_ssm", ac, 1).reshape(2, 128, 16, 64, 128)
    s_cssm = cat("s_cssm", ac, 1).reshape(2, 128, 3, 2048)
    s_ccf = cat("s_ccf", ac, 1).reshape(2, 128, 30, 1024)
    s_csc = cat("s_csc", ac, 1).reshape(2, 128, 2, 1024)
    s_k = cat("s_k", ac, 1).reshape(2, 128, 128, 4, 64)
    s_v = cat("s_v", ac, 1).reshape(2, 128, 128, 4, 64)
    return (y_prompt, y_sample, p_ssm, p_cssm, p_ccf, p_csc, p_k, p_v,
            s_ssm, s_cssm, s_ccf, s_csc, s_k, s_v)
```
